# Optimizing a Trainium2 kernel written in Bass

```python
import math
import jax, jax.numpy as jnp
from jax import lax
import numpy as np

D_MODEL = 2048
BATCH = 1
SEQ = 8192
DEPTH = 4

CHUNK = 64
Q_BLOCK = 128
D_CONV = 1024
CONV_WIDTH = 3
N_HEADS = 8
HEAD_DIM = 64
V_DIM = 2 * HEAD_DIM
D_ATTN = N_HEADS * V_DIM
ROPE_THETA = 10000.0
EPS = 1e-6
LAMBDA_STD = 0.1
D_IN = 4 * D_CONV + 4 * D_ATTN + 2 * D_MODEL

kernel_name = "hybrid_shortconv_diffattn_block"


def rms_norm(x, g):
    xf = x.astype(jnp.float32)
    y = xf * lax.rsqrt(jnp.mean(xf * xf, axis=-1, keepdims=True) + EPS)
    return (y * g.astype(jnp.float32)).astype(x.dtype)


def split_columns(p, sizes):
    outs, start = [], 0
    for s in sizes:
        outs.append(p[..., start:start + s])
        start += s
    return outs


def rope(x, pos):
    half = HEAD_DIM // 2
    inv = ROPE_THETA ** (-jnp.arange(half, dtype=jnp.float32) / half)
    ang = pos.astype(jnp.float32)[:, None] * inv[None, :]
    cos = jnp.cos(ang)[None, :, None, None, :]
    sin = jnp.sin(ang)[None, :, None, None, :]
    xf = x.astype(jnp.float32)
    x1, x2 = xf[..., :half], xf[..., half:]
    out = jnp.concatenate([x1 * cos - x2 * sin, x2 * cos + x1 * sin], axis=-1)
    return out.astype(x.dtype)


def short_conv_branch(u, b_gate, c_gate, z, conv_w, w_out):
    v = c_gate * u
    y = lax.conv_general_dilated(
        v, conv_w[:, None, :], window_strides=(1,),
        padding=((CONV_WIDTH - 1, 0),),
        dimension_numbers=("NWC", "WIO", "NWC"),
        feature_group_count=D_CONV)
    return (b_gate * y * jax.nn.silu(z)) @ w_out


def diff_attention(q, k, v, lam, lambda_init, subln_g):
    bsz, seq = q.shape[0], q.shape[1]
    scale = HEAD_DIM ** -0.5
    k_chunk = jnp.arange(seq) // CHUNK

    def block(i):
        start = i * Q_BLOCK
        qb = lax.dynamic_slice_in_dim(q, start, Q_BLOCK, axis=1)
        s = jnp.einsum("bqhcd,bkhcd->bhcqk", qb, k,
                       preferred_element_type=jnp.float32) * scale
        q_chunk = (start + jnp.arange(Q_BLOCK)) // CHUNK
        mask = k_chunk[None, :] <= q_chunk[:, None]
        s = jnp.where(mask, s, -jnp.inf)
        p = jax.nn.softmax(s, axis=-1)
        a = p[:, :, 0] - lam * p[:, :, 1]
        return jnp.einsum("bhqk,bkhe->bqhe", a.astype(v.dtype), v)

    out = lax.map(block, jnp.arange(seq // Q_BLOCK))
    out = jnp.moveaxis(out, 0, 1).reshape(bsz, seq, N_HEADS, V_DIM)
    out = rms_norm(out, subln_g) * (1.0 - lambda_init)
    return out.reshape(bsz, seq, D_ATTN)


def setup_inputs(seed: int = 0) -> dict:
    key = jax.random.key(seed)
    ks = jax.random.split(key, 18)
    f32 = jnp.float32
    n = lambda k, shape, s: jax.random.normal(k, shape, f32) * s
    return {
        "x": n(ks[0], (BATCH, SEQ, D_MODEL), 1.0),
        "c": n(ks[1], (BATCH, D_MODEL), 1.0),
        "ada_w": n(ks[2], (DEPTH, D_MODEL, 3 * D_MODEL), D_MODEL ** -0.5),
        "ada_b": n(ks[3], (DEPTH, 3 * D_MODEL), 0.01),
        "norm_g": 1.0 + n(ks[4], (DEPTH, D_MODEL), 0.01),
        "w_in": n(ks[5], (DEPTH, D_MODEL, D_IN), D_MODEL ** -0.5),
        "conv_w": n(ks[6], (DEPTH, CONV_WIDTH, D_CONV), CONV_WIDTH ** -0.5),
        "w_conv_out": n(ks[7], (DEPTH, D_CONV, D_MODEL), D_CONV ** -0.5),
        "q_norm_g": 1.0 + n(ks[8], (DEPTH, HEAD_DIM), 0.01),
        "k_norm_g": 1.0 + n(ks[9], (DEPTH, HEAD_DIM), 0.01),
        "lam_q1": n(ks[10], (DEPTH, HEAD_DIM), LAMBDA_STD),
        "lam_k1": n(ks[11], (DEPTH, HEAD_DIM), LAMBDA_STD),
        "lam_q2": n(ks[12], (DEPTH, HEAD_DIM), LAMBDA_STD),
        "lam_k2": n(ks[13], (DEPTH, HEAD_DIM), LAMBDA_STD),
        "subln_g": 1.0 + n(ks[14], (DEPTH, V_DIM), 0.01),
        "w_attn_out": n(ks[15], (DEPTH, D_ATTN, D_MODEL), D_ATTN ** -0.5),
        "w_o": n(ks[16], (DEPTH, D_MODEL, D_MODEL), D_MODEL ** -0.5),
    }


def reference(x, c, ada_w, ada_b, norm_g, w_in, conv_w, w_conv_out, q_norm_g, k_norm_g,
              lam_q1, lam_k1, lam_q2, lam_k2, subln_g, w_attn_out, w_o):
    bsz, seq = x.shape[0], x.shape[1]
    pos = jnp.arange(seq)
    c_act = jax.nn.silu(c)
    sizes = [D_CONV] * 4 + [D_ATTN] * 4 + [D_MODEL] * 2
    for l in range(DEPTH):
        mod = c_act @ ada_w[l] + ada_b[l]
        shift, scale, gate = jnp.split(mod, 3, axis=-1)
        h = rms_norm(x, norm_g[l]) * (1.0 + scale[:, None, :]) + shift[:, None, :]

        proj = h @ w_in[l]
        u, bg, cg, za, q, k, v, zb, ga, gb = split_columns(proj, sizes)

        y_conv = short_conv_branch(u, bg, cg, za, conv_w[l], w_conv_out[l])

        q = rope(rms_norm(q.reshape(bsz, seq, N_HEADS, 2, HEAD_DIM), q_norm_g[l]), pos)
        k = rope(rms_norm(k.reshape(bsz, seq, N_HEADS, 2, HEAD_DIM), k_norm_g[l]), pos)
        v = v.reshape(bsz, seq, N_HEADS, V_DIM)
        lambda_init = 0.8 - 0.6 * math.exp(-0.3 * l)
        lam = (jnp.exp(jnp.sum(lam_q1[l].astype(jnp.float32) * lam_k1[l].astype(jnp.float32)))
               - jnp.exp(jnp.sum(lam_q2[l].astype(jnp.float32) * lam_k2[l].astype(jnp.float32)))
               + lambda_init)
        o_attn = diff_attention(q, k, v, lam, lambda_init, subln_g[l])
        y_attn = (o_attn * jax.nn.silu(zb)) @ w_attn_out[l]

        merged = jax.nn.sigmoid(ga) * y_conv + jax.nn.sigmoid(gb) * y_attn
        x = x + gate[:, None, :] * (merged @ w_o[l])
    return x
```

```python
import math
import contextlib
import numpy as np
import ml_dtypes
import concourse.bass as bass
import concourse.mybir as mybir
from concourse.bass_utils import run_bass_kernel_spmd

F32 = mybir.dt.float32
BF16 = mybir.dt.bfloat16
AF = mybir.ActivationFunctionType
ALU = mybir.AluOpType
AX = mybir.AxisListType

NCORES = 8
D = 2048
SEQ = 8192
T = 1024
DEPTH = 4
DIN = 12288
EPS = 1e-6
C_U, C_BG, C_CG, C_ZA, C_Q, C_K, C_V, C_ZB, C_GA, C_GB = 0, 1024, 2048, 3072, 4096, 5120, 6144, 7168, 8192, 10240
NEG = -30000.0

ENGS = ("pe", "act", "dve", "pool", "sp")
SEM_LIMIT = 30000


class Region:
    __slots__ = ("w", "rc", "rd")

    def __init__(self):
        self.w = None
        self.rc = {}
        self.rd = []


class Rec:
    __slots__ = ("eng", "fn", "deps", "needs_inc", "is_dma", "dma_slot", "dma_val", "semref")

    def __init__(self, eng, fn, is_dma):
        self.eng = eng
        self.fn = fn
        self.deps = []
        self.needs_inc = False
        self.is_dma = is_dma
        self.dma_slot = None
        self.dma_val = None
        self.semref = None


class Sched:
    def __init__(self, nc, n_dma_sems=32, same_engine_sync=True):
        self.nc = nc
        self.q = {e: [] for e in ENGS}
        self.n_dma_sems = n_dma_sems
        self.dma_count = 0
        self.dma_slot_last = [None] * n_dma_sems
        self.same_engine_sync = same_engine_sync
        self.regions = {}
        self.pending = {}

    def R(self, *key):
        r = self.regions.get(key)
        if r is None:
            r = Region()
            self.regions[key] = r
        return r

    def barrier(self):
        deps = [self.q[e][-1] for e in ENGS if self.q[e]]
        deps += [d for d in self.dma_slot_last if d is not None]
        for e in ENGS:
            self.pending[e] = list(deps)

    def _add(self, eng, fn, reads, writes, is_dma):
        rec = Rec(eng, fn, is_dma)
        deps = []
        for r in reads:
            if r.w is not None:
                deps.append(r.w)
        for w in writes:
            if w.w is not None:
                deps.append(w.w)
            deps.extend(w.rc.values())
            deps.extend(w.rd)
        if eng in self.pending:
            deps.extend(self.pending.pop(eng))
        if is_dma:
            slot = self.dma_count % self.n_dma_sems
            rec.dma_slot = slot
            rec.dma_val = 16 * (self.dma_count // self.n_dma_sems + 1)
            prev = self.dma_slot_last[slot]
            if prev is not None:
                deps.append(prev)
            self.dma_slot_last[slot] = rec
            self.dma_count += 1
        seen = set()
        for d in deps:
            if d is rec or id(d) in seen:
                continue
            seen.add(id(d))
            if (not d.is_dma) and d.eng == eng and not is_dma:
                if eng == "pe" or not self.same_engine_sync:
                    continue
            rec.deps.append(d)
            if not d.is_dma:
                d.needs_inc = True
        self.q[eng].append(rec)
        for r in reads:
            if is_dma:
                r.rd.append(rec)
            else:
                r.rc[eng] = rec
        for w in writes:
            w.w = rec
            w.rc = {}
            w.rd = []
        return rec

    def op(self, eng, fn, reads=(), writes=()):
        return self._add(eng, fn, reads, writes, False)

    def dma(self, eng, fn, reads=(), writes=()):
        return self._add(eng, fn, reads, writes, True)

    def emit(self, final_wait_recs=()):
        nc = self.nc
        with contextlib.ExitStack() as es:
            for e in ENGS:
                cnt = 0
                sems = []
                for rec in self.q[e]:
                    if rec.is_dma or not rec.needs_inc:
                        continue
                    si = cnt // SEM_LIMIT
                    while len(sems) <= si:
                        sems.append(es.enter_context(nc.semaphore(f"s_{e}_{len(sems)}")))
                    rec.semref = (sems[si], cnt % SEM_LIMIT + 1)
                    cnt += 1
            dma_sems = [es.enter_context(nc.semaphore(f"s_dma_{i}")) for i in range(self.n_dma_sems)]
            for e in ENGS:
                for rec in self.q[e]:
                    if rec.is_dma:
                        rec.semref = (dma_sems[rec.dma_slot], rec.dma_val)
            block = es.enter_context(nc.Block())
            engmap = {"pe": block.tensor, "act": block.scalar, "dve": block.vector,
                      "pool": block.gpsimd, "sp": block.sync}

            def make(e):
                def body(engine):
                    known = {}
                    for rec in self.q[e]:
                        for d in rec.deps:
                            sem, val = d.semref
                            k = id(sem)
                            if known.get(k, 0) >= val:
                                continue
                            known[k] = val
                            engine.wait_ge(sem, val)
                        ins = rec.fn(engine)
                        if rec.is_dma:
                            ins.then_inc(rec.semref[0], 16)
                        elif rec.needs_inc:
                            ins.then_inc(rec.semref[0], 1)
                    if e == "sp":
                        for d in final_wait_recs:
                            sem, val = d.semref
                            engine.wait_ge(sem, val)
                return body

            for e in ENGS:
                engmap[e](make(e))


DEBUG = False


def build_program(has_B, has_A, first):
    nc = bass.Bass("TRN2", target_bir_lowering=False)
    S = Sched(nc)
    R = S.R

    def din(name, shape, dt=F32):
        return nc.dram_tensor(name, list(shape), dt, kind="ExternalInput").ap()

    def dout(name, shape, dt=F32):
        return nc.dram_tensor(name, list(shape), dt, kind="ExternalOutput").ap()

    cos_d = din("cos", [128, T])
    sin_d = din("sin", [128, T])
    rmat_d = din("rmat", [128, 128])
    ones_d = din("ones", [128, 128])
    blk_d = din("blk64", [128, 128])
    ident_d = din("ident", [128, 128], BF16)
    onesb_d = din("onesb", [128, 128], BF16)
    mask_d = din("mask", [128, 8, 128], BF16)
    xT_in = din("xT_in", [D, T])

    if has_A:
        c_d = din("c_vec", [128, 16])
        adaw_d = din("ada_w", [D, 3 * D])
        adab_d = din("ada_b", [128, 48])
        ng_d = din("norm_g", [128, 16])
        win_d = din("w_in", [D, DIN])
        gq_d = din("gq128", [128, 1])
        gk_d = din("gk128", [128, 1])
        kT_o = dout("kT_o", [1024, T], BF16)
        v_o = dout("v_o", [T, 1024], BF16)
        cv_o = dout("cv_o", [1024, T])
        qT_o = dout("qT_o", [1024, T], BF16)
        bz_o = dout("bz_o", [1024, T], BF16)
        szb_o = dout("szb_o", [1024, T], BF16)
        sga_o = dout("sga_o", [D, T], BF16)
        sgb_o = dout("sgb_o", [D, T], BF16)
        gate_o = dout("gate_o", [128, 16])
        if DEBUG:
            dbg_hT = dout("dbg_hT", [D, T], BF16)
            dbg_w = dout("dbg_w", [128, 16 * 512], BF16)
    if has_B:
        qT_i = din("qT_i", [1024, T], BF16)
        bz_i = din("bz_i", [1024, T], BF16)
        szb_i = din("szb_i", [1024, T], BF16)
        sga_i = din("sga_i", [D, T], BF16)
        sgb_i = din("sgb_i", [D, T], BF16)
        gate_i = din("gate_i", [128, 16])
        cvh_i = din("cvh_i", [1024, 8 * 130])
        K_i = din("K_all", [8, 128, SEQ], BF16)
        V_i = din("V_all", [8, 128, 64 * 128], BF16)
        cw_d = din("conv_w", [128, 24])
        wco_d = din("w_conv_out", [1024, D])
        wao_d = din("w_attn_out", [1024, D])
        wo_d = din("w_o", [D, D])
        lamp_d = din("lamp", [4, 64])
        lconst_d = din("lconst", [128, 2])
        subg_d = din("subg", [128, 1])
    xT_o = dout("xT_o", [D, T])

    fin = []
    with contextlib.ExitStack() as es:
        def sb(name, shape, dt, stack=es):
            return stack.enter_context(nc.sbuf_tensor(name, list(shape), dt))

        ps = [es.enter_context(nc.psum_tensor(f"ps{i}", [128, 512], F32)) for i in range(8)]
        PS = [R("ps", i) for i in range(8)]

        xT = sb("xT", [128, 16, T], F32)
        rmat_s = sb("rmat_s", [128, 128], F32)
        ones_s = sb("ones_s", [128, 128], F32)
        blk_s = sb("blk_s", [128, 128], F32)
        ident_s = sb("ident_s", [128, 128], BF16)
        onesb_s = sb("onesb_s", [128, 128], BF16)
        mask_s = sb("mask_s", [128, 8, 128], BF16)
        wbuf = [None, None]

        def ld(eng, dst, src, reg):
            return S.dma(eng, lambda e: e.dma_start(out=dst, in_=src), writes=[reg])

        ld("sp", rmat_s[:], rmat_d, R("rmat"))
        ld("sp", ones_s[:], ones_d, R("ones"))
        ld("sp", blk_s[:], blk_d, R("blk"))
        ld("sp", ident_s[:], ident_d, R("ident"))
        ld("sp", onesb_s[:], onesb_d, R("onesb"))
        ld("sp", mask_s[:], mask_d, R("mask"))
        for kc in range(16):
            S.dma("sp", lambda e, kc=kc: e.dma_start(out=xT[:, kc, :], in_=xT_in[kc * 128:(kc + 1) * 128, :]),
                  writes=[R("xT", kc, 0), R("xT", kc, 1)])

        wstate = {"n": 0}

        def wload(w_ap, nk, col0, ncols):
            s = wstate["n"] % 2
            wstate["n"] += 1
            src = w_ap[:, col0:col0 + ncols].rearrange("(kc p) n -> p kc n", p=128)
            dst = wbuf[s][:, 0:nk, 0:ncols]
            S.dma("pool", lambda e: e.dma_start(out=dst, in_=src), writes=[R("w", s)])
            return s

        def mm(out, lhsT, rhs, start, stop, reads, writes):
            S.op("pe", lambda e: e.matmul(out, lhsT, rhs, start=start, stop=stop), reads=reads, writes=writes)

        rot = {}

        def rotbuf(stack, name, n, shape, dt):
            bufs = [sb(f"{name}{i}", shape, dt, stack) for i in range(n)]
            rot[name] = [0, n, bufs]

        def nxt(name):
            st = rot[name]
            i = st[0] % st[1]
            st[0] += 1
            return st[2][i], R(name, i)

        psr = {"n": 0}

        def nps(lo=0, n=4):
            i = lo + psr["n"] % n
            psr["n"] += 1
            return i

        def phase_B():
            with contextlib.ExitStack() as bs:
                gate_s = sb("gate_s", [128, 16], F32, bs)
                cw_s = sb("cw_s", [128, 24], F32, bs)
                lamp_s = sb("lamp_s", [128, 4, 64], F32, bs)
                lconst_s = sb("lconst_s", [128, 2], F32, bs)
                subg_s = sb("subg_s", [128, 1], F32, bs)
                lam_t = sb("lam_t", [128, 2, 64], F32, bs)
                lam_r = sb("lam_r", [128, 4], F32, bs)
                neglam = sb("neglam", [128, 1], F32, bs)
                g_s = sb("g_s", [128, 8, T], BF16, bs)
                onT = sb("onT", [128, 8, T], BF16, bs)
                ld("sp", gate_s[:], gate_i, R("gate_s"))
                ld("sp", cw_s[:], cw_d, R("cw"))
                ld("sp", lconst_s[:], lconst_d, R("lconst"))
                ld("sp", subg_s[:], subg_d, R("subg"))
                lamp_b = bass.AP(lamp_d.tensor, 0, [[0, 128], [1, 256]])
                ld("sp", lamp_s[:].rearrange("p a b -> p (a b)"), lamp_b, R("lamp"))
                S.op("dve", lambda e: e.tensor_tensor(lam_t[:, 0, :], lamp_s[:, 0, :], lamp_s[:, 1, :], ALU.mult), reads=[R("lamp")], writes=[R("lamt0")])
                S.op("dve", lambda e: e.tensor_tensor(lam_t[:, 1, :], lamp_s[:, 2, :], lamp_s[:, 3, :], ALU.mult), reads=[R("lamp")], writes=[R("lamt1")])
                S.op("dve", lambda e: e.reduce_sum(lam_r[:, 0:1], lam_t[:, 0, :], axis=AX.X), reads=[R("lamt0")], writes=[R("lamr0")])
                S.op("dve", lambda e: e.reduce_sum(lam_r[:, 1:2], lam_t[:, 1, :], axis=AX.X), reads=[R("lamt1")], writes=[R("lamr1")])
                S.op("act", lambda e: e.activation(out=lam_r[:, 2:4], in_=lam_r[:, 0:2], func=AF.Exp), reads=[R("lamr0"), R("lamr1")], writes=[R("lamr2")])
                S.op("dve", lambda e: e.tensor_tensor(neglam[:], lam_r[:, 3:4], lam_r[:, 2:3], ALU.subtract), reads=[R("lamr2")], writes=[R("neglam0")])
                S.op("dve", lambda e: e.tensor_tensor(neglam[:], neglam[:], lconst_s[:, 0:1], ALU.subtract), reads=[R("neglam0"), R("lconst")], writes=[R("neglam")])
                S.op("dve", lambda e: e.tensor_tensor(subg_s[:], subg_s[:], lconst_s[:, 1:2], ALU.mult), reads=[R("subg"), R("lconst")], writes=[R("subg")])

                with contextlib.ExitStack() as cs:
                    cvh = [sb(f"cvh{i}", [128, 8, 130], F32, cs) for i in range(2)]
                    bzs = [sb(f"bzs{i}", [128, T], BF16, cs) for i in range(2)]
                    ytmp = [sb(f"ytmp{i}", [128, 8, 128], F32, cs) for i in range(2)]
                    for c in range(8):
                        b = c % 2
                        ld("sp", cvh[b][:].rearrange("p a b -> p (a b)"), cvh_i[c * 128:(c + 1) * 128, :], R("cvh", b))
                        ld("sp", bzs[b][:], bz_i[c * 128:(c + 1) * 128, :], R("bzs", b))
                        y = ytmp[b]
                        cv = cvh[b]
                        S.op("dve", lambda e, y=y, cv=cv, c=c: e.tensor_scalar(y[:], cv[:, :, 2:130], cw_s[:, 16 + c:17 + c], None, ALU.mult),
                             reads=[R("cvh", b), R("cw")], writes=[R("ytmp", b)])
                        S.op("dve", lambda e, y=y, cv=cv, c=c: e.scalar_tensor_tensor(y[:], cv[:, :, 1:129], cw_s[:, 8 + c:9 + c], y[:], ALU.mult, ALU.add),
                             reads=[R("cvh", b), R("cw"), R("ytmp", b)], writes=[R("ytmp", b)])
                        S.op("dve", lambda e, y=y, cv=cv, c=c: e.scalar_tensor_tensor(y[:], cv[:, :, 0:128], cw_s[:, c:c + 1], y[:], ALU.mult, ALU.add),
                             reads=[R("cvh", b), R("cw"), R("ytmp", b)], writes=[R("ytmp", b)])
                        bz = bzs[b]
                        S.op("dve", lambda e, y=y, bz=bz, c=c: e.tensor_tensor(g_s[:, c, :], y[:].rearrange("p a b -> p (a b)"), bz[:], ALU.mult),
                             reads=[R("ytmp", b), R("bzs", b)], writes=[R("g", c)])
                S.barrier()

                with contextlib.ExitStack() as as_:
                    Kh = [sb(f"Kh{i}", [128, 32, 128], BF16, as_) for i in range(4)]
                    Vh = [sb(f"Vh{i}", [128, 32, 128], BF16, as_) for i in range(4)]
                    qh_s = [sb(f"qh{i}", [128, T], BF16, as_) for i in range(2)]
                    zb_s = [sb(f"zbh{i}", [128, T], BF16, as_) for i in range(2)]
                    rotbuf(as_, "pt", 4, [128, 512], BF16)
                    rotbuf(as_, "ef", 6, [128, 512], F32)
                    hslot = {"n": 0}

                    def load_half(h, hf):
                        s = hslot["n"] % 4
                        hslot["n"] += 1
                        ld("sp", Kh[s][:].rearrange("p a b -> p (a b)"), K_i[h, :, hf * 4096:(hf + 1) * 4096], R("Kh", s))
                        ld("sp", Vh[s][:].rearrange("p a b -> p (a b)"), V_i[h, :, hf * 4096:(hf + 1) * 4096], R("Vh", s))
                        return s

                    def load_head(h):
                        b = h % 2
                        ld("sp", qh_s[b][:], qT_i[h * 128:(h + 1) * 128, :], R("qh", b))
                        ld("sp", zb_s[b][:], szb_i[h * 128:(h + 1) * 128, :], R("zbh", b))
                        return (load_half(h, 0), load_half(h, 1))

                    slots_next = load_head(0)
                    for h in range(8):
                        slots = slots_next
                        hb = h % 2
                        for qh in range(2):
                            nkt = 32 * qh + 32
                            for kt in range(nkt):
                                j0 = max(kt // 8, 4 * qh)
                                c0 = j0 * 128
                                c1 = (4 * qh + 4) * 128
                                n = c1 - c0
                                off = c0 - 4 * qh * 128
                                s = slots[kt // 32]
                                ktl = kt % 32
                                diag = (kt // 8 == j0)
                                sbank = []
                                for comp in range(2):
                                    bnk = (kt % 2) * 2 + comp
                                    sbank.append(bnk)
                                    lo, hi = comp * 64, comp * 64 + 64
                                    mm(ps[bnk][:, 0:n], Kh[s][lo:hi, ktl, :], qh_s[hb][lo:hi, c0:c1], True, not diag,
                                       [R("Kh", s), R("qh", hb)], [PS[bnk]])
                                    if diag:
                                        mm(ps[bnk][:, 0:128], ident_s[:], mask_s[:, kt % 8, :], False, True,
                                           [R("ident"), R("mask")], [PS[bnk]])
                                for comp in range(2):
                                    bnk = sbank[comp]
                                    pt, ptr = nxt("pt")
                                    S.op("act", lambda e, pt=pt, bnk=bnk, n=n: e.activation(out=pt[:, 0:n], in_=ps[bnk][:, 0:n], func=AF.Exp, scale=0.125),
                                         reads=[PS[bnk]], writes=[ptr])
                                    first_kt = (kt == 0)
                                    last_kt = (kt == nkt - 1)
                                    mm(ps[4 + comp][:, off:off + n], Vh[s][:, ktl, :], pt[:, 0:n], first_kt, last_kt,
                                       [R("Vh", s), ptr], [PS[4 + comp]])
                                    mm(ps[6 + comp][:, off:off + n], onesb_s[:], pt[:, 0:n], first_kt, last_kt,
                                       [R("onesb"), ptr], [PS[6 + comp]])
                            if qh == 0 and h + 1 < 8:
                                pass
                            cs0 = qh * 512
                            r0, r0r = nxt("ef")
                            r1, r1r = nxt("ef")
                            o0, o0r = nxt("ef")
                            o1, o1r = nxt("ef")
                            sq, sqr = nxt("ef")
                            rs, rsr = nxt("ef")
                            S.op("dve", lambda e, r0=r0: e.reciprocal(r0[:], ps[6][:]), reads=[PS[6]], writes=[r0r])
                            S.op("dve", lambda e, r1=r1: e.reciprocal(r1[:], ps[7][:]), reads=[PS[7]], writes=[r1r])
                            S.op("dve", lambda e, o0=o0, r0=r0: e.tensor_tensor(o0[:], ps[4][:], r0[:], ALU.mult), reads=[PS[4], r0r], writes=[o0r])
                            S.op("dve", lambda e, o1=o1, r1=r1: e.tensor_tensor(o1[:], ps[5][:], r1[:], ALU.mult), reads=[PS[5], r1r], writes=[o1r])
                            S.op("dve", lambda e, o0=o0, o1=o1: e.scalar_tensor_tensor(o0[:], o1[:], neglam[:, 0:1], o0[:], ALU.mult, ALU.add),
                                 reads=[o0r, o1r, R("neglam")], writes=[o0r])
                            S.op("pool", lambda e, o0=o0, sq=sq: e.tensor_tensor(sq[:], o0[:], o0[:], ALU.mult), reads=[o0r], writes=[sqr])
                            mm(ps[0][:], ones_s[:], sq[:], True, True, [R("ones"), sqr], [PS[0]])
                            S.op("act", lambda e, rs=rs: e.activation(out=rs[:], in_=ps[0][:], func=AF.Sqrt, scale=1.0 / 128.0, bias=eps_s[:, 0:1]),
                                 reads=[PS[0], R("eps")], writes=[rsr])
                            S.op("dve", lambda e, rs=rs: e.reciprocal(rs[:], rs[:]), reads=[rsr], writes=[rsr])
                            S.op("dve", lambda e, o0=o0, rs=rs: e.tensor_tensor(o0[:], o0[:], rs[:], ALU.mult), reads=[o0r, rsr], writes=[o0r])
                            S.op("dve", lambda e, o0=o0, h=h, hb=hb, cs0=cs0: e.scalar_tensor_tensor(
                                onT[:, h, cs0:cs0 + 512], o0[:], subg_s[:, 0:1], zb_s[hb][:, cs0:cs0 + 512], ALU.mult, ALU.mult),
                                reads=[o0r, R("subg"), R("zbh", hb)], writes=[R("onT", h, qh)])
                            if qh == 0 and h + 1 < 8:
                                slots_next = load_head(h + 1)
                S.barrier()

                with contextlib.ExitStack() as ms:
                    merged = sb("merged", [128, 16, T], BF16, ms)
                    wbuf[0] = sb("wbufB0", [128, 16, 512], BF16, ms)
                    wbuf[1] = sb("wbufB1", [128, 16, 512], BF16, ms)
                    sga_s = [sb(f"sga{i}", [128, T], BF16, ms) for i in range(2)]
                    sgb_s = [sb(f"sgb{i}", [128, T], BF16, ms) for i in range(2)]
                    rotbuf(ms, "mf", 4, [128, 512], F32)
                    for blk in range(4):
                        sc = wload(wco_d, 8, blk * 512, 512)
                        sa = wload(wao_d, 8, blk * 512, 512)
                        for f in range(4):
                            dc = blk * 4 + f
                            gb_ = dc % 2
                            ld("sp", sga_s[gb_][:], sga_i[dc * 128:(dc + 1) * 128, :], R("sga", gb_))
                            ld("sp", sgb_s[gb_][:], sgb_i[dc * 128:(dc + 1) * 128, :], R("sgb", gb_))
                            for tb in range(2):
                                tsl = slice(tb * 512, tb * 512 + 512)
                                pc = nps()
                                for c in range(8):
                                    mm(ps[pc][:], wbuf[sc][:, c, f * 128:(f + 1) * 128], g_s[:, c, tsl], c == 0, c == 7,
                                       [R("w", sc), R("g", c)], [PS[pc]])
                                pa = nps()
                                for hh in range(8):
                                    mm(ps[pa][:], wbuf[sa][:, hh, f * 128:(f + 1) * 128], onT[:, hh, tsl], hh == 0, hh == 7,
                                       [R("w", sa), R("onT", hh, tb)], [PS[pa]])
                                t1, t1r = nxt("mf")
                                t2, t2r = nxt("mf")
                                S.op("dve", lambda e, t1=t1, pc=pc, gb_=gb_, tsl=tsl: e.tensor_tensor(t1[:], ps[pc][:], sga_s[gb_][:, tsl], ALU.mult),
                                     reads=[PS[pc], R("sga", gb_)], writes=[t1r])
                                S.op("dve", lambda e, t2=t2, pa=pa, gb_=gb_, tsl=tsl: e.tensor_tensor(t2[:], ps[pa][:], sgb_s[gb_][:, tsl], ALU.mult),
                                     reads=[PS[pa], R("sgb", gb_)], writes=[t2r])
                                S.op("pool", lambda e, t1=t1, t2=t2, dc=dc, tsl=tsl: e.tensor_tensor(merged[:, dc, tsl], t1[:], t2[:], ALU.add),
                                     reads=[t1r, t2r], writes=[R("merged", dc, tb)])
                    for blk in range(4):
                        so = wload(wo_d, 16, blk * 512, 512)
                        for f in range(4):
                            dc = blk * 4 + f
                            for tb in range(2):
                                tsl = slice(tb * 512, tb * 512 + 512)
                                po = nps()
                                for kc in range(16):
                                    mm(ps[po][:], wbuf[so][:, kc, f * 128:(f + 1) * 128], merged[:, kc, tsl], kc == 0, kc == 15,
                                       [R("w", so), R("merged", kc, tb)], [PS[po]])
                                S.op("dve", lambda e, po=po, dc=dc, tsl=tsl: e.scalar_tensor_tensor(
                                    xT[:, dc, tsl], ps[po][:], gate_s[:, dc:dc + 1], xT[:, dc, tsl], ALU.mult, ALU.add),
                                    reads=[PS[po], R("gate_s"), R("xT", dc, tb)], writes=[R("xT", dc, tb)])
                S.barrier()

        def phase_A():
            with contextlib.ExitStack() as a_:
                hT = sb("hT", [128, 16, T], BF16, a_)
                c_s = sb("c_s", [128, 16], F32, a_)
                cact = sb("cact", [128, 16], F32, a_)
                adab_s = sb("adab_s", [128, 48], F32, a_)
                ng_s = sb("ng_s", [128, 16], F32, a_)
                mod_s = sb("mod_s", [128, 48], F32, a_)
                asc = sb("asc", [128, 16], F32, a_)
                gq_s = sb("gq_s", [128, 1], F32, a_)
                gk_s = sb("gk_s", [128, 1], F32, a_)
                cos_s = sb("cos_s", [128, T], F32, a_)
                sin_s = sb("sin_s", [128, T], F32, a_)
                ld("sp", cos_s[:], cos_d, R("cos"))
                ld("sp", sin_s[:], sin_d, R("sin"))
                rotbuf(a_, "af", 12, [128, 512], F32)
                rsn = [sb(f"rsn{i}", [128, 512], F32, a_) for i in range(2)]
                rotbuf(a_, "ob", 4, [128, 512], BF16)
                rotbuf(a_, "of", 2, [128, 512], F32)
                m_ = contextlib.ExitStack()
                adaw = [sb(f"adaw{i}", [128, 16, 256], F32, m_) for i in range(2)]
                ld("sp", c_s[:], c_d, R("c_s"))
                ld("sp", adab_s[:], adab_d, R("adab"))
                ld("sp", ng_s[:], ng_d, R("ng"))
                ld("sp", gq_s[:], gq_d, R("gq"))
                ld("sp", gk_s[:], gk_d, R("gk"))
                for kc in range(16):
                    fin.append(S.dma("sp", lambda e, kc=kc: e.dma_start(out=xT_o[kc * 128:(kc + 1) * 128, :], in_=xT[:, kc, :]),
                                     reads=[R("xT", kc, 0), R("xT", kc, 1)]))
                S.op("act", lambda e: e.activation(out=cact[:], in_=c_s[:], func=AF.Silu), reads=[R("c_s")], writes=[R("cact")])
                for blk in range(24):
                    b = blk % 2
                    src = adaw_d[:, blk * 256:(blk + 1) * 256].rearrange("(kc p) n -> p kc n", p=128)
                    S.dma("sp", lambda e, b=b, src=src: e.dma_start(out=adaw[b][:], in_=src), writes=[R("adaw", b)])
                    for f in range(2):
                        cc = blk * 2 + f
                        for kc in range(16):
                            mm(ps[7][:, cc:cc + 1], adaw[b][:, kc, f * 128:(f + 1) * 128], cact[:, kc:kc + 1], kc == 0, kc == 15,
                               [R("adaw", b), R("cact")], [PS[7]])
                S.op("dve", lambda e: e.tensor_tensor(mod_s[:], ps[7][:, 0:48], adab_s[:], ALU.add), reads=[PS[7], R("adab")], writes=[R("mod")])
                S.op("dve", lambda e: e.tensor_scalar(asc[:], mod_s[:, 16:32], 1.0, None, ALU.add), reads=[R("mod")], writes=[R("asc0")])
                S.op("dve", lambda e: e.tensor_tensor(asc[:], asc[:], ng_s[:], ALU.mult), reads=[R("asc0"), R("ng")], writes=[R("asc")])
                fin.append(S.dma("sp", lambda e: e.dma_start(out=gate_o, in_=mod_s[:, 32:48]), reads=[R("mod")]))
                S.barrier()
                m_.close()
                pair = sb("pair", [128, 4, T], F32, a_)
                wbuf[0] = sb("wbufA0", [128, 16, 512], BF16, a_)
                wbuf[1] = sb("wbufA1", [128, 16, 512], BF16, a_)
                for tb in range(2):
                    tsl = slice(tb * 512, tb * 512 + 512)
                    for kc in range(16):
                        sq, sqr = nxt("af")
                        S.op("pool", lambda e, sq=sq, kc=kc, tsl=tsl: e.tensor_tensor(sq[:], xT[:, kc, tsl], xT[:, kc, tsl], ALU.mult),
                             reads=[R("xT", kc, tb)], writes=[sqr])
                        mm(ps[4][:], ones_s[:], sq[:], kc == 0, kc == 15, [R("ones"), sqr], [PS[4]])
                    rs, rsr = rsn[tb], R("rsn", tb)
                    S.op("act", lambda e, rs=rs: e.activation(out=rs[:], in_=ps[4][:], func=AF.Sqrt, scale=1.0 / D, bias=eps_s[:, 0:1]),
                         reads=[PS[4], R("eps")], writes=[rsr])
                    S.op("dve", lambda e, rs=rs: e.reciprocal(rs[:], rs[:]), reads=[rsr], writes=[rsr])
                    for kc in range(16):
                        tmp, tmr = nxt("af")
                        S.op("dve", lambda e, tmp=tmp, rs=rs, kc=kc, tsl=tsl: e.scalar_tensor_tensor(
                            tmp[:], xT[:, kc, tsl], asc[:, kc:kc + 1], rs[:], ALU.mult, ALU.mult),
                            reads=[R("xT", kc, tb), R("asc"), rsr], writes=[tmr])
                        S.op("act", lambda e, tmp=tmp, kc=kc, tsl=tsl: e.activation(
                            out=hT[:, kc, tsl], in_=tmp[:], func=AF.Identity, bias=mod_s[:, kc:kc + 1], scale=1.0),
                            reads=[tmr, R("mod")], writes=[R("hT", kc, tb)])

                if DEBUG:
                    for kc in range(16):
                        fin.append(S.dma("sp", lambda e, kc=kc: e.dma_start(out=dbg_hT[kc * 128:(kc + 1) * 128, :], in_=hT[:, kc, :]),
                                         reads=[R("hT", kc, 0), R("hT", kc, 1)]))

                def hreads(tb):
                    return [R("hT", kc, tb) for kc in range(16)]

                def proj_fm(ws, f, tb):
                    b = nps()
                    tsl = slice(tb * 512, tb * 512 + 512)
                    for kc in range(16):
                        mm(ps[b][:], wbuf[ws][:, kc, f * 128:(f + 1) * 128], hT[:, kc, tsl], kc == 0, kc == 15,
                           [R("w", ws), R("hT", kc, tb)], [PS[b]])
                    return b

                def st(dst, src_buf, reg):
                    fin.append(S.dma("sp", lambda e: e.dma_start(out=dst, in_=src_buf), reads=[reg]))

                def qk_epilogue(b, g_s_, g_r, tb, dst):
                    tsl = slice(tb * 512, tb * 512 + 512)
                    raw, rawr = nxt("af")
                    kg, kgr = nxt("af")
                    sq, sqr = nxt("af")
                    rs, rsr = nxt("af")
                    t1, t1r = nxt("af")
                    t2, t2r = nxt("af")
                    S.op("act", lambda e: e.activation(out=raw[:], in_=ps[b][:], func=AF.Identity), reads=[PS[b]], writes=[rawr])
                    S.op("dve", lambda e: e.tensor_scalar(kg[:], raw[:], g_s_[:, 0:1], None, ALU.mult), reads=[rawr, g_r], writes=[kgr])
                    S.op("pool", lambda e: e.tensor_tensor(sq[:], raw[:], raw[:], ALU.mult), reads=[rawr], writes=[sqr])
                    mm(ps[6][:], blk_s[:], sq[:], True, True, [R("blk"), sqr], [PS[6]])
                    S.op("act", lambda e: e.activation(out=rs[:], in_=ps[6][:], func=AF.Sqrt, scale=1.0 / 64.0, bias=eps_s[:, 0:1]),
                         reads=[PS[6], R("eps")], writes=[rsr])
                    S.op("dve", lambda e: e.reciprocal(rs[:], rs[:]), reads=[rsr], writes=[rsr])
                    mm(ps[5][:], rmat_s[:], kg[:], True, True, [R("rmat"), kgr], [PS[5]])
                    S.op("dve", lambda e: e.tensor_tensor(t1[:], kg[:], cos_s[:, tsl], ALU.mult), reads=[kgr, R("cos")], writes=[t1r])
                    S.op("dve", lambda e: e.tensor_tensor(t2[:], ps[5][:], sin_s[:, tsl], ALU.mult), reads=[PS[5], R("sin")], writes=[t2r])
                    S.op("pool", lambda e: e.tensor_tensor(t1[:], t1[:], t2[:], ALU.add), reads=[t1r, t2r], writes=[t1r])
                    ob, obr = nxt("ob")
                    S.op("dve", lambda e: e.tensor_tensor(ob[:], t1[:], rs[:], ALU.mult), reads=[t1r, rsr], writes=[obr])
                    st(dst, ob[:], obr)

                for blk in range(2):
                    ws = wload(win_d, 16, C_K + blk * 512, 512)
                    if DEBUG and blk == 0:
                        fin.append(S.dma("sp", lambda e, ws=ws: e.dma_start(out=dbg_w, in_=wbuf[ws][:].rearrange("p a b -> p (a b)")), reads=[R("w", ws)]))
                    for f in range(4):
                        hh = blk * 4 + f
                        for tb in range(2):
                            b = proj_fm(ws, f, tb)
                            qk_epilogue(b, gk_s, R("gk"), tb, kT_o[hh * 128:(hh + 1) * 128, tb * 512:(tb + 1) * 512])
                for blk in range(2):
                    ws = wload(win_d, 16, C_V + blk * 512, 512)
                    for tt in range(8):
                        b = nps()
                        for kc in range(16):
                            mm(ps[b][:], hT[:, kc, tt * 128:(tt + 1) * 128], wbuf[ws][:, kc, :], kc == 0, kc == 15,
                               [R("w", ws), R("hT", kc, tt // 4)], [PS[b]])
                        ob, obr = nxt("ob")
                        S.op("act", lambda e, ob=ob, b=b: e.activation(out=ob[:], in_=ps[b][:], func=AF.Identity), reads=[PS[b]], writes=[obr])
                        st(v_o[tt * 128:(tt + 1) * 128, blk * 512:(blk + 1) * 512], ob[:], obr)
                for blk in range(2):
                    ws = wload(win_d, 16, C_U + blk * 512, 512)
                    for f in range(4):
                        for tb in range(2):
                            b = proj_fm(ws, f, tb)
                            S.op("act", lambda e, b=b, f=f, tb=tb: e.activation(out=pair[:, f, tb * 512:(tb + 1) * 512], in_=ps[b][:], func=AF.Identity),
                                 reads=[PS[b]], writes=[R("pair", f, tb)])
                    ws = wload(win_d, 16, C_CG + blk * 512, 512)
                    for f in range(4):
                        c = blk * 4 + f
                        for tb in range(2):
                            b = proj_fm(ws, f, tb)
                            of, ofr = nxt("of")
                            S.op("dve", lambda e, of=of, b=b, f=f, tb=tb: e.tensor_tensor(of[:], ps[b][:], pair[:, f, tb * 512:(tb + 1) * 512], ALU.mult),
                                 reads=[PS[b], R("pair", f, tb)], writes=[ofr])
                            st(cv_o[c * 128:(c + 1) * 128, tb * 512:(tb + 1) * 512], of[:], ofr)
                for blk in range(2):
                    ws = wload(win_d, 16, C_Q + blk * 512, 512)
                    for f in range(4):
                        hh = blk * 4 + f
                        for tb in range(2):
                            b = proj_fm(ws, f, tb)
                            qk_epilogue(b, gq_s, R("gq"), tb, qT_o[hh * 128:(hh + 1) * 128, tb * 512:(tb + 1) * 512])
                for blk in range(2):
                    ws = wload(win_d, 16, C_ZA + blk * 512, 512)
                    for f in range(4):
                        for tb in range(2):
                            b = proj_fm(ws, f, tb)
                            S.op("act", lambda e, b=b, f=f, tb=tb: e.activation(out=pair[:, f, tb * 512:(tb + 1) * 512], in_=ps[b][:], func=AF.Silu),
                                 reads=[PS[b]], writes=[R("pair", f, tb)])
                    ws = wload(win_d, 16, C_BG + blk * 512, 512)
                    for f in range(4):
                        c = blk * 4 + f
                        for tb in range(2):
                            b = proj_fm(ws, f, tb)
                            ob, obr = nxt("ob")
                            S.op("dve", lambda e, ob=ob, b=b, f=f, tb=tb: e.tensor_tensor(ob[:], ps[b][:], pair[:, f, tb * 512:(tb + 1) * 512], ALU.mult),
                                 reads=[PS[b], R("pair", f, tb)], writes=[obr])
                            st(bz_o[c * 128:(c + 1) * 128, tb * 512:(tb + 1) * 512], ob[:], obr)
                for (col, nblk, func, dst) in ((C_ZB, 2, AF.Silu, szb_o), (C_GA, 4, AF.Sigmoid, sga_o), (C_GB, 4, AF.Sigmoid, sgb_o)):
                    for blk in range(nblk):
                        ws = wload(win_d, 16, col + blk * 512, 512)
                        for f in range(4):
                            c = blk * 4 + f
                            for tb in range(2):
                                b = proj_fm(ws, f, tb)
                                ob, obr = nxt("ob")
                                S.op("act", lambda e, ob=ob, b=b, func=func: e.activation(out=ob[:], in_=ps[b][:], func=func), reads=[PS[b]], writes=[obr])
                                st(dst[c * 128:(c + 1) * 128, tb * 512:(tb + 1) * 512], ob[:], obr)
                S.barrier()

        eps_s = sb("eps_s", [128, 1], F32)
        S.op("pool", lambda e: e.memset(eps_s[:], EPS), writes=[R("eps")])

        if has_B:
            phase_B()
        if has_A:
            phase_A()
        else:
            for kc in range(16):
                fin.append(S.dma("sp", lambda e, kc=kc: e.dma_start(out=xT_o[kc * 128:(kc + 1) * 128, :], in_=xT[:, kc, :]),
                                 reads=[R("xT", kc, 0), R("xT", kc, 1)]))
        S.emit(final_wait_recs=fin)
    return nc


_PROGS = {}


def _prog(has_B, has_A, first):
    key = (has_B, has_A, first)
    if key not in _PROGS:
        _PROGS[key] = build_program(has_B, has_A, first)
    return _PROGS[key]


def _consts():
    bf = ml_dtypes.bfloat16
    half = 32
    inv = (10000.0 ** (-np.arange(half, dtype=np.float64) / half))
    rmat = np.zeros((128, 128), np.float32)
    for m in range(128):
        d = m % 64
        if d < 32:
            rmat[m + 32, m] = -1.0
        else:
            rmat[m - 32, m] = 1.0
    blk = np.zeros((128, 128), np.float32)
    blk[:64, :64] = 1.0
    blk[64:, 64:] = 1.0
    per_core = []
    for i in range(NCORES):
        pos = np.concatenate([np.arange(128) + (8 * j + i) * 128 for j in range(8)]).astype(np.float64)
        ang = inv[np.arange(128) % 32][:, None] * pos[None, :]
        ang = ang.astype(np.float32).astype(np.float64)
        cos = np.cos(ang).astype(np.float32)
        sin = np.sin(ang).astype(np.float32)
        mask = np.zeros((128, 8, 128), np.float32)
        for ip in range(8):
            if ip > i:
                mask[:, ip, :] = NEG
            elif ip == i:
                kc = np.arange(128)[:, None] // 64
                qc = np.arange(128)[None, :] // 64
                mask[:, ip, :] = np.where(kc <= qc, 0.0, NEG)
        per_core.append({
            "cos": cos, "sin": sin, "rmat": rmat, "ones": np.ones((128, 128), np.float32), "blk64": blk,
            "ident": np.eye(128, dtype=np.float32).astype(bf), "onesb": np.ones((128, 128), np.float32).astype(bf),
            "mask": mask.astype(bf),
        })
    return per_core


def _pc(v):
    return np.ascontiguousarray(v.reshape(-1, 128).T)


def kernel(x, c, ada_w, ada_b, norm_g, w_in, conv_w, w_conv_out, q_norm_g, k_norm_g,
           lam_q1, lam_k1, lam_q2, lam_k2, subln_g, w_attn_out, w_o):
    f = lambda a: np.ascontiguousarray(np.asarray(a, dtype=np.float32))
    x, c, ada_w, ada_b, norm_g, w_in, conv_w, w_conv_out = map(f, (x, c, ada_w, ada_b, norm_g, w_in, conv_w, w_conv_out))
    q_norm_g, k_norm_g, lam_q1, lam_k1, lam_q2, lam_k2, subln_g, w_attn_out, w_o = map(
        f, (q_norm_g, k_norm_g, lam_q1, lam_k1, lam_q2, lam_k2, subln_g, w_attn_out, w_o))
    consts = _consts()
    xt = x[0].reshape(8, 8, 128, D)
    xT = [np.ascontiguousarray(xt[:, i].reshape(T, D).T) for i in range(NCORES)]
    state = None
    for k in range(DEPTH + 1):
        has_B = k > 0
        has_A = k < DEPTH
        nc = _prog(has_B, has_A, k == 0)
        in_maps = []
        for i in range(NCORES):
            m = dict(consts[i])
            m["xT_in"] = xT[i]
            if has_A:
                la = k
                m.update({
                    "c_vec": _pc(c[0]), "ada_w": ada_w[la], "ada_b": _pc(ada_b[la]), "norm_g": _pc(norm_g[la]),
                    "w_in": w_in[la],
                    "gq128": np.ascontiguousarray(np.tile(q_norm_g[la], 2)[:, None]),
                    "gk128": np.ascontiguousarray(np.tile(k_norm_g[la], 2)[:, None]),
                })
            if has_B:
                lb = k - 1
                lam_init = 0.8 - 0.6 * math.exp(-0.3 * lb)
                lconst = np.empty((128, 2), np.float32)
                lconst[:, 0] = lam_init
                lconst[:, 1] = 1.0 - lam_init
                m.update({
                    "qT_i": state[i]["qT_o"], "bz_i": state[i]["bz_o"], "szb_i": state[i]["szb_o"],
                    "sga_i": state[i]["sga_o"], "sgb_i": state[i]["sgb_o"], "gate_i": state[i]["gate_o"],
                    "cvh_i": state[i]["cvh"], "K_all": state["K_all"], "V_all": state["V_all"],
                    "conv_w": np.ascontiguousarray(conv_w[lb].reshape(3, 8, 128).transpose(2, 0, 1).reshape(128, 24)),
                    "w_conv_out": w_conv_out[lb], "w_attn_out": w_attn_out[lb], "w_o": w_o[lb],
                    "lamp": np.stack([lam_q1[lb], lam_k1[lb], lam_q2[lb], lam_k2[lb]]),
                    "lconst": lconst, "subg": np.ascontiguousarray(subln_g[lb][:, None]),
                })
            in_maps.append(m)
        res = run_bass_kernel_spmd(nc, in_maps, core_ids=list(range(NCORES)))
        outs = res.results
        xT = [np.asarray(outs[i]["xT_o"]) for i in range(NCORES)]
        if has_A:
            kk = np.stack([np.asarray(outs[i]["kT_o"]) for i in range(NCORES)])
            kk = kk.reshape(8, 8, 128, 8, 128).transpose(1, 2, 3, 0, 4)
            K_all = np.ascontiguousarray(kk.reshape(8, 128, SEQ))
            vv = np.stack([np.asarray(outs[i]["v_o"]) for i in range(NCORES)])
            vv = vv.reshape(8, 8, 128, 8, 128).transpose(3, 2, 1, 0, 4)
            V_all = np.ascontiguousarray(vv.reshape(8, 128, 64 * 128))
            cvs = [np.asarray(outs[i]["cv_o"]).reshape(1024, 8, 128) for i in range(NCORES)]
            state = {"K_all": K_all, "V_all": V_all}
            for i in range(NCORES):
                cvh = np.zeros((1024, 8, 130), np.float32)
                cvh[:, :, 2:] = cvs[i]
                for j in range(8):
                    g = 8 * j + i
                    if g > 0:
                        cvh[:, j, 0:2] = cvs[(g - 1) % 8][:, (g - 1) // 8, 126:128]
                st = {kname: np.asarray(outs[i][kname]) for kname in ("qT_o", "bz_o", "szb_o", "sga_o", "sgb_o", "gate_o")}
                st["cvh"] = cvh.reshape(1024, 8 * 130)
                state[i] = st
    out = np.empty((8, 8, 128, D), np.float32)
    for i in range(NCORES):
        out[:, i] = xT[i].T.reshape(8, 128, D)
    return out.reshape(1, SEQ, D)
```

```python
import math
import contextlib
import numpy as np
import ml_dtypes
import concourse.bass as bass
import concourse.mybir as mybir
from concourse.bass_utils import run_bass_kernel_spmd

F32 = mybir.dt.float32
BF16 = mybir.dt.bfloat16
AF = mybir.ActivationFunctionType
ALU = mybir.AluOpType
AX = mybir.AxisListType

NCORES = 8
D = 2048
SEQ = 8192
T = 1024
DEPTH = 4
DIN = 12288
EPS = 1e-6
C_U, C_BG, C_CG, C_ZA, C_Q, C_K, C_V, C_ZB, C_GA, C_GB = 0, 1024, 2048, 3072, 4096, 5120, 6144, 7168, 8192, 10240
NEG = -30000.0

ENGS = ("pe", "act", "dve", "pool", "sp")
SEM_LIMIT = 30000


class Region:
    __slots__ = ("w", "rc", "rd")

    def __init__(self):
        self.w = None
        self.rc = {}
        self.rd = []


class Rec:
    __slots__ = ("eng", "fn", "deps", "needs_inc", "is_dma", "dma_slot", "dma_val", "semref")

    def __init__(self, eng, fn, is_dma):
        self.eng = eng
        self.fn = fn
        self.deps = []
        self.needs_inc = False
        self.is_dma = is_dma
        self.dma_slot = None
        self.dma_val = None
        self.semref = None


class Sched:
    def __init__(self, nc, n_dma_sems=32, same_engine_sync=True):
        self.nc = nc
        self.q = {e: [] for e in ENGS}
        self.n_dma_sems = n_dma_sems
        self.dma_count = 0
        self.dma_slot_last = [None] * n_dma_sems
        self.same_engine_sync = same_engine_sync
        self.regions = {}
        self.pending = {}

    def R(self, *key):
        r = self.regions.get(key)
        if r is None:
            r = Region()
            self.regions[key] = r
        return r

    def barrier(self):
        deps = [self.q[e][-1] for e in ENGS if self.q[e]]
        deps += [d for d in self.dma_slot_last if d is not None]
        for e in ENGS:
            self.pending[e] = list(deps)

    def _add(self, eng, fn, reads, writes, is_dma):
        rec = Rec(eng, fn, is_dma)
        deps = []
        for r in reads:
            if r.w is not None:
                deps.append(r.w)
        for w in writes:
            if w.w is not None:
                deps.append(w.w)
            deps.extend(w.rc.values())
            deps.extend(w.rd)
        if eng in self.pending:
            deps.extend(self.pending.pop(eng))
        if is_dma:
            slot = self.dma_count % self.n_dma_sems
            rec.dma_slot = slot
            rec.dma_val = 16 * (self.dma_count // self.n_dma_sems + 1)
            prev = self.dma_slot_last[slot]
            if prev is not None:
                deps.append(prev)
            self.dma_slot_last[slot] = rec
            self.dma_count += 1
        seen = set()
        for d in deps:
            if d is rec or id(d) in seen:
                continue
            seen.add(id(d))
            if (not d.is_dma) and d.eng == eng and not is_dma:
                if eng == "pe" or not self.same_engine_sync:
                    continue
            rec.deps.append(d)
            if not d.is_dma:
                d.needs_inc = True
        self.q[eng].append(rec)
        for r in reads:
            if is_dma:
                r.rd.append(rec)
            else:
                r.rc[eng] = rec
        for w in writes:
            w.w = rec
            w.rc = {}
            w.rd = []
        return rec

    def op(self, eng, fn, reads=(), writes=()):
        return self._add(eng, fn, reads, writes, False)

    def dma(self, eng, fn, reads=(), writes=()):
        return self._add(eng, fn, reads, writes, True)

    def emit(self, final_wait_recs=()):
        nc = self.nc
        with contextlib.ExitStack() as es:
            for e in ENGS:
                cnt = 0
                sems = []
                for rec in self.q[e]:
                    if rec.is_dma or not rec.needs_inc:
                        continue
                    si = cnt // SEM_LIMIT
                    while len(sems) <= si:
                        sems.append(es.enter_context(nc.semaphore(f"s_{e}_{len(sems)}")))
                    rec.semref = (sems[si], cnt % SEM_LIMIT + 1)
                    cnt += 1
            dma_sems = [es.enter_context(nc.semaphore(f"s_dma_{i}")) for i in range(self.n_dma_sems)]
            for e in ENGS:
                for rec in self.q[e]:
                    if rec.is_dma:
                        rec.semref = (dma_sems[rec.dma_slot], rec.dma_val)
            block = es.enter_context(nc.Block())
            engmap = {"pe": block.tensor, "act": block.scalar, "dve": block.vector,
                      "pool": block.gpsimd, "sp": block.sync}

            def make(e):
                def body(engine):
                    known = {}
                    for rec in self.q[e]:
                        for d in rec.deps:
                            sem, val = d.semref
                            k = id(sem)
                            if known.get(k, 0) >= val:
                                continue
                            known[k] = val
                            engine.wait_ge(sem, val)
                        ins = rec.fn(engine)
                        if rec.is_dma:
                            ins.then_inc(rec.semref[0], 16)
                        elif rec.needs_inc:
                            ins.then_inc(rec.semref[0], 1)
                    if e == "sp":
                        for d in final_wait_recs:
                            sem, val = d.semref
                            engine.wait_ge(sem, val)
                return body

            for e in ENGS:
                engmap[e](make(e))


DEBUG = False


def build_program(has_B, has_A, first):
    nc = bass.Bass("TRN2", target_bir_lowering=False)
    S = Sched(nc)
    R = S.R

    def din(name, shape, dt=F32):
        return nc.dram_tensor(name, list(shape), dt, kind="ExternalInput").ap()

    def dout(name, shape, dt=F32):
        return nc.dram_tensor(name, list(shape), dt, kind="ExternalOutput").ap()

    cos_d = din("cos", [128, T])
    sin_d = din("sin", [128, T])
    rmat_d = din("rmat", [128, 128])
    ones_d = din("ones", [128, 128])
    blk_d = din("blk64", [128, 128])
    ident_d = din("ident", [128, 128], BF16)
    onesb_d = din("onesb", [128, 128], BF16)
    mask_d = din("mask", [128, 8, 128], BF16)
    xT_in = din("xT_in", [D, T])

    if has_A:
        c_d = din("c_vec", [128, 16])
        adaw_d = din("ada_w", [D, 3 * D])
        adab_d = din("ada_b", [128, 48])
        ng_d = din("norm_g", [128, 16])
        win_d = din("w_in", [D, DIN])
        gq_d = din("gq128", [128, 1])
        gk_d = din("gk128", [128, 1])
        kT_o = dout("kT_o", [1024, T], BF16)
        v_o = dout("v_o", [T, 1024], BF16)
        cv_o = dout("cv_o", [1024, T])
        qT_o = dout("qT_o", [1024, T], BF16)
        bz_o = dout("bz_o", [1024, T], BF16)
        szb_o = dout("szb_o", [1024, T], BF16)
        sga_o = dout("sga_o", [D, T], BF16)
        sgb_o = dout("sgb_o", [D, T], BF16)
        gate_o = dout("gate_o", [128, 16])
        if DEBUG:
            dbg_hT = dout("dbg_hT", [D, T], BF16)
            dbg_w = dout("dbg_w", [128, 16 * 512], BF16)
    if has_B:
        qT_i = din("qT_i", [1024, T], BF16)
        bz_i = din("bz_i", [1024, T], BF16)
        szb_i = din("szb_i", [1024, T], BF16)
        sga_i = din("sga_i", [D, T], BF16)
        sgb_i = din("sgb_i", [D, T], BF16)
        gate_i = din("gate_i", [128, 16])
        cvh_i = din("cvh_i", [1024, 8 * 130])
        K_i = din("K_all", [8, 128, SEQ], BF16)
        V_i = din("V_all", [8, 128, 64 * 128], BF16)
        cw_d = din("conv_w", [128, 24])
        wco_d = din("w_conv_out", [1024, D])
        wao_d = din("w_attn_out", [1024, D])
        wo_d = din("w_o", [D, D])
        lamp_d = din("lamp", [4, 64])
        lconst_d = din("lconst", [128, 2])
        subg_d = din("subg", [128, 1])
    xT_o = dout("xT_o", [D, T])

    fin = []
    with contextlib.ExitStack() as es:
        def sb(name, shape, dt, stack=es):
            return stack.enter_context(nc.sbuf_tensor(name, list(shape), dt))

        ps = [es.enter_context(nc.psum_tensor(f"ps{i}", [128, 512], F32)) for i in range(8)]
        PS = [R("ps", i) for i in range(8)]

        xT = sb("xT", [128, 16, T], F32)
        rmat_s = sb("rmat_s", [128, 128], F32)
        ones_s = sb("ones_s", [128, 128], F32)
        blk_s = sb("blk_s", [128, 128], F32)
        ident_s = sb("ident_s", [128, 128], BF16)
        onesb_s = sb("onesb_s", [128, 128], BF16)
        mask_s = sb("mask_s", [128, 8, 128], BF16)
        wbuf = [None, None]

        def ld(eng, dst, src, reg):
            return S.dma(eng, lambda e: e.dma_start(out=dst, in_=src), writes=[reg])

        ld("sp", rmat_s[:], rmat_d, R("rmat"))
        ld("sp", ones_s[:], ones_d, R("ones"))
        ld("sp", blk_s[:], blk_d, R("blk"))
        ld("sp", ident_s[:], ident_d, R("ident"))
        ld("sp", onesb_s[:], onesb_d, R("onesb"))
        ld("sp", mask_s[:], mask_d, R("mask"))
        for kc in range(16):
            S.dma("sp", lambda e, kc=kc: e.dma_start(out=xT[:, kc, :], in_=xT_in[kc * 128:(kc + 1) * 128, :]),
                  writes=[R("xT", kc, 0), R("xT", kc, 1)])

        wstate = {"n": 0}

        def wload(w_ap, nk, col0, ncols):
            s = wstate["n"] % 2
            wstate["n"] += 1
            src = w_ap[:, col0:col0 + ncols].rearrange("(kc p) n -> p kc n", p=128)
            dst = wbuf[s][:, 0:nk, 0:ncols]
            S.dma("pool", lambda e: e.dma_start(out=dst, in_=src), writes=[R("w", s)])
            return s

        def mm(out, lhsT, rhs, start, stop, reads, writes):
            S.op("pe", lambda e: e.matmul(out, lhsT, rhs, start=start, stop=stop), reads=reads, writes=writes)

        rot = {}

        def rotbuf(stack, name, n, shape, dt):
            bufs = [sb(f"{name}{i}", shape, dt, stack) for i in range(n)]
            rot[name] = [0, n, bufs]

        def nxt(name):
            st = rot[name]
            i = st[0] % st[1]
            st[0] += 1
            return st[2][i], R(name, i)

        psr = {"n": 0}

        def nps(lo=0, n=4):
            i = lo + psr["n"] % n
            psr["n"] += 1
            return i

        def phase_B():
            with contextlib.ExitStack() as bs:
                gate_s = sb("gate_s", [128, 16], F32, bs)
                cw_s = sb("cw_s", [128, 24], F32, bs)
                lamp_s = sb("lamp_s", [128, 4, 64], F32, bs)
                lconst_s = sb("lconst_s", [128, 2], F32, bs)
                subg_s = sb("subg_s", [128, 1], F32, bs)
                lam_t = sb("lam_t", [128, 2, 64], F32, bs)
                lam_r = sb("lam_r", [128, 4], F32, bs)
                neglam = sb("neglam", [128, 1], F32, bs)
                g_s = sb("g_s", [128, 8, T], BF16, bs)
                onT = sb("onT", [128, 8, T], BF16, bs)
                ld("sp", gate_s[:], gate_i, R("gate_s"))
                ld("sp", cw_s[:], cw_d, R("cw"))
                ld("sp", lconst_s[:], lconst_d, R("lconst"))
                ld("sp", subg_s[:], subg_d, R("subg"))
                lamp_b = bass.AP(lamp_d.tensor, 0, [[0, 128], [1, 256]])
                ld("sp", lamp_s[:].rearrange("p a b -> p (a b)"), lamp_b, R("lamp"))
                S.op("dve", lambda e: e.tensor_tensor(lam_t[:, 0, :], lamp_s[:, 0, :], lamp_s[:, 1, :], ALU.mult), reads=[R("lamp")], writes=[R("lamt0")])
                S.op("dve", lambda e: e.tensor_tensor(lam_t[:, 1, :], lamp_s[:, 2, :], lamp_s[:, 3, :], ALU.mult), reads=[R("lamp")], writes=[R("lamt1")])
                S.op("dve", lambda e: e.reduce_sum(lam_r[:, 0:1], lam_t[:, 0, :], axis=AX.X), reads=[R("lamt0")], writes=[R("lamr0")])
                S.op("dve", lambda e: e.reduce_sum(lam_r[:, 1:2], lam_t[:, 1, :], axis=AX.X), reads=[R("lamt1")], writes=[R("lamr1")])
                S.op("act", lambda e: e.activation(out=lam_r[:, 2:4], in_=lam_r[:, 0:2], func=AF.Exp), reads=[R("lamr0"), R("lamr1")], writes=[R("lamr2")])
                S.op("dve", lambda e: e.tensor_tensor(neglam[:], lam_r[:, 3:4], lam_r[:, 2:3], ALU.subtract), reads=[R("lamr2")], writes=[R("neglam0")])
                S.op("dve", lambda e: e.tensor_tensor(neglam[:], neglam[:], lconst_s[:, 0:1], ALU.subtract), reads=[R("neglam0"), R("lconst")], writes=[R("neglam")])
                S.op("dve", lambda e: e.tensor_tensor(subg_s[:], subg_s[:], lconst_s[:, 1:2], ALU.mult), reads=[R("subg"), R("lconst")], writes=[R("subg")])

                with contextlib.ExitStack() as cs:
                    cvh = [sb(f"cvh{i}", [128, 8, 130], F32, cs) for i in range(2)]
                    bzs = [sb(f"bzs{i}", [128, T], BF16, cs) for i in range(2)]
                    ytmp = [sb(f"ytmp{i}", [128, 8, 128], F32, cs) for i in range(2)]
                    for c in range(8):
                        b = c % 2
                        ld("sp", cvh[b][:].rearrange("p a b -> p (a b)"), cvh_i[c * 128:(c + 1) * 128, :], R("cvh", b))
                        ld("sp", bzs[b][:], bz_i[c * 128:(c + 1) * 128, :], R("bzs", b))
                        y = ytmp[b]
                        cv = cvh[b]
                        S.op("dve", lambda e, y=y, cv=cv, c=c: e.tensor_scalar(y[:], cv[:, :, 2:130], cw_s[:, 16 + c:17 + c], None, ALU.mult),
                             reads=[R("cvh", b), R("cw")], writes=[R("ytmp", b)])
                        S.op("dve", lambda e, y=y, cv=cv, c=c: e.scalar_tensor_tensor(y[:], cv[:, :, 1:129], cw_s[:, 8 + c:9 + c], y[:], ALU.mult, ALU.add),
                             reads=[R("cvh", b), R("cw"), R("ytmp", b)], writes=[R("ytmp", b)])
                        S.op("dve", lambda e, y=y, cv=cv, c=c: e.scalar_tensor_tensor(y[:], cv[:, :, 0:128], cw_s[:, c:c + 1], y[:], ALU.mult, ALU.add),
                             reads=[R("cvh", b), R("cw"), R("ytmp", b)], writes=[R("ytmp", b)])
                        bz = bzs[b]
                        S.op("dve", lambda e, y=y, bz=bz, c=c: e.tensor_tensor(g_s[:, c, :], y[:].rearrange("p a b -> p (a b)"), bz[:], ALU.mult),
                             reads=[R("ytmp", b), R("bzs", b)], writes=[R("g", c)])
                S.barrier()

                with contextlib.ExitStack() as as_:
                    Kh = [sb(f"Kh{i}", [128, 32, 128], BF16, as_) for i in range(4)]
                    Vh = [sb(f"Vh{i}", [128, 32, 128], BF16, as_) for i in range(4)]
                    qh_s = [sb(f"qh{i}", [128, T], BF16, as_) for i in range(2)]
                    zb_s = [sb(f"zbh{i}", [128, T], BF16, as_) for i in range(2)]
                    rotbuf(as_, "pt", 4, [128, 512], BF16)
                    rotbuf(as_, "ef", 12, [128, 512], F32)
                    hslot = {"n": 0}

                    def load_half(h, hf):
                        s = hslot["n"] % 4
                        hslot["n"] += 1
                        ld("sp", Kh[s][:].rearrange("p a b -> p (a b)"), K_i[h, :, hf * 4096:(hf + 1) * 4096], R("Kh", s))
                        ld("sp", Vh[s][:].rearrange("p a b -> p (a b)"), V_i[h, :, hf * 4096:(hf + 1) * 4096], R("Vh", s))
                        return s

                    def load_head(h):
                        b = h % 2
                        ld("sp", qh_s[b][:], qT_i[h * 128:(h + 1) * 128, :], R("qh", b))
                        ld("sp", zb_s[b][:], szb_i[h * 128:(h + 1) * 128, :], R("zbh", b))
                        return (load_half(h, 0), load_half(h, 1))

                    slots_next = load_head(0)
                    deferred = []

                    def rec_S(h, hb, slots, qh, kt):
                        j0 = max(kt // 8, 4 * qh)
                        c0 = j0 * 128
                        c1 = (4 * qh + 4) * 128
                        n = c1 - c0
                        s = slots[kt // 32]
                        ktl = kt % 32
                        diag = (kt // 8 == j0)
                        for comp in range(2):
                            bnk = (kt % 2) * 2 + comp
                            lo, hi = comp * 64, comp * 64 + 64
                            mm(ps[bnk][:, 0:n], Kh[s][lo:hi, ktl, :], qh_s[hb][lo:hi, c0:c1], True, not diag,
                               [R("Kh", s), R("qh", hb)], [PS[bnk]])
                            if diag:
                                mm(ps[bnk][:, 0:128], ident_s[:], mask_s[:, kt % 8, :], False, True,
                                   [R("ident"), R("mask")], [PS[bnk]])

                    def rec_PV(h, hb, slots, qh, kt, nkt):
                        j0 = max(kt // 8, 4 * qh)
                        c0 = j0 * 128
                        n = (4 * qh + 4) * 128 - c0
                        off = c0 - 4 * qh * 128
                        s = slots[kt // 32]
                        ktl = kt % 32
                        for comp in range(2):
                            bnk = (kt % 2) * 2 + comp
                            pt, ptr = nxt("pt")
                            S.op("act", lambda e, pt=pt, bnk=bnk, n=n: e.activation(out=pt[:, 0:n], in_=ps[bnk][:, 0:n], func=AF.Exp, scale=0.125),
                                 reads=[PS[bnk]], writes=[ptr])
                            mm(ps[4 + comp][:, off:off + n], Vh[s][:, ktl, :], pt[:, 0:n], kt == 0, kt == nkt - 1,
                               [R("Vh", s), ptr], [PS[4 + comp]])
                            mm(ps[6 + comp][:, off:off + n], onesb_s[:], pt[:, 0:n], kt == 0, kt == nkt - 1,
                               [R("onesb"), ptr], [PS[6 + comp]])

                    def epilogue(h, hb, qh):
                        cs0 = qh * 512
                        r0, r0r = nxt("ef")
                        r1, r1r = nxt("ef")
                        o0, o0r = nxt("ef")
                        o1, o1r = nxt("ef")
                        sq, sqr = nxt("ef")
                        rs, rsr = nxt("ef")
                        S.op("dve", lambda e: e.reciprocal(r0[:], ps[6][:]), reads=[PS[6]], writes=[r0r])
                        S.op("dve", lambda e: e.reciprocal(r1[:], ps[7][:]), reads=[PS[7]], writes=[r1r])
                        S.op("dve", lambda e: e.tensor_tensor(o0[:], ps[4][:], r0[:], ALU.mult), reads=[PS[4], r0r], writes=[o0r])
                        S.op("dve", lambda e: e.tensor_tensor(o1[:], ps[5][:], r1[:], ALU.mult), reads=[PS[5], r1r], writes=[o1r])

                        def part2():
                            S.op("dve", lambda e: e.scalar_tensor_tensor(o0[:], o1[:], neglam[:, 0:1], o0[:], ALU.mult, ALU.add),
                                 reads=[o0r, o1r, R("neglam")], writes=[o0r])
                            S.op("pool", lambda e: e.tensor_tensor(sq[:], o0[:], o0[:], ALU.mult), reads=[o0r], writes=[sqr])
                            mm(ps[0][:], ones_s[:], sq[:], True, True, [R("ones"), sqr], [PS[0]])
                            S.op("act", lambda e: e.activation(out=rs[:], in_=ps[0][:], func=AF.Sqrt, scale=1.0 / 128.0, bias=eps_s[:, 0:1]),
                                 reads=[PS[0], R("eps")], writes=[rsr])
                            S.op("dve", lambda e: e.reciprocal(rs[:], rs[:]), reads=[rsr], writes=[rsr])
                            S.op("dve", lambda e: e.tensor_tensor(o0[:], o0[:], rs[:], ALU.mult), reads=[o0r, rsr], writes=[o0r])
                            S.op("dve", lambda e: e.scalar_tensor_tensor(
                                onT[:, h, cs0:cs0 + 512], o0[:], subg_s[:, 0:1], zb_s[hb][:, cs0:cs0 + 512], ALU.mult, ALU.mult),
                                reads=[o0r, R("subg"), R("zbh", hb)], writes=[R("onT", h, qh)])
                        deferred.append(part2)

                    for h in range(8):
                        slots = slots_next
                        hb = h % 2
                        for qh in range(2):
                            nkt = 32 * qh + 32
                            rec_S(h, hb, slots, qh, 0)
                            for kt in range(nkt):
                                if kt + 1 < nkt:
                                    rec_S(h, hb, slots, qh, kt + 1)
                                rec_PV(h, hb, slots, qh, kt, nkt)
                                if kt == 0 and deferred:
                                    deferred.pop(0)()
                            epilogue(h, hb, qh)
                            if qh == 0 and h + 1 < 8:
                                slots_next = load_head(h + 1)
                    while deferred:
                        deferred.pop(0)()
                S.barrier()

                with contextlib.ExitStack() as ms:
                    merged = sb("merged", [128, 16, T], BF16, ms)
                    wbuf[0] = sb("wbufB0", [128, 16, 512], BF16, ms)
                    wbuf[1] = sb("wbufB1", [128, 16, 512], BF16, ms)
                    sga_s = [sb(f"sga{i}", [128, T], BF16, ms) for i in range(2)]
                    sgb_s = [sb(f"sgb{i}", [128, T], BF16, ms) for i in range(2)]
                    rotbuf(ms, "mf", 4, [128, 512], F32)
                    for blk in range(4):
                        sc = wload(wco_d, 8, blk * 512, 512)
                        sa = wload(wao_d, 8, blk * 512, 512)
                        for f in range(4):
                            dc = blk * 4 + f
                            gb_ = dc % 2
                            ld("sp", sga_s[gb_][:], sga_i[dc * 128:(dc + 1) * 128, :], R("sga", gb_))
                            ld("sp", sgb_s[gb_][:], sgb_i[dc * 128:(dc + 1) * 128, :], R("sgb", gb_))
                            for tb in range(2):
                                tsl = slice(tb * 512, tb * 512 + 512)
                                pc = nps()
                                for c in range(8):
                                    mm(ps[pc][:], wbuf[sc][:, c, f * 128:(f + 1) * 128], g_s[:, c, tsl], c == 0, c == 7,
                                       [R("w", sc), R("g", c)], [PS[pc]])
                                pa = nps()
                                for hh in range(8):
                                    mm(ps[pa][:], wbuf[sa][:, hh, f * 128:(f + 1) * 128], onT[:, hh, tsl], hh == 0, hh == 7,
                                       [R("w", sa), R("onT", hh, tb)], [PS[pa]])
                                t1, t1r = nxt("mf")
                                t2, t2r = nxt("mf")
                                S.op("dve", lambda e, t1=t1, pc=pc, gb_=gb_, tsl=tsl: e.tensor_tensor(t1[:], ps[pc][:], sga_s[gb_][:, tsl], ALU.mult),
                                     reads=[PS[pc], R("sga", gb_)], writes=[t1r])
                                S.op("dve", lambda e, t2=t2, pa=pa, gb_=gb_, tsl=tsl: e.tensor_tensor(t2[:], ps[pa][:], sgb_s[gb_][:, tsl], ALU.mult),
                                     reads=[PS[pa], R("sgb", gb_)], writes=[t2r])
                                S.op("pool", lambda e, t1=t1, t2=t2, dc=dc, tsl=tsl: e.tensor_tensor(merged[:, dc, tsl], t1[:], t2[:], ALU.add),
                                     reads=[t1r, t2r], writes=[R("merged", dc, tb)])
                    for blk in range(4):
                        so = wload(wo_d, 16, blk * 512, 512)
                        for f in range(4):
                            dc = blk * 4 + f
                            for tb in range(2):
                                tsl = slice(tb * 512, tb * 512 + 512)
                                po = nps()
                                for kc in range(16):
                                    mm(ps[po][:], wbuf[so][:, kc, f * 128:(f + 1) * 128], merged[:, kc, tsl], kc == 0, kc == 15,
                                       [R("w", so), R("merged", kc, tb)], [PS[po]])
                                S.op("dve", lambda e, po=po, dc=dc, tsl=tsl: e.scalar_tensor_tensor(
                                    xT[:, dc, tsl], ps[po][:], gate_s[:, dc:dc + 1], xT[:, dc, tsl], ALU.mult, ALU.add),
                                    reads=[PS[po], R("gate_s"), R("xT", dc, tb)], writes=[R("xT", dc, tb)])
                S.barrier()

        def phase_A():
            with contextlib.ExitStack() as a_:
                hT = sb("hT", [128, 16, T], BF16, a_)
                c_s = sb("c_s", [128, 16], F32, a_)
                cact = sb("cact", [128, 16], F32, a_)
                adab_s = sb("adab_s", [128, 48], F32, a_)
                ng_s = sb("ng_s", [128, 16], F32, a_)
                mod_s = sb("mod_s", [128, 48], F32, a_)
                asc = sb("asc", [128, 16], F32, a_)
                gq_s = sb("gq_s", [128, 1], F32, a_)
                gk_s = sb("gk_s", [128, 1], F32, a_)
                cos_s = sb("cos_s", [128, T], F32, a_)
                sin_s = sb("sin_s", [128, T], F32, a_)
                ld("sp", cos_s[:], cos_d, R("cos"))
                ld("sp", sin_s[:], sin_d, R("sin"))
                rotbuf(a_, "af", 12, [128, 512], F32)
                rsn = [sb(f"rsn{i}", [128, 512], F32, a_) for i in range(2)]
                rotbuf(a_, "ob", 4, [128, 512], BF16)
                rotbuf(a_, "of", 2, [128, 512], F32)
                m_ = contextlib.ExitStack()
                adaw = [sb(f"adaw{i}", [128, 16, 256], F32, m_) for i in range(2)]
                ld("sp", c_s[:], c_d, R("c_s"))
                ld("sp", adab_s[:], adab_d, R("adab"))
                ld("sp", ng_s[:], ng_d, R("ng"))
                ld("sp", gq_s[:], gq_d, R("gq"))
                ld("sp", gk_s[:], gk_d, R("gk"))
                for kc in range(16):
                    fin.append(S.dma("sp", lambda e, kc=kc: e.dma_start(out=xT_o[kc * 128:(kc + 1) * 128, :], in_=xT[:, kc, :]),
                                     reads=[R("xT", kc, 0), R("xT", kc, 1)]))
                S.op("act", lambda e: e.activation(out=cact[:], in_=c_s[:], func=AF.Silu), reads=[R("c_s")], writes=[R("cact")])
                for blk in range(24):
                    b = blk % 2
                    src = adaw_d[:, blk * 256:(blk + 1) * 256].rearrange("(kc p) n -> p kc n", p=128)
                    S.dma("sp", lambda e, b=b, src=src: e.dma_start(out=adaw[b][:], in_=src), writes=[R("adaw", b)])
                    for f in range(2):
                        cc = blk * 2 + f
                        for kc in range(16):
                            mm(ps[7][:, cc:cc + 1], adaw[b][:, kc, f * 128:(f + 1) * 128], cact[:, kc:kc + 1], kc == 0, kc == 15,
                               [R("adaw", b), R("cact")], [PS[7]])
                S.op("dve", lambda e: e.tensor_tensor(mod_s[:], ps[7][:, 0:48], adab_s[:], ALU.add), reads=[PS[7], R("adab")], writes=[R("mod")])
                S.op("dve", lambda e: e.tensor_scalar(asc[:], mod_s[:, 16:32], 1.0, None, ALU.add), reads=[R("mod")], writes=[R("asc0")])
                S.op("dve", lambda e: e.tensor_tensor(asc[:], asc[:], ng_s[:], ALU.mult), reads=[R("asc0"), R("ng")], writes=[R("asc")])
                fin.append(S.dma("sp", lambda e: e.dma_start(out=gate_o, in_=mod_s[:, 32:48]), reads=[R("mod")]))
                S.barrier()
                m_.close()
                pair = sb("pair", [128, 4, T], F32, a_)
                wbuf[0] = sb("wbufA0", [128, 16, 512], BF16, a_)
                wbuf[1] = sb("wbufA1", [128, 16, 512], BF16, a_)
                for tb in range(2):
                    tsl = slice(tb * 512, tb * 512 + 512)
                    for kc in range(16):
                        sq, sqr = nxt("af")
                        S.op("pool", lambda e, sq=sq, kc=kc, tsl=tsl: e.tensor_tensor(sq[:], xT[:, kc, tsl], xT[:, kc, tsl], ALU.mult),
                             reads=[R("xT", kc, tb)], writes=[sqr])
                        mm(ps[4][:], ones_s[:], sq[:], kc == 0, kc == 15, [R("ones"), sqr], [PS[4]])
                    rs, rsr = rsn[tb], R("rsn", tb)
                    S.op("act", lambda e, rs=rs: e.activation(out=rs[:], in_=ps[4][:], func=AF.Sqrt, scale=1.0 / D, bias=eps_s[:, 0:1]),
                         reads=[PS[4], R("eps")], writes=[rsr])
                    S.op("dve", lambda e, rs=rs: e.reciprocal(rs[:], rs[:]), reads=[rsr], writes=[rsr])
                    for kc in range(16):
                        tmp, tmr = nxt("af")
                        S.op("dve", lambda e, tmp=tmp, rs=rs, kc=kc, tsl=tsl: e.scalar_tensor_tensor(
                            tmp[:], xT[:, kc, tsl], asc[:, kc:kc + 1], rs[:], ALU.mult, ALU.mult),
                            reads=[R("xT", kc, tb), R("asc"), rsr], writes=[tmr])
                        S.op("act", lambda e, tmp=tmp, kc=kc, tsl=tsl: e.activation(
                            out=hT[:, kc, tsl], in_=tmp[:], func=AF.Identity, bias=mod_s[:, kc:kc + 1], scale=1.0),
                            reads=[tmr, R("mod")], writes=[R("hT", kc, tb)])

                if DEBUG:
                    for kc in range(16):
                        fin.append(S.dma("sp", lambda e, kc=kc: e.dma_start(out=dbg_hT[kc * 128:(kc + 1) * 128, :], in_=hT[:, kc, :]),
                                         reads=[R("hT", kc, 0), R("hT", kc, 1)]))

                def hreads(tb):
                    return [R("hT", kc, tb) for kc in range(16)]

                def proj_fm(ws, f, tb):
                    b = nps()
                    tsl = slice(tb * 512, tb * 512 + 512)
                    for kc in range(16):
                        mm(ps[b][:], wbuf[ws][:, kc, f * 128:(f + 1) * 128], hT[:, kc, tsl], kc == 0, kc == 15,
                           [R("w", ws), R("hT", kc, tb)], [PS[b]])
                    return b

                def st(dst, src_buf, reg):
                    fin.append(S.dma("sp", lambda e: e.dma_start(out=dst, in_=src_buf), reads=[reg]))

                def qk_epilogue(b, g_s_, g_r, tb, dst):
                    tsl = slice(tb * 512, tb * 512 + 512)
                    raw, rawr = nxt("af")
                    kg, kgr = nxt("af")
                    sq, sqr = nxt("af")
                    rs, rsr = nxt("af")
                    t1, t1r = nxt("af")
                    t2, t2r = nxt("af")
                    S.op("act", lambda e: e.activation(out=raw[:], in_=ps[b][:], func=AF.Identity), reads=[PS[b]], writes=[rawr])
                    S.op("dve", lambda e: e.tensor_scalar(kg[:], raw[:], g_s_[:, 0:1], None, ALU.mult), reads=[rawr, g_r], writes=[kgr])
                    S.op("pool", lambda e: e.tensor_tensor(sq[:], raw[:], raw[:], ALU.mult), reads=[rawr], writes=[sqr])
                    mm(ps[6][:], blk_s[:], sq[:], True, True, [R("blk"), sqr], [PS[6]])
                    S.op("act", lambda e: e.activation(out=rs[:], in_=ps[6][:], func=AF.Sqrt, scale=1.0 / 64.0, bias=eps_s[:, 0:1]),
                         reads=[PS[6], R("eps")], writes=[rsr])
                    S.op("dve", lambda e: e.reciprocal(rs[:], rs[:]), reads=[rsr], writes=[rsr])
                    mm(ps[5][:], rmat_s[:], kg[:], True, True, [R("rmat"), kgr], [PS[5]])
                    S.op("dve", lambda e: e.tensor_tensor(t1[:], kg[:], cos_s[:, tsl], ALU.mult), reads=[kgr, R("cos")], writes=[t1r])
                    S.op("dve", lambda e: e.tensor_tensor(t2[:], ps[5][:], sin_s[:, tsl], ALU.mult), reads=[PS[5], R("sin")], writes=[t2r])
                    S.op("pool", lambda e: e.tensor_tensor(t1[:], t1[:], t2[:], ALU.add), reads=[t1r, t2r], writes=[t1r])
                    ob, obr = nxt("ob")
                    S.op("dve", lambda e: e.tensor_tensor(ob[:], t1[:], rs[:], ALU.mult), reads=[t1r, rsr], writes=[obr])
                    st(dst, ob[:], obr)

                for blk in range(2):
                    ws = wload(win_d, 16, C_K + blk * 512, 512)
                    if DEBUG and blk == 0:
                        fin.append(S.dma("sp", lambda e, ws=ws: e.dma_start(out=dbg_w, in_=wbuf[ws][:].rearrange("p a b -> p (a b)")), reads=[R("w", ws)]))
                    for f in range(4):
                        hh = blk * 4 + f
                        for tb in range(2):
                            b = proj_fm(ws, f, tb)
                            qk_epilogue(b, gk_s, R("gk"), tb, kT_o[hh * 128:(hh + 1) * 128, tb * 512:(tb + 1) * 512])
                for blk in range(2):
                    ws = wload(win_d, 16, C_V + blk * 512, 512)
                    for tt in range(8):
                        b = nps()
                        for kc in range(16):
                            mm(ps[b][:], hT[:, kc, tt * 128:(tt + 1) * 128], wbuf[ws][:, kc, :], kc == 0, kc == 15,
                               [R("w", ws), R("hT", kc, tt // 4)], [PS[b]])
                        ob, obr = nxt("ob")
                        S.op("act", lambda e, ob=ob, b=b: e.activation(out=ob[:], in_=ps[b][:], func=AF.Identity), reads=[PS[b]], writes=[obr])
                        st(v_o[tt * 128:(tt + 1) * 128, blk * 512:(blk + 1) * 512], ob[:], obr)
                for blk in range(2):
                    ws = wload(win_d, 16, C_U + blk * 512, 512)
                    for f in range(4):
                        for tb in range(2):
                            b = proj_fm(ws, f, tb)
                            S.op("act", lambda e, b=b, f=f, tb=tb: e.activation(out=pair[:, f, tb * 512:(tb + 1) * 512], in_=ps[b][:], func=AF.Identity),
                                 reads=[PS[b]], writes=[R("pair", f, tb)])
                    ws = wload(win_d, 16, C_CG + blk * 512, 512)
                    for f in range(4):
                        c = blk * 4 + f
                        for tb in range(2):
                            b = proj_fm(ws, f, tb)
                            of, ofr = nxt("of")
                            S.op("dve", lambda e, of=of, b=b, f=f, tb=tb: e.tensor_tensor(of[:], ps[b][:], pair[:, f, tb * 512:(tb + 1) * 512], ALU.mult),
                                 reads=[PS[b], R("pair", f, tb)], writes=[ofr])
                            st(cv_o[c * 128:(c + 1) * 128, tb * 512:(tb + 1) * 512], of[:], ofr)
                for blk in range(2):
                    ws = wload(win_d, 16, C_Q + blk * 512, 512)
                    for f in range(4):
                        hh = blk * 4 + f
                        for tb in range(2):
                            b = proj_fm(ws, f, tb)
                            qk_epilogue(b, gq_s, R("gq"), tb, qT_o[hh * 128:(hh + 1) * 128, tb * 512:(tb + 1) * 512])
                for blk in range(2):
                    ws = wload(win_d, 16, C_ZA + blk * 512, 512)
                    for f in range(4):
                        for tb in range(2):
                            b = proj_fm(ws, f, tb)
                            S.op("act", lambda e, b=b, f=f, tb=tb: e.activation(out=pair[:, f, tb * 512:(tb + 1) * 512], in_=ps[b][:], func=AF.Silu),
                                 reads=[PS[b]], writes=[R("pair", f, tb)])
                    ws = wload(win_d, 16, C_BG + blk * 512, 512)
                    for f in range(4):
                        c = blk * 4 + f
                        for tb in range(2):
                            b = proj_fm(ws, f, tb)
                            ob, obr = nxt("ob")
                            S.op("dve", lambda e, ob=ob, b=b, f=f, tb=tb: e.tensor_tensor(ob[:], ps[b][:], pair[:, f, tb * 512:(tb + 1) * 512], ALU.mult),
                                 reads=[PS[b], R("pair", f, tb)], writes=[obr])
                            st(bz_o[c * 128:(c + 1) * 128, tb * 512:(tb + 1) * 512], ob[:], obr)
                for (col, nblk, func, dst) in ((C_ZB, 2, AF.Silu, szb_o), (C_GA, 4, AF.Sigmoid, sga_o), (C_GB, 4, AF.Sigmoid, sgb_o)):
                    for blk in range(nblk):
                        ws = wload(win_d, 16, col + blk * 512, 512)
                        for f in range(4):
                            c = blk * 4 + f
                            for tb in range(2):
                                b = proj_fm(ws, f, tb)
                                ob, obr = nxt("ob")
                                S.op("act", lambda e, ob=ob, b=b, func=func: e.activation(out=ob[:], in_=ps[b][:], func=func), reads=[PS[b]], writes=[obr])
                                st(dst[c * 128:(c + 1) * 128, tb * 512:(tb + 1) * 512], ob[:], obr)
                S.barrier()

        eps_s = sb("eps_s", [128, 1], F32)
        S.op("pool", lambda e: e.memset(eps_s[:], EPS), writes=[R("eps")])

        if has_B:
            phase_B()
        if has_A:
            phase_A()
        else:
            for kc in range(16):
                fin.append(S.dma("sp", lambda e, kc=kc: e.dma_start(out=xT_o[kc * 128:(kc + 1) * 128, :], in_=xT[:, kc, :]),
                                 reads=[R("xT", kc, 0), R("xT", kc, 1)]))
        S.emit(final_wait_recs=fin)
    return nc


_PROGS = {}


def _prog(has_B, has_A, first):
    key = (has_B, has_A, first)
    if key not in _PROGS:
        _PROGS[key] = build_program(has_B, has_A, first)
    return _PROGS[key]


def _consts():
    bf = ml_dtypes.bfloat16
    half = 32
    inv = (10000.0 ** (-np.arange(half, dtype=np.float64) / half))
    rmat = np.zeros((128, 128), np.float32)
    for m in range(128):
        d = m % 64
        if d < 32:
            rmat[m + 32, m] = -1.0
        else:
            rmat[m - 32, m] = 1.0
    blk = np.zeros((128, 128), np.float32)
    blk[:64, :64] = 1.0
    blk[64:, 64:] = 1.0
    per_core = []
    for i in range(NCORES):
        pos = np.concatenate([np.arange(128) + (8 * j + i) * 128 for j in range(8)]).astype(np.float64)
        ang = inv[np.arange(128) % 32][:, None] * pos[None, :]
        ang = ang.astype(np.float32).astype(np.float64)
        cos = np.cos(ang).astype(np.float32)
        sin = np.sin(ang).astype(np.float32)
        mask = np.zeros((128, 8, 128), np.float32)
        for ip in range(8):
            if ip > i:
                mask[:, ip, :] = NEG
            elif ip == i:
                kc = np.arange(128)[:, None] // 64
                qc = np.arange(128)[None, :] // 64
                mask[:, ip, :] = np.where(kc <= qc, 0.0, NEG)
        per_core.append({
            "cos": cos, "sin": sin, "rmat": rmat, "ones": np.ones((128, 128), np.float32), "blk64": blk,
            "ident": np.eye(128, dtype=np.float32).astype(bf), "onesb": np.ones((128, 128), np.float32).astype(bf),
            "mask": mask.astype(bf),
        })
    return per_core


def _pc(v):
    return np.ascontiguousarray(v.reshape(-1, 128).T)


def kernel(x, c, ada_w, ada_b, norm_g, w_in, conv_w, w_conv_out, q_norm_g, k_norm_g,
           lam_q1, lam_k1, lam_q2, lam_k2, subln_g, w_attn_out, w_o):
    f = lambda a: np.ascontiguousarray(np.asarray(a, dtype=np.float32))
    x, c, ada_w, ada_b, norm_g, w_in, conv_w, w_conv_out = map(f, (x, c, ada_w, ada_b, norm_g, w_in, conv_w, w_conv_out))
    q_norm_g, k_norm_g, lam_q1, lam_k1, lam_q2, lam_k2, subln_g, w_attn_out, w_o = map(
        f, (q_norm_g, k_norm_g, lam_q1, lam_k1, lam_q2, lam_k2, subln_g, w_attn_out, w_o))
    consts = _consts()
    xt = x[0].reshape(8, 8, 128, D)
    xT = [np.ascontiguousarray(xt[:, i].reshape(T, D).T) for i in range(NCORES)]
    state = None
    for k in range(DEPTH + 1):
        has_B = k > 0
        has_A = k < DEPTH
        nc = _prog(has_B, has_A, k == 0)
        in_maps = []
        for i in range(NCORES):
            m = dict(consts[i])
            m["xT_in"] = xT[i]
            if has_A:
                la = k
                m.update({
                    "c_vec": _pc(c[0]), "ada_w": ada_w[la], "ada_b": _pc(ada_b[la]), "norm_g": _pc(norm_g[la]),
                    "w_in": w_in[la],
                    "gq128": np.ascontiguousarray(np.tile(q_norm_g[la], 2)[:, None]),
                    "gk128": np.ascontiguousarray(np.tile(k_norm_g[la], 2)[:, None]),
                })
            if has_B:
                lb = k - 1
                lam_init = 0.8 - 0.6 * math.exp(-0.3 * lb)
                lconst = np.empty((128, 2), np.float32)
                lconst[:, 0] = lam_init
                lconst[:, 1] = 1.0 - lam_init
                m.update({
                    "qT_i": state[i]["qT_o"], "bz_i": state[i]["bz_o"], "szb_i": state[i]["szb_o"],
                    "sga_i": state[i]["sga_o"], "sgb_i": state[i]["sgb_o"], "gate_i": state[i]["gate_o"],
                    "cvh_i": state[i]["cvh"], "K_all": state["K_all"], "V_all": state["V_all"],
                    "conv_w": np.ascontiguousarray(conv_w[lb].reshape(3, 8, 128).transpose(2, 0, 1).reshape(128, 24)),
                    "w_conv_out": w_conv_out[lb], "w_attn_out": w_attn_out[lb], "w_o": w_o[lb],
                    "lamp": np.stack([lam_q1[lb], lam_k1[lb], lam_q2[lb], lam_k2[lb]]),
                    "lconst": lconst, "subg": np.ascontiguousarray(subln_g[lb][:, None]),
                })
            in_maps.append(m)
        res = run_bass_kernel_spmd(nc, in_maps, core_ids=list(range(NCORES)))
        outs = res.results
        xT = [np.asarray(outs[i]["xT_o"]) for i in range(NCORES)]
        if has_A:
            kk = np.stack([np.asarray(outs[i]["kT_o"]) for i in range(NCORES)])
            kk = kk.reshape(8, 8, 128, 8, 128).transpose(1, 2, 3, 0, 4)
            K_all = np.ascontiguousarray(kk.reshape(8, 128, SEQ))
            vv = np.stack([np.asarray(outs[i]["v_o"]) for i in range(NCORES)])
            vv = vv.reshape(8, 8, 128, 8, 128).transpose(3, 2, 1, 0, 4)
            V_all = np.ascontiguousarray(vv.reshape(8, 128, 64 * 128))
            cvs = [np.asarray(outs[i]["cv_o"]).reshape(1024, 8, 128) for i in range(NCORES)]
            state = {"K_all": K_all, "V_all": V_all}
            for i in range(NCORES):
                cvh = np.zeros((1024, 8, 130), np.float32)
                cvh[:, :, 2:] = cvs[i]
                for j in range(8):
                    g = 8 * j + i
                    if g > 0:
                        cvh[:, j, 0:2] = cvs[(g - 1) % 8][:, (g - 1) // 8, 126:128]
                st = {kname: np.asarray(outs[i][kname]) for kname in ("qT_o", "bz_o", "szb_o", "sga_o", "sgb_o", "gate_o")}
                st["cvh"] = cvh.reshape(1024, 8 * 130)
                state[i] = st
    out = np.empty((8, 8, 128, D), np.float32)
    for i in range(NCORES):
        out[:, i] = xT[i].T.reshape(8, 128, D)
    return out.reshape(1, SEQ, D)
```

```python
import math
import contextlib
import numpy as np
import ml_dtypes
import concourse.bass as bass
import concourse.mybir as mybir
from concourse.bass_utils import run_bass_kernel_spmd

F32 = mybir.dt.float32
BF16 = mybir.dt.bfloat16
AF = mybir.ActivationFunctionType
ALU = mybir.AluOpType
AX = mybir.AxisListType

NCORES = 8
D = 2048
SEQ = 8192
T = 1024
DEPTH = 4
DIN = 12288
EPS = 1e-6
C_U, C_BG, C_CG, C_ZA, C_Q, C_K, C_V, C_ZB, C_GA, C_GB = 0, 1024, 2048, 3072, 4096, 5120, 6144, 7168, 8192, 10240
NEG = -30000.0

ENGS = ("pe", "act", "dve", "pool", "sp")
SEM_LIMIT = 30000


class Region:
    __slots__ = ("w", "rc", "rd")

    def __init__(self):
        self.w = None
        self.rc = {}
        self.rd = []


class Rec:
    __slots__ = ("eng", "fn", "deps", "needs_inc", "is_dma", "dma_slot", "dma_val", "semref")

    def __init__(self, eng, fn, is_dma):
        self.eng = eng
        self.fn = fn
        self.deps = []
        self.needs_inc = False
        self.is_dma = is_dma
        self.dma_slot = None
        self.dma_val = None
        self.semref = None


class Sched:
    def __init__(self, nc, n_dma_sems=32, same_engine_sync=True):
        self.nc = nc
        self.q = {e: [] for e in ENGS}
        self.n_dma_sems = n_dma_sems
        self.dma_count = 0
        self.dma_slot_last = [None] * n_dma_sems
        self.same_engine_sync = same_engine_sync
        self.regions = {}
        self.pending = {}

    def R(self, *key):
        r = self.regions.get(key)
        if r is None:
            r = Region()
            self.regions[key] = r
        return r

    def barrier(self):
        deps = [self.q[e][-1] for e in ENGS if self.q[e]]
        deps += [d for d in self.dma_slot_last if d is not None]
        for e in ENGS:
            self.pending[e] = list(deps)

    def _add(self, eng, fn, reads, writes, is_dma):
        rec = Rec(eng, fn, is_dma)
        deps = []
        for r in reads:
            if r.w is not None:
                deps.append(r.w)
        for w in writes:
            if w.w is not None:
                deps.append(w.w)
            deps.extend(w.rc.values())
            deps.extend(w.rd)
        if eng in self.pending:
            deps.extend(self.pending.pop(eng))
        if is_dma:
            slot = self.dma_count % self.n_dma_sems
            rec.dma_slot = slot
            rec.dma_val = 16 * (self.dma_count // self.n_dma_sems + 1)
            prev = self.dma_slot_last[slot]
            if prev is not None:
                deps.append(prev)
            self.dma_slot_last[slot] = rec
            self.dma_count += 1
        seen = set()
        for d in deps:
            if d is rec or id(d) in seen:
                continue
            seen.add(id(d))
            if (not d.is_dma) and d.eng == eng and not is_dma:
                if eng == "pe" or not self.same_engine_sync:
                    continue
            rec.deps.append(d)
            if not d.is_dma:
                d.needs_inc = True
        self.q[eng].append(rec)
        for r in reads:
            if is_dma:
                r.rd.append(rec)
            else:
                r.rc[eng] = rec
        for w in writes:
            w.w = rec
            w.rc = {}
            w.rd = []
        return rec

    def op(self, eng, fn, reads=(), writes=()):
        return self._add(eng, fn, reads, writes, False)

    def dma(self, eng, fn, reads=(), writes=()):
        return self._add(eng, fn, reads, writes, True)

    def emit(self, final_wait_recs=()):
        nc = self.nc
        with contextlib.ExitStack() as es:
            for e in ENGS:
                cnt = 0
                sems = []
                for rec in self.q[e]:
                    if rec.is_dma or not rec.needs_inc:
                        continue
                    si = cnt // SEM_LIMIT
                    while len(sems) <= si:
                        sems.append(es.enter_context(nc.semaphore(f"s_{e}_{len(sems)}")))
                    rec.semref = (sems[si], cnt % SEM_LIMIT + 1)
                    cnt += 1
            dma_sems = [es.enter_context(nc.semaphore(f"s_dma_{i}")) for i in range(self.n_dma_sems)]
            for e in ENGS:
                for rec in self.q[e]:
                    if rec.is_dma:
                        rec.semref = (dma_sems[rec.dma_slot], rec.dma_val)
            block = es.enter_context(nc.Block())
            engmap = {"pe": block.tensor, "act": block.scalar, "dve": block.vector,
                      "pool": block.gpsimd, "sp": block.sync}

            def make(e):
                def body(engine):
                    known = {}
                    for rec in self.q[e]:
                        for d in rec.deps:
                            sem, val = d.semref
                            k = id(sem)
                            if known.get(k, 0) >= val:
                                continue
                            known[k] = val
                            engine.wait_ge(sem, val)
                        ins = rec.fn(engine)
                        if rec.is_dma:
                            ins.then_inc(rec.semref[0], 16)
                        elif rec.needs_inc:
                            ins.then_inc(rec.semref[0], 1)
                    if e == "sp":
                        for d in final_wait_recs:
                            sem, val = d.semref
                            engine.wait_ge(sem, val)
                return body

            for e in ENGS:
                engmap[e](make(e))


DEBUG = False


def build_program(has_B, has_A, first):
    nc = bass.Bass("TRN2", target_bir_lowering=False)
    S = Sched(nc)
    R = S.R

    def din(name, shape, dt=F32):
        return nc.dram_tensor(name, list(shape), dt, kind="ExternalInput").ap()

    def dout(name, shape, dt=F32):
        return nc.dram_tensor(name, list(shape), dt, kind="ExternalOutput").ap()

    cos_d = din("cos", [128, T])
    sin_d = din("sin", [128, T])
    rmat_d = din("rmat", [128, 128])
    ones_d = din("ones", [128, 128])
    blk_d = din("blk64", [128, 128])
    ident_d = din("ident", [128, 128], BF16)
    onesb_d = din("onesb", [128, 128], BF16)
    mask_d = din("mask", [128, 8, 128], BF16)
    xT_in = din("xT_in", [D, T])

    if has_A:
        c_d = din("c_vec", [128, 16])
        adaw_d = din("ada_w", [D, 3 * D])
        adab_d = din("ada_b", [128, 48])
        ng_d = din("norm_g", [128, 16])
        win_d = din("w_in", [D, DIN])
        gq_d = din("gq128", [128, 1])
        gk_d = din("gk128", [128, 1])
        kT_o = dout("kT_o", [1024, T], BF16)
        v_o = dout("v_o", [T, 1024], BF16)
        cv_o = dout("cv_o", [1024, T])
        qT_o = dout("qT_o", [1024, T], BF16)
        bz_o = dout("bz_o", [1024, T], BF16)
        szb_o = dout("szb_o", [1024, T], BF16)
        sga_o = dout("sga_o", [D, T], BF16)
        sgb_o = dout("sgb_o", [D, T], BF16)
        gate_o = dout("gate_o", [128, 16])
        if DEBUG:
            dbg_hT = dout("dbg_hT", [D, T], BF16)
            dbg_w = dout("dbg_w", [128, 16 * 512], BF16)
    if has_B:
        qT_i = din("qT_i", [1024, T], BF16)
        bz_i = din("bz_i", [1024, T], BF16)
        szb_i = din("szb_i", [1024, T], BF16)
        sga_i = din("sga_i", [D, T], BF16)
        sgb_i = din("sgb_i", [D, T], BF16)
        gate_i = din("gate_i", [128, 16])
        cvh_i = din("cvh_i", [1024, 8 * 130])
        K_i = din("K_all", [8, 128, SEQ], BF16)
        V_i = din("V_all", [8, 128, 64 * 128], BF16)
        cw_d = din("conv_w", [128, 24])
        wco_d = din("w_conv_out", [1024, D])
        wao_d = din("w_attn_out", [1024, D])
        wo_d = din("w_o", [D, D])
        lamp_d = din("lamp", [4, 64])
        lconst_d = din("lconst", [128, 2])
        subg_d = din("subg", [128, 1])
    xT_o = dout("xT_o", [D, T])

    fin = []
    with contextlib.ExitStack() as es:
        def sb(name, shape, dt, stack=es):
            return stack.enter_context(nc.sbuf_tensor(name, list(shape), dt))

        ps = [es.enter_context(nc.psum_tensor(f"ps{i}", [128, 512], F32)) for i in range(8)]
        PS = [R("ps", i) for i in range(8)]

        xT = sb("xT", [128, 16, T], F32)
        rmat_s = sb("rmat_s", [128, 128], F32)
        ones_s = sb("ones_s", [128, 128], F32)
        blk_s = sb("blk_s", [128, 128], F32)
        ident_s = sb("ident_s", [128, 128], BF16)
        onesb_s = sb("onesb_s", [128, 128], BF16)
        mask_s = sb("mask_s", [128, 8, 128], BF16)
        wbuf = [None, None]

        def ld(eng, dst, src, reg):
            return S.dma(eng, lambda e: e.dma_start(out=dst, in_=src), writes=[reg])

        ld("sp", rmat_s[:], rmat_d, R("rmat"))
        ld("sp", ones_s[:], ones_d, R("ones"))
        ld("sp", blk_s[:], blk_d, R("blk"))
        ld("sp", ident_s[:], ident_d, R("ident"))
        ld("sp", onesb_s[:], onesb_d, R("onesb"))
        ld("sp", mask_s[:], mask_d, R("mask"))
        for kc in range(16):
            S.dma("sp", lambda e, kc=kc: e.dma_start(out=xT[:, kc, :], in_=xT_in[kc * 128:(kc + 1) * 128, :]),
                  writes=[R("xT", kc, 0), R("xT", kc, 1)])

        wstate = {"n": 0}

        def wload(w_ap, nk, col0, ncols):
            s = wstate["n"] % 2
            wstate["n"] += 1
            src = w_ap[:, col0:col0 + ncols].rearrange("(kc p) n -> p kc n", p=128)
            dst = wbuf[s][:, 0:nk, 0:ncols]
            S.dma("pool", lambda e: e.dma_start(out=dst, in_=src), writes=[R("w", s)])
            return s

        def mm(out, lhsT, rhs, start, stop, reads, writes):
            S.op("pe", lambda e: e.matmul(out, lhsT, rhs, start=start, stop=stop), reads=reads, writes=writes)

        rot = {}

        def rotbuf(stack, name, n, shape, dt):
            bufs = [sb(f"{name}{i}", shape, dt, stack) for i in range(n)]
            rot[name] = [0, n, bufs]

        def nxt(name):
            st = rot[name]
            i = st[0] % st[1]
            st[0] += 1
            return st[2][i], R(name, i)

        psr = {"n": 0}

        def nps(lo=0, n=4):
            i = lo + psr["n"] % n
            psr["n"] += 1
            return i

        def phase_B():
            with contextlib.ExitStack() as bs:
                gate_s = sb("gate_s", [128, 16], F32, bs)
                cw_s = sb("cw_s", [128, 24], F32, bs)
                lamp_s = sb("lamp_s", [128, 4, 64], F32, bs)
                lconst_s = sb("lconst_s", [128, 2], F32, bs)
                subg_s = sb("subg_s", [128, 1], F32, bs)
                lam_t = sb("lam_t", [128, 2, 64], F32, bs)
                lam_r = sb("lam_r", [128, 4], F32, bs)
                neglam = sb("neglam", [128, 1], F32, bs)
                g_s = sb("g_s", [128, 8, T], BF16, bs)
                onT = sb("onT", [128, 8, T], BF16, bs)
                ld("sp", gate_s[:], gate_i, R("gate_s"))
                ld("sp", cw_s[:], cw_d, R("cw"))
                ld("sp", lconst_s[:], lconst_d, R("lconst"))
                ld("sp", subg_s[:], subg_d, R("subg"))
                lamp_b = bass.AP(lamp_d.tensor, 0, [[0, 128], [1, 256]])
                ld("sp", lamp_s[:].rearrange("p a b -> p (a b)"), lamp_b, R("lamp"))
                S.op("dve", lambda e: e.tensor_tensor(lam_t[:, 0, :], lamp_s[:, 0, :], lamp_s[:, 1, :], ALU.mult), reads=[R("lamp")], writes=[R("lamt0")])
                S.op("dve", lambda e: e.tensor_tensor(lam_t[:, 1, :], lamp_s[:, 2, :], lamp_s[:, 3, :], ALU.mult), reads=[R("lamp")], writes=[R("lamt1")])
                S.op("dve", lambda e: e.reduce_sum(lam_r[:, 0:1], lam_t[:, 0, :], axis=AX.X), reads=[R("lamt0")], writes=[R("lamr0")])
                S.op("dve", lambda e: e.reduce_sum(lam_r[:, 1:2], lam_t[:, 1, :], axis=AX.X), reads=[R("lamt1")], writes=[R("lamr1")])
                S.op("act", lambda e: e.activation(out=lam_r[:, 2:4], in_=lam_r[:, 0:2], func=AF.Exp), reads=[R("lamr0"), R("lamr1")], writes=[R("lamr2")])
                S.op("dve", lambda e: e.tensor_tensor(neglam[:], lam_r[:, 3:4], lam_r[:, 2:3], ALU.subtract), reads=[R("lamr2")], writes=[R("neglam0")])
                S.op("dve", lambda e: e.tensor_tensor(neglam[:], neglam[:], lconst_s[:, 0:1], ALU.subtract), reads=[R("neglam0"), R("lconst")], writes=[R("neglam")])
                S.op("dve", lambda e: e.tensor_tensor(subg_s[:], subg_s[:], lconst_s[:, 1:2], ALU.mult), reads=[R("subg"), R("lconst")], writes=[R("subg")])

                with contextlib.ExitStack() as cs:
                    cvh = [sb(f"cvh{i}", [128, 8, 130], F32, cs) for i in range(2)]
                    bzs = [sb(f"bzs{i}", [128, T], BF16, cs) for i in range(2)]
                    ytmp = [sb(f"ytmp{i}", [128, 8, 128], F32, cs) for i in range(2)]
                    for c in range(8):
                        b = c % 2
                        ld("sp", cvh[b][:].rearrange("p a b -> p (a b)"), cvh_i[c * 128:(c + 1) * 128, :], R("cvh", b))
                        ld("sp", bzs[b][:], bz_i[c * 128:(c + 1) * 128, :], R("bzs", b))
                        y = ytmp[b]
                        cv = cvh[b]
                        S.op("dve", lambda e, y=y, cv=cv, c=c: e.tensor_scalar(y[:], cv[:, :, 2:130], cw_s[:, 16 + c:17 + c], None, ALU.mult),
                             reads=[R("cvh", b), R("cw")], writes=[R("ytmp", b)])
                        S.op("dve", lambda e, y=y, cv=cv, c=c: e.scalar_tensor_tensor(y[:], cv[:, :, 1:129], cw_s[:, 8 + c:9 + c], y[:], ALU.mult, ALU.add),
                             reads=[R("cvh", b), R("cw"), R("ytmp", b)], writes=[R("ytmp", b)])
                        S.op("dve", lambda e, y=y, cv=cv, c=c: e.scalar_tensor_tensor(y[:], cv[:, :, 0:128], cw_s[:, c:c + 1], y[:], ALU.mult, ALU.add),
                             reads=[R("cvh", b), R("cw"), R("ytmp", b)], writes=[R("ytmp", b)])
                        bz = bzs[b]
                        S.op("dve", lambda e, y=y, bz=bz, c=c: e.tensor_tensor(g_s[:, c, :], y[:].rearrange("p a b -> p (a b)"), bz[:], ALU.mult),
                             reads=[R("ytmp", b), R("bzs", b)], writes=[R("g", c)])
                S.barrier()

                with contextlib.ExitStack() as as_:
                    Kh = [sb(f"Kh{i}", [128, 32, 128], BF16, as_) for i in range(4)]
                    Vh = [sb(f"Vh{i}", [128, 32, 128], BF16, as_) for i in range(4)]
                    qh_s = [sb(f"qh{i}", [128, T], BF16, as_) for i in range(2)]
                    zb_s = [sb(f"zbh{i}", [128, T], BF16, as_) for i in range(2)]
                    rotbuf(as_, "pt", 4, [128, 512], BF16)
                    rotbuf(as_, "ef", 12, [128, 512], F32)
                    hslot = {"n": 0}

                    def load_half(h, hf):
                        s = hslot["n"] % 4
                        hslot["n"] += 1
                        ld("sp", Kh[s][:].rearrange("p a b -> p (a b)"), K_i[h, :, hf * 4096:(hf + 1) * 4096], R("Kh", s))
                        ld("sp", Vh[s][:].rearrange("p a b -> p (a b)"), V_i[h, :, hf * 4096:(hf + 1) * 4096], R("Vh", s))
                        return s

                    def load_head(h):
                        b = h % 2
                        ld("sp", qh_s[b][:], qT_i[h * 128:(h + 1) * 128, :], R("qh", b))
                        ld("sp", zb_s[b][:], szb_i[h * 128:(h + 1) * 128, :], R("zbh", b))
                        return (load_half(h, 0), load_half(h, 1))

                    slots_next = load_head(0)
                    deferred = []

                    def rec_S(h, hb, slots, qh, kt):
                        j0 = max(kt // 8, 4 * qh)
                        c0 = j0 * 128
                        c1 = (4 * qh + 4) * 128
                        n = c1 - c0
                        s = slots[kt // 32]
                        ktl = kt % 32
                        diag = (kt // 8 == j0)
                        for comp in range(2):
                            bnk = (kt % 2) * 2 + comp
                            lo, hi = comp * 64, comp * 64 + 64
                            mm(ps[bnk][:, 0:n], Kh[s][lo:hi, ktl, :], qh_s[hb][lo:hi, c0:c1], True, not diag,
                               [R("Kh", s), R("qh", hb)], [PS[bnk]])
                            if diag:
                                mm(ps[bnk][:, 0:128], ident_s[:], mask_s[:, kt % 8, :], False, True,
                                   [R("ident"), R("mask")], [PS[bnk]])

                    def rec_PV(h, hb, slots, qh, kt, nkt):
                        j0 = max(kt // 8, 4 * qh)
                        c0 = j0 * 128
                        n = (4 * qh + 4) * 128 - c0
                        off = c0 - 4 * qh * 128
                        s = slots[kt // 32]
                        ktl = kt % 32
                        for comp in range(2):
                            bnk = (kt % 2) * 2 + comp
                            pt, ptr = nxt("pt")
                            S.op("act", lambda e, pt=pt, bnk=bnk, n=n: e.activation(out=pt[:, 0:n], in_=ps[bnk][:, 0:n], func=AF.Exp, scale=0.125),
                                 reads=[PS[bnk]], writes=[ptr])
                            mm(ps[4 + comp][:, off:off + n], Vh[s][:, ktl, :], pt[:, 0:n], kt == 0, kt == nkt - 1,
                               [R("Vh", s), ptr], [PS[4 + comp]])
                            mm(ps[6 + comp][:, off:off + n], onesb_s[:], pt[:, 0:n], kt == 0, kt == nkt - 1,
                               [R("onesb"), ptr], [PS[6 + comp]])

                    def epilogue(h, hb, qh):
                        cs0 = qh * 512
                        r0, r0r = nxt("ef")
                        r1, r1r = nxt("ef")
                        o0, o0r = nxt("ef")
                        o1, o1r = nxt("ef")
                        sq, sqr = nxt("ef")
                        rs, rsr = nxt("ef")
                        S.op("dve", lambda e: e.reciprocal(r0[:], ps[6][:]), reads=[PS[6]], writes=[r0r])
                        S.op("dve", lambda e: e.reciprocal(r1[:], ps[7][:]), reads=[PS[7]], writes=[r1r])
                        S.op("dve", lambda e: e.tensor_tensor(o0[:], ps[4][:], r0[:], ALU.mult), reads=[PS[4], r0r], writes=[o0r])
                        S.op("dve", lambda e: e.tensor_tensor(o1[:], ps[5][:], r1[:], ALU.mult), reads=[PS[5], r1r], writes=[o1r])

                        def part2():
                            S.op("dve", lambda e: e.scalar_tensor_tensor(o0[:], o1[:], neglam[:, 0:1], o0[:], ALU.mult, ALU.add),
                                 reads=[o0r, o1r, R("neglam")], writes=[o0r])
                            S.op("pool", lambda e: e.tensor_tensor(sq[:], o0[:], o0[:], ALU.mult), reads=[o0r], writes=[sqr])
                            mm(ps[0][:], ones_s[:], sq[:], True, True, [R("ones"), sqr], [PS[0]])
                            S.op("act", lambda e: e.activation(out=rs[:], in_=ps[0][:], func=AF.Sqrt, scale=1.0 / 128.0, bias=eps_s[:, 0:1]),
                                 reads=[PS[0], R("eps")], writes=[rsr])
                            S.op("dve", lambda e: e.reciprocal(rs[:], rs[:]), reads=[rsr], writes=[rsr])
                            S.op("dve", lambda e: e.tensor_tensor(o0[:], o0[:], rs[:], ALU.mult), reads=[o0r, rsr], writes=[o0r])
                            S.op("dve", lambda e: e.scalar_tensor_tensor(
                                onT[:, h, cs0:cs0 + 512], o0[:], subg_s[:, 0:1], zb_s[hb][:, cs0:cs0 + 512], ALU.mult, ALU.mult),
                                reads=[o0r, R("subg"), R("zbh", hb)], writes=[R("onT", h, qh)])
                        deferred.append(part2)

                    for h in range(8):
                        slots = slots_next
                        hb = h % 2
                        for qh in range(2):
                            nkt = 32 * qh + 32
                            rec_S(h, hb, slots, qh, 0)
                            for kt in range(nkt):
                                if kt + 1 < nkt:
                                    rec_S(h, hb, slots, qh, kt + 1)
                                rec_PV(h, hb, slots, qh, kt, nkt)
                                if kt == 0 and deferred:
                                    deferred.pop(0)()
                            epilogue(h, hb, qh)
                            if qh == 0 and h + 1 < 8:
                                slots_next = load_head(h + 1)
                    while deferred:
                        deferred.pop(0)()
                S.barrier()

                with contextlib.ExitStack() as ms:
                    merged = sb("merged", [128, 16, T], BF16, ms)
                    wbuf[0] = sb("wbufB0", [128, 16, 512], BF16, ms)
                    wbuf[1] = sb("wbufB1", [128, 16, 512], BF16, ms)
                    sga_s = [sb(f"sga{i}", [128, T], BF16, ms) for i in range(2)]
                    sgb_s = [sb(f"sgb{i}", [128, T], BF16, ms) for i in range(2)]
                    rotbuf(ms, "mf", 4, [128, 512], F32)
                    for blk in range(4):
                        sc = wload(wco_d, 8, blk * 512, 512)
                        sa = wload(wao_d, 8, blk * 512, 512)
                        for f in range(4):
                            dc = blk * 4 + f
                            gb_ = dc % 2
                            ld("sp", sga_s[gb_][:], sga_i[dc * 128:(dc + 1) * 128, :], R("sga", gb_))
                            ld("sp", sgb_s[gb_][:], sgb_i[dc * 128:(dc + 1) * 128, :], R("sgb", gb_))
                            for tb in range(2):
                                tsl = slice(tb * 512, tb * 512 + 512)
                                pc = nps()
                                for c in range(8):
                                    mm(ps[pc][:], wbuf[sc][:, c, f * 128:(f + 1) * 128], g_s[:, c, tsl], c == 0, c == 7,
                                       [R("w", sc), R("g", c)], [PS[pc]])
                                pa = nps()
                                for hh in range(8):
                                    mm(ps[pa][:], wbuf[sa][:, hh, f * 128:(f + 1) * 128], onT[:, hh, tsl], hh == 0, hh == 7,
                                       [R("w", sa), R("onT", hh, tb)], [PS[pa]])
                                t1, t1r = nxt("mf")
                                t2, t2r = nxt("mf")
                                S.op("dve", lambda e, t1=t1, pc=pc, gb_=gb_, tsl=tsl: e.tensor_tensor(t1[:], ps[pc][:], sga_s[gb_][:, tsl], ALU.mult),
                                     reads=[PS[pc], R("sga", gb_)], writes=[t1r])
                                S.op("dve", lambda e, t2=t2, pa=pa, gb_=gb_, tsl=tsl: e.tensor_tensor(t2[:], ps[pa][:], sgb_s[gb_][:, tsl], ALU.mult),
                                     reads=[PS[pa], R("sgb", gb_)], writes=[t2r])
                                S.op("pool", lambda e, t1=t1, t2=t2, dc=dc, tsl=tsl: e.tensor_tensor(merged[:, dc, tsl], t1[:], t2[:], ALU.add),
                                     reads=[t1r, t2r], writes=[R("merged", dc, tb)])
                    for blk in range(4):
                        so = wload(wo_d, 16, blk * 512, 512)
                        for f in range(4):
                            dc = blk * 4 + f
                            for tb in range(2):
                                tsl = slice(tb * 512, tb * 512 + 512)
                                po = nps()
                                for kc in range(16):
                                    mm(ps[po][:], wbuf[so][:, kc, f * 128:(f + 1) * 128], merged[:, kc, tsl], kc == 0, kc == 15,
                                       [R("w", so), R("merged", kc, tb)], [PS[po]])
                                S.op("dve", lambda e, po=po, dc=dc, tsl=tsl: e.scalar_tensor_tensor(
                                    xT[:, dc, tsl], ps[po][:], gate_s[:, dc:dc + 1], xT[:, dc, tsl], ALU.mult, ALU.add),
                                    reads=[PS[po], R("gate_s"), R("xT", dc, tb)], writes=[R("xT", dc, tb)])
                S.barrier()

        def phase_A():
            with contextlib.ExitStack() as a_:
                hT = sb("hT", [128, 16, T], BF16, a_)
                c_s = sb("c_s", [128, 16], F32, a_)
                cact = sb("cact", [128, 16], F32, a_)
                adab_s = sb("adab_s", [128, 48], F32, a_)
                ng_s = sb("ng_s", [128, 16], F32, a_)
                mod_s = sb("mod_s", [128, 48], F32, a_)
                asc = sb("asc", [128, 16], F32, a_)
                gq_s = sb("gq_s", [128, 1], F32, a_)
                gk_s = sb("gk_s", [128, 1], F32, a_)
                cos_s = sb("cos_s", [128, T], F32, a_)
                sin_s = sb("sin_s", [128, T], F32, a_)
                ld("sp", cos_s[:], cos_d, R("cos"))
                ld("sp", sin_s[:], sin_d, R("sin"))
                rotbuf(a_, "af", 12, [128, 512], F32)
                rsn = [sb(f"rsn{i}", [128, 512], F32, a_) for i in range(2)]
                rotbuf(a_, "ob", 4, [128, 512], BF16)
                rotbuf(a_, "of", 2, [128, 512], F32)
                m_ = contextlib.ExitStack()
                adaw = [sb(f"adaw{i}", [128, 16, 256], F32, m_) for i in range(2)]
                rotbuf(m_, "macc_dve", 2, [128, 256], F32)
                rotbuf(m_, "macc_pool", 2, [128, 256], F32)
                ld("sp", c_s[:], c_d, R("c_s"))
                ld("sp", adab_s[:], adab_d, R("adab"))
                ld("sp", ng_s[:], ng_d, R("ng"))
                ld("sp", gq_s[:], gq_d, R("gq"))
                ld("sp", gk_s[:], gk_d, R("gk"))
                for kc in range(16):
                    fin.append(S.dma("sp", lambda e, kc=kc: e.dma_start(out=xT_o[kc * 128:(kc + 1) * 128, :], in_=xT[:, kc, :]),
                                     reads=[R("xT", kc, 0), R("xT", kc, 1)]))
                S.op("act", lambda e: e.activation(out=cact[:], in_=c_s[:], func=AF.Silu), reads=[R("c_s")], writes=[R("cact")])
                for blk in range(24):
                    b = blk % 2
                    src = adaw_d[:, blk * 256:(blk + 1) * 256].rearrange("(kc p) n -> p kc n", p=128)
                    S.dma("sp", lambda e, b=b, src=src: e.dma_start(out=adaw[b][:], in_=src), writes=[R("adaw", b)])
                    eng = "dve"
                    acc, accr = nxt("macc_" + eng)
                    S.op(eng, lambda e, acc=acc, b=b: e.tensor_scalar(acc[:], adaw[b][:, 0, :], cact[:, 0:1], 0.0, ALU.mult, ALU.add),
                         reads=[R("adaw", b), R("cact")], writes=[accr])
                    for kc in range(1, 16):
                        S.op(eng, lambda e, acc=acc, b=b, kc=kc: e.scalar_tensor_tensor(
                            acc[:], adaw[b][:, kc, :], cact[:, kc:kc + 1], acc[:], ALU.mult, ALU.add),
                            reads=[R("adaw", b), R("cact"), accr], writes=[accr])
                    for f in range(2):
                        cc = blk * 2 + f
                        mm(ps[7][:, cc:cc + 1], acc[:, f * 128:(f + 1) * 128], ones_s[:, 0:1], True, True,
                           [accr, R("ones")], [PS[7]])
                S.op("dve", lambda e: e.tensor_tensor(mod_s[:], ps[7][:, 0:48], adab_s[:], ALU.add), reads=[PS[7], R("adab")], writes=[R("mod")])
                S.op("dve", lambda e: e.tensor_scalar(asc[:], mod_s[:, 16:32], 1.0, None, ALU.add), reads=[R("mod")], writes=[R("asc0")])
                S.op("dve", lambda e: e.tensor_tensor(asc[:], asc[:], ng_s[:], ALU.mult), reads=[R("asc0"), R("ng")], writes=[R("asc")])
                fin.append(S.dma("sp", lambda e: e.dma_start(out=gate_o, in_=mod_s[:, 32:48]), reads=[R("mod")]))
                S.barrier()
                m_.close()
                pair = sb("pair", [128, 4, T], F32, a_)
                wbuf[0] = sb("wbufA0", [128, 16, 512], BF16, a_)
                wbuf[1] = sb("wbufA1", [128, 16, 512], BF16, a_)
                for tb in range(2):
                    tsl = slice(tb * 512, tb * 512 + 512)
                    for kc in range(16):
                        sq, sqr = nxt("af")
                        S.op("pool", lambda e, sq=sq, kc=kc, tsl=tsl: e.tensor_tensor(sq[:], xT[:, kc, tsl], xT[:, kc, tsl], ALU.mult),
                             reads=[R("xT", kc, tb)], writes=[sqr])
                        mm(ps[4][:], ones_s[:], sq[:], kc == 0, kc == 15, [R("ones"), sqr], [PS[4]])
                    rs, rsr = rsn[tb], R("rsn", tb)
                    S.op("act", lambda e, rs=rs: e.activation(out=rs[:], in_=ps[4][:], func=AF.Sqrt, scale=1.0 / D, bias=eps_s[:, 0:1]),
                         reads=[PS[4], R("eps")], writes=[rsr])
                    S.op("dve", lambda e, rs=rs: e.reciprocal(rs[:], rs[:]), reads=[rsr], writes=[rsr])
                    for kc in range(16):
                        tmp, tmr = nxt("af")
                        S.op("dve", lambda e, tmp=tmp, rs=rs, kc=kc, tsl=tsl: e.scalar_tensor_tensor(
                            tmp[:], xT[:, kc, tsl], asc[:, kc:kc + 1], rs[:], ALU.mult, ALU.mult),
                            reads=[R("xT", kc, tb), R("asc"), rsr], writes=[tmr])
                        S.op("act", lambda e, tmp=tmp, kc=kc, tsl=tsl: e.activation(
                            out=hT[:, kc, tsl], in_=tmp[:], func=AF.Identity, bias=mod_s[:, kc:kc + 1], scale=1.0),
                            reads=[tmr, R("mod")], writes=[R("hT", kc, tb)])

                if DEBUG:
                    for kc in range(16):
                        fin.append(S.dma("sp", lambda e, kc=kc: e.dma_start(out=dbg_hT[kc * 128:(kc + 1) * 128, :], in_=hT[:, kc, :]),
                                         reads=[R("hT", kc, 0), R("hT", kc, 1)]))

                def hreads(tb):
                    return [R("hT", kc, tb) for kc in range(16)]

                def proj_fm(ws, f, tb):
                    b = nps()
                    tsl = slice(tb * 512, tb * 512 + 512)
                    for kc in range(16):
                        mm(ps[b][:], wbuf[ws][:, kc, f * 128:(f + 1) * 128], hT[:, kc, tsl], kc == 0, kc == 15,
                           [R("w", ws), R("hT", kc, tb)], [PS[b]])
                    return b

                def st(dst, src_buf, reg):
                    fin.append(S.dma("sp", lambda e: e.dma_start(out=dst, in_=src_buf), reads=[reg]))

                pend = []

                def flush():
                    while pend:
                        pend.pop(0)()

                def qk_epilogue(b, g_s_, g_r, tb, dst):
                    tsl = slice(tb * 512, tb * 512 + 512)
                    raw, rawr = nxt("af")
                    kg, kgr = nxt("af")
                    sq, sqr = nxt("af")
                    rs, rsr = nxt("af")
                    t1, t1r = nxt("af")
                    t2, t2r = nxt("af")
                    S.op("act", lambda e: e.activation(out=raw[:], in_=ps[b][:], func=AF.Identity), reads=[PS[b]], writes=[rawr])
                    S.op("dve", lambda e: e.tensor_scalar(kg[:], raw[:], g_s_[:, 0:1], None, ALU.mult), reads=[rawr, g_r], writes=[kgr])
                    S.op("pool", lambda e: e.tensor_tensor(sq[:], raw[:], raw[:], ALU.mult), reads=[rawr], writes=[sqr])

                    def part2():
                        mm(ps[6][:], blk_s[:], sq[:], True, True, [R("blk"), sqr], [PS[6]])
                        mm(ps[5][:], rmat_s[:], kg[:], True, True, [R("rmat"), kgr], [PS[5]])
                        S.op("act", lambda e: e.activation(out=rs[:], in_=ps[6][:], func=AF.Sqrt, scale=1.0 / 64.0, bias=eps_s[:, 0:1]),
                             reads=[PS[6], R("eps")], writes=[rsr])
                        S.op("dve", lambda e: e.reciprocal(rs[:], rs[:]), reads=[rsr], writes=[rsr])
                        S.op("dve", lambda e: e.tensor_tensor(t1[:], kg[:], cos_s[:, tsl], ALU.mult), reads=[kgr, R("cos")], writes=[t1r])
                        S.op("dve", lambda e: e.tensor_tensor(t2[:], ps[5][:], sin_s[:, tsl], ALU.mult), reads=[PS[5], R("sin")], writes=[t2r])
                        S.op("pool", lambda e: e.tensor_tensor(t1[:], t1[:], t2[:], ALU.add), reads=[t1r, t2r], writes=[t1r])
                        ob, obr = nxt("ob")
                        S.op("dve", lambda e: e.tensor_tensor(ob[:], t1[:], rs[:], ALU.mult), reads=[t1r, rsr], writes=[obr])
                        st(dst, ob[:], obr)
                    pend.append(part2)

                for blk in range(2):
                    ws = wload(win_d, 16, C_K + blk * 512, 512)
                    if DEBUG and blk == 0:
                        fin.append(S.dma("sp", lambda e, ws=ws: e.dma_start(out=dbg_w, in_=wbuf[ws][:].rearrange("p a b -> p (a b)")), reads=[R("w", ws)]))
                    for f in range(4):
                        hh = blk * 4 + f
                        for tb in range(2):
                            b = proj_fm(ws, f, tb)
                            flush()
                            qk_epilogue(b, gk_s, R("gk"), tb, kT_o[hh * 128:(hh + 1) * 128, tb * 512:(tb + 1) * 512])
                for blk in range(2):
                    ws = wload(win_d, 16, C_V + blk * 512, 512)
                    for tt in range(8):
                        b = nps()
                        for kc in range(16):
                            mm(ps[b][:], hT[:, kc, tt * 128:(tt + 1) * 128], wbuf[ws][:, kc, :], kc == 0, kc == 15,
                               [R("w", ws), R("hT", kc, tt // 4)], [PS[b]])
                        flush()
                        ob, obr = nxt("ob")
                        S.op("act", lambda e, ob=ob, b=b: e.activation(out=ob[:], in_=ps[b][:], func=AF.Identity), reads=[PS[b]], writes=[obr])
                        st(v_o[tt * 128:(tt + 1) * 128, blk * 512:(blk + 1) * 512], ob[:], obr)
                for blk in range(2):
                    ws = wload(win_d, 16, C_U + blk * 512, 512)
                    for f in range(4):
                        for tb in range(2):
                            b = proj_fm(ws, f, tb)
                            S.op("act", lambda e, b=b, f=f, tb=tb: e.activation(out=pair[:, f, tb * 512:(tb + 1) * 512], in_=ps[b][:], func=AF.Identity),
                                 reads=[PS[b]], writes=[R("pair", f, tb)])
                    ws = wload(win_d, 16, C_CG + blk * 512, 512)
                    for f in range(4):
                        c = blk * 4 + f
                        for tb in range(2):
                            b = proj_fm(ws, f, tb)
                            of, ofr = nxt("of")
                            S.op("dve", lambda e, of=of, b=b, f=f, tb=tb: e.tensor_tensor(of[:], ps[b][:], pair[:, f, tb * 512:(tb + 1) * 512], ALU.mult),
                                 reads=[PS[b], R("pair", f, tb)], writes=[ofr])
                            st(cv_o[c * 128:(c + 1) * 128, tb * 512:(tb + 1) * 512], of[:], ofr)
                for blk in range(2):
                    ws = wload(win_d, 16, C_Q + blk * 512, 512)
                    for f in range(4):
                        hh = blk * 4 + f
                        for tb in range(2):
                            b = proj_fm(ws, f, tb)
                            flush()
                            qk_epilogue(b, gq_s, R("gq"), tb, qT_o[hh * 128:(hh + 1) * 128, tb * 512:(tb + 1) * 512])
                flush()
                for blk in range(2):
                    ws = wload(win_d, 16, C_ZA + blk * 512, 512)
                    for f in range(4):
                        for tb in range(2):
                            b = proj_fm(ws, f, tb)
                            S.op("act", lambda e, b=b, f=f, tb=tb: e.activation(out=pair[:, f, tb * 512:(tb + 1) * 512], in_=ps[b][:], func=AF.Silu),
                                 reads=[PS[b]], writes=[R("pair", f, tb)])
                    ws = wload(win_d, 16, C_BG + blk * 512, 512)
                    for f in range(4):
                        c = blk * 4 + f
                        for tb in range(2):
                            b = proj_fm(ws, f, tb)
                            ob, obr = nxt("ob")
                            S.op("dve", lambda e, ob=ob, b=b, f=f, tb=tb: e.tensor_tensor(ob[:], ps[b][:], pair[:, f, tb * 512:(tb + 1) * 512], ALU.mult),
                                 reads=[PS[b], R("pair", f, tb)], writes=[obr])
                            st(bz_o[c * 128:(c + 1) * 128, tb * 512:(tb + 1) * 512], ob[:], obr)
                for (col, nblk, func, dst) in ((C_ZB, 2, AF.Silu, szb_o), (C_GA, 4, AF.Sigmoid, sga_o), (C_GB, 4, AF.Sigmoid, sgb_o)):
                    for blk in range(nblk):
                        ws = wload(win_d, 16, col + blk * 512, 512)
                        for f in range(4):
                            c = blk * 4 + f
                            for tb in range(2):
                                b = proj_fm(ws, f, tb)
                                ob, obr = nxt("ob")
                                S.op("act", lambda e, ob=ob, b=b, func=func: e.activation(out=ob[:], in_=ps[b][:], func=func), reads=[PS[b]], writes=[obr])
                                st(dst[c * 128:(c + 1) * 128, tb * 512:(tb + 1) * 512], ob[:], obr)
                S.barrier()

        eps_s = sb("eps_s", [128, 1], F32)
        S.op("pool", lambda e: e.memset(eps_s[:], EPS), writes=[R("eps")])

        if has_B:
            phase_B()
        if has_A:
            phase_A()
        else:
            for kc in range(16):
                fin.append(S.dma("sp", lambda e, kc=kc: e.dma_start(out=xT_o[kc * 128:(kc + 1) * 128, :], in_=xT[:, kc, :]),
                                 reads=[R("xT", kc, 0), R("xT", kc, 1)]))
        S.emit(final_wait_recs=fin)
    return nc


_PROGS = {}


def _prog(has_B, has_A, first):
    key = (has_B, has_A, first)
    if key not in _PROGS:
        _PROGS[key] = build_program(has_B, has_A, first)
    return _PROGS[key]


def _consts():
    bf = ml_dtypes.bfloat16
    half = 32
    inv = (10000.0 ** (-np.arange(half, dtype=np.float64) / half))
    rmat = np.zeros((128, 128), np.float32)
    for m in range(128):
        d = m % 64
        if d < 32:
            rmat[m + 32, m] = -1.0
        else:
            rmat[m - 32, m] = 1.0
    blk = np.zeros((128, 128), np.float32)
    blk[:64, :64] = 1.0
    blk[64:, 64:] = 1.0
    per_core = []
    for i in range(NCORES):
        pos = np.concatenate([np.arange(128) + (8 * j + i) * 128 for j in range(8)]).astype(np.float64)
        ang = inv[np.arange(128) % 32][:, None] * pos[None, :]
        ang = ang.astype(np.float32).astype(np.float64)
        cos = np.cos(ang).astype(np.float32)
        sin = np.sin(ang).astype(np.float32)
        mask = np.zeros((128, 8, 128), np.float32)
        for ip in range(8):
            if ip > i:
                mask[:, ip, :] = NEG
            elif ip == i:
                kc = np.arange(128)[:, None] // 64
                qc = np.arange(128)[None, :] // 64
                mask[:, ip, :] = np.where(kc <= qc, 0.0, NEG)
        per_core.append({
            "cos": cos, "sin": sin, "rmat": rmat, "ones": np.ones((128, 128), np.float32), "blk64": blk,
            "ident": np.eye(128, dtype=np.float32).astype(bf), "onesb": np.ones((128, 128), np.float32).astype(bf),
            "mask": mask.astype(bf),
        })
    return per_core


def _pc(v):
    return np.ascontiguousarray(v.reshape(-1, 128).T)


def kernel(x, c, ada_w, ada_b, norm_g, w_in, conv_w, w_conv_out, q_norm_g, k_norm_g,
           lam_q1, lam_k1, lam_q2, lam_k2, subln_g, w_attn_out, w_o):
    f = lambda a: np.ascontiguousarray(np.asarray(a, dtype=np.float32))
    x, c, ada_w, ada_b, norm_g, w_in, conv_w, w_conv_out = map(f, (x, c, ada_w, ada_b, norm_g, w_in, conv_w, w_conv_out))
    q_norm_g, k_norm_g, lam_q1, lam_k1, lam_q2, lam_k2, subln_g, w_attn_out, w_o = map(
        f, (q_norm_g, k_norm_g, lam_q1, lam_k1, lam_q2, lam_k2, subln_g, w_attn_out, w_o))
    consts = _consts()
    xt = x[0].reshape(8, 8, 128, D)
    xT = [np.ascontiguousarray(xt[:, i].reshape(T, D).T) for i in range(NCORES)]
    state = None
    for k in range(DEPTH + 1):
        has_B = k > 0
        has_A = k < DEPTH
        nc = _prog(has_B, has_A, k == 0)
        in_maps = []
        for i in range(NCORES):
            m = dict(consts[i])
            m["xT_in"] = xT[i]
            if has_A:
                la = k
                m.update({
                    "c_vec": _pc(c[0]), "ada_w": ada_w[la], "ada_b": _pc(ada_b[la]), "norm_g": _pc(norm_g[la]),
                    "w_in": w_in[la],
                    "gq128": np.ascontiguousarray(np.tile(q_norm_g[la], 2)[:, None]),
                    "gk128": np.ascontiguousarray(np.tile(k_norm_g[la], 2)[:, None]),
                })
            if has_B:
                lb = k - 1
                lam_init = 0.8 - 0.6 * math.exp(-0.3 * lb)
                lconst = np.empty((128, 2), np.float32)
                lconst[:, 0] = lam_init
                lconst[:, 1] = 1.0 - lam_init
                m.update({
                    "qT_i": state[i]["qT_o"], "bz_i": state[i]["bz_o"], "szb_i": state[i]["szb_o"],
                    "sga_i": state[i]["sga_o"], "sgb_i": state[i]["sgb_o"], "gate_i": state[i]["gate_o"],
                    "cvh_i": state[i]["cvh"], "K_all": state["K_all"], "V_all": state["V_all"],
                    "conv_w": np.ascontiguousarray(conv_w[lb].reshape(3, 8, 128).transpose(2, 0, 1).reshape(128, 24)),
                    "w_conv_out": w_conv_out[lb], "w_attn_out": w_attn_out[lb], "w_o": w_o[lb],
                    "lamp": np.stack([lam_q1[lb], lam_k1[lb], lam_q2[lb], lam_k2[lb]]),
                    "lconst": lconst, "subg": np.ascontiguousarray(subln_g[lb][:, None]),
                })
            in_maps.append(m)
        res = run_bass_kernel_spmd(nc, in_maps, core_ids=list(range(NCORES)))
        outs = res.results
        xT = [np.asarray(outs[i]["xT_o"]) for i in range(NCORES)]
        if has_A:
            kk = np.stack([np.asarray(outs[i]["kT_o"]) for i in range(NCORES)])
            kk = kk.reshape(8, 8, 128, 8, 128).transpose(1, 2, 3, 0, 4)
            K_all = np.ascontiguousarray(kk.reshape(8, 128, SEQ))
            vv = np.stack([np.asarray(outs[i]["v_o"]) for i in range(NCORES)])
            vv = vv.reshape(8, 8, 128, 8, 128).transpose(3, 2, 1, 0, 4)
            V_all = np.ascontiguousarray(vv.reshape(8, 128, 64 * 128))
            cvs = [np.asarray(outs[i]["cv_o"]).reshape(1024, 8, 128) for i in range(NCORES)]
            state = {"K_all": K_all, "V_all": V_all}
            for i in range(NCORES):
                cvh = np.zeros((1024, 8, 130), np.float32)
                cvh[:, :, 2:] = cvs[i]
                for j in range(8):
                    g = 8 * j + i
                    if g > 0:
                        cvh[:, j, 0:2] = cvs[(g - 1) % 8][:, (g - 1) // 8, 126:128]
                st = {kname: np.asarray(outs[i][kname]) for kname in ("qT_o", "bz_o", "szb_o", "sga_o", "sgb_o", "gate_o")}
                st["cvh"] = cvh.reshape(1024, 8 * 130)
                state[i] = st
    out = np.empty((8, 8, 128, D), np.float32)
    for i in range(NCORES):
        out[:, i] = xT[i].T.reshape(8, 128, D)
    return out.reshape(1, SEQ, D)
```

```python
import math
import contextlib
import numpy as np
import ml_dtypes
import concourse.bass as bass
import concourse.mybir as mybir
from concourse.bass_utils import run_bass_kernel_spmd

F32 = mybir.dt.float32
BF16 = mybir.dt.bfloat16
AF = mybir.ActivationFunctionType
ALU = mybir.AluOpType
AX = mybir.AxisListType

NCORES = 8
D = 2048
SEQ = 8192
T = 1024
DEPTH = 4
DIN = 12288
EPS = 1e-6
C_U, C_BG, C_CG, C_ZA, C_Q, C_K, C_V, C_ZB, C_GA, C_GB = 0, 1024, 2048, 3072, 4096, 5120, 6144, 7168, 8192, 10240
NEG = -30000.0

ENGS = ("pe", "act", "dve", "pool", "sp")
SEM_LIMIT = 30000


class Region:
    __slots__ = ("w", "rc", "rd")

    def __init__(self):
        self.w = None
        self.rc = {}
        self.rd = []


class Rec:
    __slots__ = ("eng", "fn", "deps", "needs_inc", "is_dma", "dma_slot", "dma_val", "semref")

    def __init__(self, eng, fn, is_dma):
        self.eng = eng
        self.fn = fn
        self.deps = []
        self.needs_inc = False
        self.is_dma = is_dma
        self.dma_slot = None
        self.dma_val = None
        self.semref = None


class Sched:
    def __init__(self, nc, n_dma_sems=32, same_engine_sync=True):
        self.nc = nc
        self.q = {e: [] for e in ENGS}
        self.n_dma_sems = n_dma_sems
        self.n_sw = 8
        self.n_hw = n_dma_sems - self.n_sw
        self.dma_count = 0
        self.dma_cnt = {"sw": 0, "hw": 0}
        self.dma_slot_last = [None] * n_dma_sems
        self.same_engine_sync = same_engine_sync
        self.regions = {}
        self.pending = {}

    def R(self, *key):
        r = self.regions.get(key)
        if r is None:
            r = Region()
            self.regions[key] = r
        return r

    def barrier(self):
        deps = [self.q[e][-1] for e in ENGS if self.q[e]]
        deps += [d for d in self.dma_slot_last if d is not None]
        for e in ENGS:
            self.pending[e] = list(deps)

    def _add(self, eng, fn, reads, writes, is_dma):
        rec = Rec(eng, fn, is_dma)
        deps = []
        for r in reads:
            if r.w is not None:
                deps.append(r.w)
        for w in writes:
            if w.w is not None:
                deps.append(w.w)
            deps.extend(w.rc.values())
            deps.extend(w.rd)
        if eng in self.pending:
            deps.extend(self.pending.pop(eng))
        if is_dma:
            if eng == "pool":
                c = self.dma_cnt["sw"]
                self.dma_cnt["sw"] += 1
                slot = c % self.n_sw
                rec.dma_val = 16 * (c // self.n_sw + 1)
            else:
                c = self.dma_cnt["hw"]
                self.dma_cnt["hw"] += 1
                slot = self.n_sw + c % self.n_hw
                rec.dma_val = 16 * (c // self.n_hw + 1)
            rec.dma_slot = slot
            prev = self.dma_slot_last[slot]
            if prev is not None:
                deps.append(prev)
            self.dma_slot_last[slot] = rec
            self.dma_count += 1
        seen = set()
        for d in deps:
            if d is rec or id(d) in seen:
                continue
            seen.add(id(d))
            if (not d.is_dma) and d.eng == eng and not is_dma:
                if eng == "pe" or not self.same_engine_sync:
                    continue
            rec.deps.append(d)
            if not d.is_dma:
                d.needs_inc = True
        self.q[eng].append(rec)
        for r in reads:
            if is_dma:
                r.rd.append(rec)
            else:
                r.rc[eng] = rec
        for w in writes:
            w.w = rec
            w.rc = {}
            w.rd = []
        return rec

    def op(self, eng, fn, reads=(), writes=()):
        return self._add(eng, fn, reads, writes, False)

    def dma(self, eng, fn, reads=(), writes=()):
        return self._add(eng, fn, reads, writes, True)

    def emit(self, final_wait_recs=()):
        nc = self.nc
        with contextlib.ExitStack() as es:
            for e in ENGS:
                cnt = 0
                sems = []
                for rec in self.q[e]:
                    if rec.is_dma or not rec.needs_inc:
                        continue
                    si = cnt // SEM_LIMIT
                    while len(sems) <= si:
                        sems.append(es.enter_context(nc.semaphore(f"s_{e}_{len(sems)}")))
                    rec.semref = (sems[si], cnt % SEM_LIMIT + 1)
                    cnt += 1
            dma_sems = [es.enter_context(nc.semaphore(f"s_dma_{i}")) for i in range(self.n_dma_sems)]
            for e in ENGS:
                for rec in self.q[e]:
                    if rec.is_dma:
                        rec.semref = (dma_sems[rec.dma_slot], rec.dma_val)
            block = es.enter_context(nc.Block())
            engmap = {"pe": block.tensor, "act": block.scalar, "dve": block.vector,
                      "pool": block.gpsimd, "sp": block.sync}

            def make(e):
                def body(engine):
                    known = {}
                    for rec in self.q[e]:
                        for d in rec.deps:
                            sem, val = d.semref
                            k = id(sem)
                            if known.get(k, 0) >= val:
                                continue
                            known[k] = val
                            engine.wait_ge(sem, val)
                        ins = rec.fn(engine)
                        if rec.is_dma:
                            ins.then_inc(rec.semref[0], 16)
                        elif rec.needs_inc:
                            ins.then_inc(rec.semref[0], 1)
                    if e == "sp":
                        for d in final_wait_recs:
                            sem, val = d.semref
                            engine.wait_ge(sem, val)
                return body

            for e in ENGS:
                engmap[e](make(e))


DEBUG = False


def build_program(has_B, has_A, first):
    nc = bass.Bass("TRN2", target_bir_lowering=False)
    S = Sched(nc)
    R = S.R

    def din(name, shape, dt=F32):
        return nc.dram_tensor(name, list(shape), dt, kind="ExternalInput").ap()

    def dout(name, shape, dt=F32):
        return nc.dram_tensor(name, list(shape), dt, kind="ExternalOutput").ap()

    cos_d = din("cos", [128, T])
    sin_d = din("sin", [128, T])
    rmat_d = din("rmat", [128, 128])
    ones_d = din("ones", [128, 128])
    blk_d = din("blk64", [128, 128])
    ident_d = din("ident", [128, 128], BF16)
    onesb_d = din("onesb", [128, 128], BF16)
    mask_d = din("mask", [128, 8, 128], BF16)
    xT_in = din("xT_in", [D, T])

    if has_A:
        c_d = din("c_vec", [128, 16])
        adaw_d = din("ada_w", [D, 3 * D])
        adab_d = din("ada_b", [128, 48])
        ng_d = din("norm_g", [128, 16])
        win_d = din("w_in", [D, DIN])
        gq_d = din("gq128", [128, 1])
        gk_d = din("gk128", [128, 1])
        kT_o = dout("kT_o", [1024, T], BF16)
        v_o = dout("v_o", [T, 1024], BF16)
        cv_o = dout("cv_o", [1024, T])
        qT_o = dout("qT_o", [1024, T], BF16)
        bz_o = dout("bz_o", [1024, T], BF16)
        szb_o = dout("szb_o", [1024, T], BF16)
        sga_o = dout("sga_o", [D, T], BF16)
        sgb_o = dout("sgb_o", [D, T], BF16)
        gate_o = dout("gate_o", [128, 16])
        if DEBUG:
            dbg_hT = dout("dbg_hT", [D, T], BF16)
            dbg_w = dout("dbg_w", [128, 16 * 512], BF16)
    if has_B:
        qT_i = din("qT_i", [1024, T], BF16)
        bz_i = din("bz_i", [1024, T], BF16)
        szb_i = din("szb_i", [1024, T], BF16)
        sga_i = din("sga_i", [D, T], BF16)
        sgb_i = din("sgb_i", [D, T], BF16)
        gate_i = din("gate_i", [128, 16])
        cvh_i = din("cvh_i", [1024, 8 * 130])
        K_i = din("K_all", [8, 128, SEQ], BF16)
        V_i = din("V_all", [8, 128, 64 * 128], BF16)
        cw_d = din("conv_w", [128, 24])
        wco_d = din("w_conv_out", [1024, D])
        wao_d = din("w_attn_out", [1024, D])
        wo_d = din("w_o", [D, D])
        lamp_d = din("lamp", [4, 64])
        lconst_d = din("lconst", [128, 2])
        subg_d = din("subg", [128, 1])
    xT_o = dout("xT_o", [D, T])

    fin = []
    with contextlib.ExitStack() as es:
        def sb(name, shape, dt, stack=es):
            return stack.enter_context(nc.sbuf_tensor(name, list(shape), dt))

        ps = [es.enter_context(nc.psum_tensor(f"ps{i}", [128, 512], F32)) for i in range(8)]
        PS = [R("ps", i) for i in range(8)]

        xT = sb("xT", [128, 16, T], F32)
        rmat_s = sb("rmat_s", [128, 128], F32)
        ones_s = sb("ones_s", [128, 128], F32)
        blk_s = sb("blk_s", [128, 128], F32)
        ident_s = sb("ident_s", [128, 128], BF16)
        onesb_s = sb("onesb_s", [128, 128], BF16)
        mask_s = sb("mask_s", [128, 8, 128], BF16)
        wbuf = [None, None]

        def ld(eng, dst, src, reg):
            return S.dma(eng, lambda e: e.dma_start(out=dst, in_=src), writes=[reg])

        ld("sp", rmat_s[:], rmat_d, R("rmat"))
        ld("sp", ones_s[:], ones_d, R("ones"))
        ld("sp", blk_s[:], blk_d, R("blk"))
        ld("sp", ident_s[:], ident_d, R("ident"))
        ld("sp", onesb_s[:], onesb_d, R("onesb"))
        ld("sp", mask_s[:], mask_d, R("mask"))
        for kc in range(16):
            S.dma("sp", lambda e, kc=kc: e.dma_start(out=xT[:, kc, :], in_=xT_in[kc * 128:(kc + 1) * 128, :]),
                  writes=[R("xT", kc, 0), R("xT", kc, 1)])

        class WStream:
            def __init__(self, blocks):
                self.blocks = blocks
                self.i = 0
                self._issue(0)

            def _issue(self, j):
                if j >= len(self.blocks):
                    return
                s = j % 2
                for (w_ap, nk, col0, ncols, kc0) in self.blocks[j]:
                    src = w_ap[:, col0:col0 + ncols].rearrange("(kc p) n -> p kc n", p=128)
                    dst = wbuf[s][:, kc0:kc0 + nk, 0:ncols]
                    regs = [R("w", s, 0), R("w", s, 1)] if nk == 16 else [R("w", s, kc0 // 8)]
                    S.dma("pool", lambda e, dst=dst, src=src: e.dma_start(out=dst, in_=src), writes=regs)

            def get(self):
                j = self.i
                self.i += 1
                self._issue(j + 1)
                return j % 2

        def WR(s):
            return [R("w", s, 0), R("w", s, 1)]

        def mm(out, lhsT, rhs, start, stop, reads, writes):
            S.op("pe", lambda e: e.matmul(out, lhsT, rhs, start=start, stop=stop), reads=reads, writes=writes)

        rot = {}

        def rotbuf(stack, name, n, shape, dt):
            bufs = [sb(f"{name}{i}", shape, dt, stack) for i in range(n)]
            rot[name] = [0, n, bufs]

        def nxt(name):
            st = rot[name]
            i = st[0] % st[1]
            st[0] += 1
            return st[2][i], R(name, i)

        psr = {"n": 0}

        def nps(lo=0, n=4):
            i = lo + psr["n"] % n
            psr["n"] += 1
            return i

        def phase_B():
            with contextlib.ExitStack() as bs:
                gate_s = sb("gate_s", [128, 16], F32, bs)
                cw_s = sb("cw_s", [128, 24], F32, bs)
                lamp_s = sb("lamp_s", [128, 4, 64], F32, bs)
                lconst_s = sb("lconst_s", [128, 2], F32, bs)
                subg_s = sb("subg_s", [128, 1], F32, bs)
                lam_t = sb("lam_t", [128, 2, 64], F32, bs)
                lam_r = sb("lam_r", [128, 4], F32, bs)
                neglam = sb("neglam", [128, 1], F32, bs)
                g_s = sb("g_s", [128, 8, T], BF16, bs)
                onT = sb("onT", [128, 8, T], BF16, bs)
                ld("sp", gate_s[:], gate_i, R("gate_s"))
                ld("sp", cw_s[:], cw_d, R("cw"))
                ld("sp", lconst_s[:], lconst_d, R("lconst"))
                ld("sp", subg_s[:], subg_d, R("subg"))
                lamp_b = bass.AP(lamp_d.tensor, 0, [[0, 128], [1, 256]])
                ld("sp", lamp_s[:].rearrange("p a b -> p (a b)"), lamp_b, R("lamp"))
                S.op("dve", lambda e: e.tensor_tensor(lam_t[:, 0, :], lamp_s[:, 0, :], lamp_s[:, 1, :], ALU.mult), reads=[R("lamp")], writes=[R("lamt0")])
                S.op("dve", lambda e: e.tensor_tensor(lam_t[:, 1, :], lamp_s[:, 2, :], lamp_s[:, 3, :], ALU.mult), reads=[R("lamp")], writes=[R("lamt1")])
                S.op("dve", lambda e: e.reduce_sum(lam_r[:, 0:1], lam_t[:, 0, :], axis=AX.X), reads=[R("lamt0")], writes=[R("lamr0")])
                S.op("dve", lambda e: e.reduce_sum(lam_r[:, 1:2], lam_t[:, 1, :], axis=AX.X), reads=[R("lamt1")], writes=[R("lamr1")])
                S.op("act", lambda e: e.activation(out=lam_r[:, 2:4], in_=lam_r[:, 0:2], func=AF.Exp), reads=[R("lamr0"), R("lamr1")], writes=[R("lamr2")])
                S.op("dve", lambda e: e.tensor_tensor(neglam[:], lam_r[:, 3:4], lam_r[:, 2:3], ALU.subtract), reads=[R("lamr2")], writes=[R("neglam0")])
                S.op("dve", lambda e: e.tensor_tensor(neglam[:], neglam[:], lconst_s[:, 0:1], ALU.subtract), reads=[R("neglam0"), R("lconst")], writes=[R("neglam")])
                S.op("dve", lambda e: e.tensor_tensor(subg_s[:], subg_s[:], lconst_s[:, 1:2], ALU.mult), reads=[R("subg"), R("lconst")], writes=[R("subg")])

                with contextlib.ExitStack() as cs:
                    cvh = [sb(f"cvh{i}", [128, 8, 130], F32, cs) for i in range(2)]
                    bzs = [sb(f"bzs{i}", [128, T], BF16, cs) for i in range(2)]
                    ytmp = [sb(f"ytmp{i}", [128, 8, 128], F32, cs) for i in range(2)]
                    for c in range(8):
                        b = c % 2
                        ld("sp", cvh[b][:].rearrange("p a b -> p (a b)"), cvh_i[c * 128:(c + 1) * 128, :], R("cvh", b))
                        ld("sp", bzs[b][:], bz_i[c * 128:(c + 1) * 128, :], R("bzs", b))
                        y = ytmp[b]
                        cv = cvh[b]
                        S.op("dve", lambda e, y=y, cv=cv, c=c: e.tensor_scalar(y[:], cv[:, :, 2:130], cw_s[:, 16 + c:17 + c], None, ALU.mult),
                             reads=[R("cvh", b), R("cw")], writes=[R("ytmp", b)])
                        S.op("dve", lambda e, y=y, cv=cv, c=c: e.scalar_tensor_tensor(y[:], cv[:, :, 1:129], cw_s[:, 8 + c:9 + c], y[:], ALU.mult, ALU.add),
                             reads=[R("cvh", b), R("cw"), R("ytmp", b)], writes=[R("ytmp", b)])
                        S.op("dve", lambda e, y=y, cv=cv, c=c: e.scalar_tensor_tensor(y[:], cv[:, :, 0:128], cw_s[:, c:c + 1], y[:], ALU.mult, ALU.add),
                             reads=[R("cvh", b), R("cw"), R("ytmp", b)], writes=[R("ytmp", b)])
                        bz = bzs[b]
                        S.op("dve", lambda e, y=y, bz=bz, c=c: e.tensor_tensor(g_s[:, c, :], y[:].rearrange("p a b -> p (a b)"), bz[:], ALU.mult),
                             reads=[R("ytmp", b), R("bzs", b)], writes=[R("g", c)])
                S.barrier()

                with contextlib.ExitStack() as as_:
                    Kh = [sb(f"Kh{i}", [128, 32, 128], BF16, as_) for i in range(4)]
                    Vh = [sb(f"Vh{i}", [128, 32, 128], BF16, as_) for i in range(4)]
                    qh_s = [sb(f"qh{i}", [128, T], BF16, as_) for i in range(2)]
                    zb_s = [sb(f"zbh{i}", [128, T], BF16, as_) for i in range(2)]
                    rotbuf(as_, "pt", 4, [128, 512], BF16)
                    rotbuf(as_, "ef", 12, [128, 512], F32)
                    hslot = {"n": 0}

                    def load_half(h, hf):
                        s = hslot["n"] % 4
                        hslot["n"] += 1
                        ld("sp", Kh[s][:].rearrange("p a b -> p (a b)"), K_i[h, :, hf * 4096:(hf + 1) * 4096], R("Kh", s))
                        ld("sp", Vh[s][:].rearrange("p a b -> p (a b)"), V_i[h, :, hf * 4096:(hf + 1) * 4096], R("Vh", s))
                        return s

                    def load_head(h):
                        b = h % 2
                        ld("sp", qh_s[b][:], qT_i[h * 128:(h + 1) * 128, :], R("qh", b))
                        ld("sp", zb_s[b][:], szb_i[h * 128:(h + 1) * 128, :], R("zbh", b))
                        return (load_half(h, 0), load_half(h, 1))

                    slots_next = load_head(0)
                    deferred = []

                    def rec_S(h, hb, slots, qh, kt):
                        j0 = max(kt // 8, 4 * qh)
                        c0 = j0 * 128
                        c1 = (4 * qh + 4) * 128
                        n = c1 - c0
                        s = slots[kt // 32]
                        ktl = kt % 32
                        diag = (kt // 8 == j0)
                        for comp in range(2):
                            bnk = (kt % 2) * 2 + comp
                            lo, hi = comp * 64, comp * 64 + 64
                            mm(ps[bnk][:, 0:n], Kh[s][lo:hi, ktl, :], qh_s[hb][lo:hi, c0:c1], True, not diag,
                               [R("Kh", s), R("qh", hb)], [PS[bnk]])
                            if diag:
                                mm(ps[bnk][:, 0:128], ident_s[:], mask_s[:, kt % 8, :], False, True,
                                   [R("ident"), R("mask")], [PS[bnk]])

                    def rec_PV(h, hb, slots, qh, kt, nkt):
                        j0 = max(kt // 8, 4 * qh)
                        c0 = j0 * 128
                        n = (4 * qh + 4) * 128 - c0
                        off = c0 - 4 * qh * 128
                        s = slots[kt // 32]
                        ktl = kt % 32
                        for comp in range(2):
                            bnk = (kt % 2) * 2 + comp
                            pt, ptr = nxt("pt")
                            S.op("act", lambda e, pt=pt, bnk=bnk, n=n: e.activation(out=pt[:, 0:n], in_=ps[bnk][:, 0:n], func=AF.Exp, scale=0.125),
                                 reads=[PS[bnk]], writes=[ptr])
                            mm(ps[4 + comp][:, off:off + n], Vh[s][:, ktl, :], pt[:, 0:n], kt == 0, kt == nkt - 1,
                               [R("Vh", s), ptr], [PS[4 + comp]])
                            mm(ps[6 + comp][:, off:off + n], onesb_s[:], pt[:, 0:n], kt == 0, kt == nkt - 1,
                               [R("onesb"), ptr], [PS[6 + comp]])

                    def epilogue(h, hb, qh):
                        cs0 = qh * 512
                        r0, r0r = nxt("ef")
                        r1, r1r = nxt("ef")
                        o0, o0r = nxt("ef")
                        o1, o1r = nxt("ef")
                        sq, sqr = nxt("ef")
                        rs, rsr = nxt("ef")
                        S.op("dve", lambda e: e.reciprocal(r0[:], ps[6][:]), reads=[PS[6]], writes=[r0r])
                        S.op("dve", lambda e: e.reciprocal(r1[:], ps[7][:]), reads=[PS[7]], writes=[r1r])
                        S.op("dve", lambda e: e.tensor_tensor(o0[:], ps[4][:], r0[:], ALU.mult), reads=[PS[4], r0r], writes=[o0r])
                        S.op("dve", lambda e: e.tensor_tensor(o1[:], ps[5][:], r1[:], ALU.mult), reads=[PS[5], r1r], writes=[o1r])

                        def part2():
                            S.op("dve", lambda e: e.scalar_tensor_tensor(o0[:], o1[:], neglam[:, 0:1], o0[:], ALU.mult, ALU.add),
                                 reads=[o0r, o1r, R("neglam")], writes=[o0r])
                            S.op("pool", lambda e: e.tensor_tensor(sq[:], o0[:], o0[:], ALU.mult), reads=[o0r], writes=[sqr])
                            mm(ps[0][:], ones_s[:], sq[:], True, True, [R("ones"), sqr], [PS[0]])
                            S.op("act", lambda e: e.activation(out=rs[:], in_=ps[0][:], func=AF.Sqrt, scale=1.0 / 128.0, bias=eps_s[:, 0:1]),
                                 reads=[PS[0], R("eps")], writes=[rsr])
                            S.op("dve", lambda e: e.reciprocal(rs[:], rs[:]), reads=[rsr], writes=[rsr])
                            S.op("dve", lambda e: e.tensor_tensor(o0[:], o0[:], rs[:], ALU.mult), reads=[o0r, rsr], writes=[o0r])
                            S.op("dve", lambda e: e.scalar_tensor_tensor(
                                onT[:, h, cs0:cs0 + 512], o0[:], subg_s[:, 0:1], zb_s[hb][:, cs0:cs0 + 512], ALU.mult, ALU.mult),
                                reads=[o0r, R("subg"), R("zbh", hb)], writes=[R("onT", h, qh)])
                        deferred.append(part2)

                    for h in range(8):
                        slots = slots_next
                        hb = h % 2
                        for qh in range(2):
                            nkt = 32 * qh + 32
                            rec_S(h, hb, slots, qh, 0)
                            for kt in range(nkt):
                                if kt + 1 < nkt:
                                    rec_S(h, hb, slots, qh, kt + 1)
                                rec_PV(h, hb, slots, qh, kt, nkt)
                                if kt == 0 and deferred:
                                    deferred.pop(0)()
                            epilogue(h, hb, qh)
                            if qh == 0 and h + 1 < 8:
                                slots_next = load_head(h + 1)
                    while deferred:
                        deferred.pop(0)()
                S.barrier()

                with contextlib.ExitStack() as ms:
                    merged = sb("merged", [128, 16, T], BF16, ms)
                    wbuf[0] = sb("wbufB0", [128, 16, 512], BF16, ms)
                    wbuf[1] = sb("wbufB1", [128, 16, 512], BF16, ms)
                    sga_s = [sb(f"sga{i}", [128, T], BF16, ms) for i in range(2)]
                    sgb_s = [sb(f"sgb{i}", [128, T], BF16, ms) for i in range(2)]
                    rotbuf(ms, "mf", 4, [128, 512], F32)
                    wsB = WStream([[(wco_d, 8, blk * 512, 512, 0), (wao_d, 8, blk * 512, 512, 8)] for blk in range(4)]
                                  + [[(wo_d, 16, blk * 512, 512, 0)] for blk in range(4)])
                    for blk in range(4):
                        sc = wsB.get()
                        sa = sc
                        for f in range(4):
                            dc = blk * 4 + f
                            gb_ = dc % 2
                            ld("sp", sga_s[gb_][:], sga_i[dc * 128:(dc + 1) * 128, :], R("sga", gb_))
                            ld("sp", sgb_s[gb_][:], sgb_i[dc * 128:(dc + 1) * 128, :], R("sgb", gb_))
                            for tb in range(2):
                                tsl = slice(tb * 512, tb * 512 + 512)
                                pc = nps()
                                for c in range(8):
                                    mm(ps[pc][:], wbuf[sc][:, c, f * 128:(f + 1) * 128], g_s[:, c, tsl], c == 0, c == 7,
                                       [R("w", sc, 0), R("g", c)], [PS[pc]])
                                pa = nps()
                                for hh in range(8):
                                    mm(ps[pa][:], wbuf[sa][:, 8 + hh, f * 128:(f + 1) * 128], onT[:, hh, tsl], hh == 0, hh == 7,
                                       [R("w", sa, 1), R("onT", hh, tb)], [PS[pa]])
                                t1, t1r = nxt("mf")
                                t2, t2r = nxt("mf")
                                S.op("dve", lambda e, t1=t1, pc=pc, gb_=gb_, tsl=tsl: e.tensor_tensor(t1[:], ps[pc][:], sga_s[gb_][:, tsl], ALU.mult),
                                     reads=[PS[pc], R("sga", gb_)], writes=[t1r])
                                S.op("dve", lambda e, t2=t2, pa=pa, gb_=gb_, tsl=tsl: e.tensor_tensor(t2[:], ps[pa][:], sgb_s[gb_][:, tsl], ALU.mult),
                                     reads=[PS[pa], R("sgb", gb_)], writes=[t2r])
                                S.op("pool", lambda e, t1=t1, t2=t2, dc=dc, tsl=tsl: e.tensor_tensor(merged[:, dc, tsl], t1[:], t2[:], ALU.add),
                                     reads=[t1r, t2r], writes=[R("merged", dc, tb)])
                    for blk in range(4):
                        so = wsB.get()
                        for f in range(4):
                            dc = blk * 4 + f
                            for tb in range(2):
                                tsl = slice(tb * 512, tb * 512 + 512)
                                po = nps()
                                for kc in range(16):
                                    mm(ps[po][:], wbuf[so][:, kc, f * 128:(f + 1) * 128], merged[:, kc, tsl], kc == 0, kc == 15,
                                       WR(so) + [R("merged", kc, tb)], [PS[po]])
                                S.op("dve", lambda e, po=po, dc=dc, tsl=tsl: e.scalar_tensor_tensor(
                                    xT[:, dc, tsl], ps[po][:], gate_s[:, dc:dc + 1], xT[:, dc, tsl], ALU.mult, ALU.add),
                                    reads=[PS[po], R("gate_s"), R("xT", dc, tb)], writes=[R("xT", dc, tb)])
                S.barrier()

        def phase_A():
            with contextlib.ExitStack() as a_:
                hT = sb("hT", [128, 16, T], BF16, a_)
                c_s = sb("c_s", [128, 16], F32, a_)
                cact = sb("cact", [128, 16], F32, a_)
                adab_s = sb("adab_s", [128, 48], F32, a_)
                ng_s = sb("ng_s", [128, 16], F32, a_)
                mod_s = sb("mod_s", [128, 48], F32, a_)
                asc = sb("asc", [128, 16], F32, a_)
                gq_s = sb("gq_s", [128, 1], F32, a_)
                gk_s = sb("gk_s", [128, 1], F32, a_)
                cos_s = sb("cos_s", [128, T], F32, a_)
                sin_s = sb("sin_s", [128, T], F32, a_)
                ld("sp", cos_s[:], cos_d, R("cos"))
                ld("sp", sin_s[:], sin_d, R("sin"))
                rotbuf(a_, "af", 12, [128, 512], F32)
                rsn = [sb(f"rsn{i}", [128, 512], F32, a_) for i in range(2)]
                rotbuf(a_, "ob", 4, [128, 512], BF16)
                rotbuf(a_, "of", 2, [128, 512], F32)
                m_ = contextlib.ExitStack()
                adaw = [sb(f"adaw{i}", [128, 16, 256], F32, m_) for i in range(2)]
                rotbuf(m_, "macc_dve", 2, [128, 256], F32)
                rotbuf(m_, "macc_pool", 2, [128, 256], F32)
                ld("sp", c_s[:], c_d, R("c_s"))
                ld("sp", adab_s[:], adab_d, R("adab"))
                ld("sp", ng_s[:], ng_d, R("ng"))
                ld("sp", gq_s[:], gq_d, R("gq"))
                ld("sp", gk_s[:], gk_d, R("gk"))
                for kc in range(16):
                    fin.append(S.dma("sp", lambda e, kc=kc: e.dma_start(out=xT_o[kc * 128:(kc + 1) * 128, :], in_=xT[:, kc, :]),
                                     reads=[R("xT", kc, 0), R("xT", kc, 1)]))
                S.op("act", lambda e: e.activation(out=cact[:], in_=c_s[:], func=AF.Silu), reads=[R("c_s")], writes=[R("cact")])
                for blk in range(24):
                    b = blk % 2
                    src = adaw_d[:, blk * 256:(blk + 1) * 256].rearrange("(kc p) n -> p kc n", p=128)
                    S.dma("sp", lambda e, b=b, src=src: e.dma_start(out=adaw[b][:], in_=src), writes=[R("adaw", b)])
                    eng = "dve"
                    acc, accr = nxt("macc_" + eng)
                    S.op(eng, lambda e, acc=acc, b=b: e.tensor_scalar(acc[:], adaw[b][:, 0, :], cact[:, 0:1], 0.0, ALU.mult, ALU.add),
                         reads=[R("adaw", b), R("cact")], writes=[accr])
                    for kc in range(1, 16):
                        S.op(eng, lambda e, acc=acc, b=b, kc=kc: e.scalar_tensor_tensor(
                            acc[:], adaw[b][:, kc, :], cact[:, kc:kc + 1], acc[:], ALU.mult, ALU.add),
                            reads=[R("adaw", b), R("cact"), accr], writes=[accr])
                    for f in range(2):
                        cc = blk * 2 + f
                        mm(ps[7][:, cc:cc + 1], acc[:, f * 128:(f + 1) * 128], ones_s[:, 0:1], True, True,
                           [accr, R("ones")], [PS[7]])
                S.op("dve", lambda e: e.tensor_tensor(mod_s[:], ps[7][:, 0:48], adab_s[:], ALU.add), reads=[PS[7], R("adab")], writes=[R("mod")])
                S.op("dve", lambda e: e.tensor_scalar(asc[:], mod_s[:, 16:32], 1.0, None, ALU.add), reads=[R("mod")], writes=[R("asc0")])
                S.op("dve", lambda e: e.tensor_tensor(asc[:], asc[:], ng_s[:], ALU.mult), reads=[R("asc0"), R("ng")], writes=[R("asc")])
                fin.append(S.dma("sp", lambda e: e.dma_start(out=gate_o, in_=mod_s[:, 32:48]), reads=[R("mod")]))
                S.barrier()
                m_.close()
                pair = sb("pair", [128, 4, T], F32, a_)
                wbuf[0] = sb("wbufA0", [128, 16, 512], BF16, a_)
                wbuf[1] = sb("wbufA1", [128, 16, 512], BF16, a_)
                for tb in range(2):
                    tsl = slice(tb * 512, tb * 512 + 512)
                    for kc in range(16):
                        sq, sqr = nxt("af")
                        S.op("pool", lambda e, sq=sq, kc=kc, tsl=tsl: e.tensor_tensor(sq[:], xT[:, kc, tsl], xT[:, kc, tsl], ALU.mult),
                             reads=[R("xT", kc, tb)], writes=[sqr])
                        mm(ps[4][:], ones_s[:], sq[:], kc == 0, kc == 15, [R("ones"), sqr], [PS[4]])
                    rs, rsr = rsn[tb], R("rsn", tb)
                    S.op("act", lambda e, rs=rs: e.activation(out=rs[:], in_=ps[4][:], func=AF.Sqrt, scale=1.0 / D, bias=eps_s[:, 0:1]),
                         reads=[PS[4], R("eps")], writes=[rsr])
                    S.op("dve", lambda e, rs=rs: e.reciprocal(rs[:], rs[:]), reads=[rsr], writes=[rsr])
                    for kc in range(16):
                        tmp, tmr = nxt("af")
                        S.op("dve", lambda e, tmp=tmp, rs=rs, kc=kc, tsl=tsl: e.scalar_tensor_tensor(
                            tmp[:], xT[:, kc, tsl], asc[:, kc:kc + 1], rs[:], ALU.mult, ALU.mult),
                            reads=[R("xT", kc, tb), R("asc"), rsr], writes=[tmr])
                        S.op("act", lambda e, tmp=tmp, kc=kc, tsl=tsl: e.activation(
                            out=hT[:, kc, tsl], in_=tmp[:], func=AF.Identity, bias=mod_s[:, kc:kc + 1], scale=1.0),
                            reads=[tmr, R("mod")], writes=[R("hT", kc, tb)])

                if DEBUG:
                    for kc in range(16):
                        fin.append(S.dma("sp", lambda e, kc=kc: e.dma_start(out=dbg_hT[kc * 128:(kc + 1) * 128, :], in_=hT[:, kc, :]),
                                         reads=[R("hT", kc, 0), R("hT", kc, 1)]))

                blocksA = []
                for col in (C_K, C_V):
                    blocksA += [[(win_d, 16, col + blk * 512, 512, 0)] for blk in range(2)]
                for blk in range(2):
                    blocksA += [[(win_d, 16, C_U + blk * 512, 512, 0)], [(win_d, 16, C_CG + blk * 512, 512, 0)]]
                blocksA += [[(win_d, 16, C_Q + blk * 512, 512, 0)] for blk in range(2)]
                for blk in range(2):
                    blocksA += [[(win_d, 16, C_ZA + blk * 512, 512, 0)], [(win_d, 16, C_BG + blk * 512, 512, 0)]]
                blocksA += [[(win_d, 16, C_ZB + blk * 512, 512, 0)] for blk in range(2)]
                blocksA += [[(win_d, 16, C_GA + blk * 512, 512, 0)] for blk in range(4)]
                blocksA += [[(win_d, 16, C_GB + blk * 512, 512, 0)] for blk in range(4)]
                wsA = WStream(blocksA)

                def hreads(tb):
                    return [R("hT", kc, tb) for kc in range(16)]

                def proj_fm(ws, f, tb):
                    b = nps()
                    tsl = slice(tb * 512, tb * 512 + 512)
                    for kc in range(16):
                        mm(ps[b][:], wbuf[ws][:, kc, f * 128:(f + 1) * 128], hT[:, kc, tsl], kc == 0, kc == 15,
                           WR(ws) + [R("hT", kc, tb)], [PS[b]])
                    return b

                def st(dst, src_buf, reg):
                    fin.append(S.dma("sp", lambda e: e.dma_start(out=dst, in_=src_buf), reads=[reg]))

                pend = []

                def flush():
                    while pend:
                        pend.pop(0)()

                def qk_epilogue(b, g_s_, g_r, tb, dst):
                    tsl = slice(tb * 512, tb * 512 + 512)
                    raw, rawr = nxt("af")
                    kg, kgr = nxt("af")
                    sq, sqr = nxt("af")
                    rs, rsr = nxt("af")
                    t1, t1r = nxt("af")
                    t2, t2r = nxt("af")
                    S.op("act", lambda e: e.activation(out=raw[:], in_=ps[b][:], func=AF.Identity), reads=[PS[b]], writes=[rawr])
                    S.op("dve", lambda e: e.tensor_scalar(kg[:], raw[:], g_s_[:, 0:1], None, ALU.mult), reads=[rawr, g_r], writes=[kgr])
                    S.op("pool", lambda e: e.tensor_tensor(sq[:], raw[:], raw[:], ALU.mult), reads=[rawr], writes=[sqr])

                    def part2():
                        mm(ps[6][:], blk_s[:], sq[:], True, True, [R("blk"), sqr], [PS[6]])
                        mm(ps[5][:], rmat_s[:], kg[:], True, True, [R("rmat"), kgr], [PS[5]])
                        S.op("act", lambda e: e.activation(out=rs[:], in_=ps[6][:], func=AF.Sqrt, scale=1.0 / 64.0, bias=eps_s[:, 0:1]),
                             reads=[PS[6], R("eps")], writes=[rsr])
                        S.op("dve", lambda e: e.reciprocal(rs[:], rs[:]), reads=[rsr], writes=[rsr])
                        S.op("dve", lambda e: e.tensor_tensor(t1[:], kg[:], cos_s[:, tsl], ALU.mult), reads=[kgr, R("cos")], writes=[t1r])
                        S.op("dve", lambda e: e.tensor_tensor(t2[:], ps[5][:], sin_s[:, tsl], ALU.mult), reads=[PS[5], R("sin")], writes=[t2r])
                        S.op("pool", lambda e: e.tensor_tensor(t1[:], t1[:], t2[:], ALU.add), reads=[t1r, t2r], writes=[t1r])
                        ob, obr = nxt("ob")
                        S.op("dve", lambda e: e.tensor_tensor(ob[:], t1[:], rs[:], ALU.mult), reads=[t1r, rsr], writes=[obr])
                        st(dst, ob[:], obr)
                    pend.append(part2)

                for blk in range(2):
                    ws = wsA.get()
                    if DEBUG and blk == 0:
                        fin.append(S.dma("sp", lambda e, ws=ws: e.dma_start(out=dbg_w, in_=wbuf[ws][:].rearrange("p a b -> p (a b)")), reads=WR(ws)))
                    for f in range(4):
                        hh = blk * 4 + f
                        for tb in range(2):
                            b = proj_fm(ws, f, tb)
                            flush()
                            qk_epilogue(b, gk_s, R("gk"), tb, kT_o[hh * 128:(hh + 1) * 128, tb * 512:(tb + 1) * 512])
                for blk in range(2):
                    ws = wsA.get()
                    for tt in range(8):
                        b = nps()
                        for kc in range(16):
                            mm(ps[b][:], hT[:, kc, tt * 128:(tt + 1) * 128], wbuf[ws][:, kc, :], kc == 0, kc == 15,
                               WR(ws) + [R("hT", kc, tt // 4)], [PS[b]])
                        flush()
                        ob, obr = nxt("ob")
                        S.op("act", lambda e, ob=ob, b=b: e.activation(out=ob[:], in_=ps[b][:], func=AF.Identity), reads=[PS[b]], writes=[obr])
                        st(v_o[tt * 128:(tt + 1) * 128, blk * 512:(blk + 1) * 512], ob[:], obr)
                for blk in range(2):
                    ws = wsA.get()
                    for f in range(4):
                        for tb in range(2):
                            b = proj_fm(ws, f, tb)
                            S.op("act", lambda e, b=b, f=f, tb=tb: e.activation(out=pair[:, f, tb * 512:(tb + 1) * 512], in_=ps[b][:], func=AF.Identity),
                                 reads=[PS[b]], writes=[R("pair", f, tb)])
                    ws = wsA.get()
                    for f in range(4):
                        c = blk * 4 + f
                        for tb in range(2):
                            b = proj_fm(ws, f, tb)
                            of, ofr = nxt("of")
                            S.op("dve", lambda e, of=of, b=b, f=f, tb=tb: e.tensor_tensor(of[:], ps[b][:], pair[:, f, tb * 512:(tb + 1) * 512], ALU.mult),
                                 reads=[PS[b], R("pair", f, tb)], writes=[ofr])
                            st(cv_o[c * 128:(c + 1) * 128, tb * 512:(tb + 1) * 512], of[:], ofr)
                for blk in range(2):
                    ws = wsA.get()
                    for f in range(4):
                        hh = blk * 4 + f
                        for tb in range(2):
                            b = proj_fm(ws, f, tb)
                            flush()
                            qk_epilogue(b, gq_s, R("gq"), tb, qT_o[hh * 128:(hh + 1) * 128, tb * 512:(tb + 1) * 512])
                flush()
                for blk in range(2):
                    ws = wsA.get()
                    for f in range(4):
                        for tb in range(2):
                            b = proj_fm(ws, f, tb)
                            S.op("act", lambda e, b=b, f=f, tb=tb: e.activation(out=pair[:, f, tb * 512:(tb + 1) * 512], in_=ps[b][:], func=AF.Silu),
                                 reads=[PS[b]], writes=[R("pair", f, tb)])
                    ws = wsA.get()
                    for f in range(4):
                        c = blk * 4 + f
                        for tb in range(2):
                            b = proj_fm(ws, f, tb)
                            ob, obr = nxt("ob")
                            S.op("dve", lambda e, ob=ob, b=b, f=f, tb=tb: e.tensor_tensor(ob[:], ps[b][:], pair[:, f, tb * 512:(tb + 1) * 512], ALU.mult),
                                 reads=[PS[b], R("pair", f, tb)], writes=[obr])
                            st(bz_o[c * 128:(c + 1) * 128, tb * 512:(tb + 1) * 512], ob[:], obr)
                for (col, nblk, func, dst) in ((C_ZB, 2, AF.Silu, szb_o), (C_GA, 4, AF.Sigmoid, sga_o), (C_GB, 4, AF.Sigmoid, sgb_o)):
                    for blk in range(nblk):
                        ws = wsA.get()
                        for f in range(4):
                            c = blk * 4 + f
                            for tb in range(2):
                                b = proj_fm(ws, f, tb)
                                ob, obr = nxt("ob")
                                S.op("act", lambda e, ob=ob, b=b, func=func: e.activation(out=ob[:], in_=ps[b][:], func=func), reads=[PS[b]], writes=[obr])
                                st(dst[c * 128:(c + 1) * 128, tb * 512:(tb + 1) * 512], ob[:], obr)
                S.barrier()

        eps_s = sb("eps_s", [128, 1], F32)
        S.op("pool", lambda e: e.memset(eps_s[:], EPS), writes=[R("eps")])

        if has_B:
            phase_B()
        if has_A:
            phase_A()
        else:
            for kc in range(16):
                fin.append(S.dma("sp", lambda e, kc=kc: e.dma_start(out=xT_o[kc * 128:(kc + 1) * 128, :], in_=xT[:, kc, :]),
                                 reads=[R("xT", kc, 0), R("xT", kc, 1)]))
        S.emit(final_wait_recs=fin)
    return nc


_PROGS = {}


def _prog(has_B, has_A, first):
    key = (has_B, has_A, first)
    if key not in _PROGS:
        _PROGS[key] = build_program(has_B, has_A, first)
    return _PROGS[key]


def _consts():
    bf = ml_dtypes.bfloat16
    half = 32
    inv = (10000.0 ** (-np.arange(half, dtype=np.float64) / half))
    rmat = np.zeros((128, 128), np.float32)
    for m in range(128):
        d = m % 64
        if d < 32:
            rmat[m + 32, m] = -1.0
        else:
            rmat[m - 32, m] = 1.0
    blk = np.zeros((128, 128), np.float32)
    blk[:64, :64] = 1.0
    blk[64:, 64:] = 1.0
    per_core = []
    for i in range(NCORES):
        pos = np.concatenate([np.arange(128) + (8 * j + i) * 128 for j in range(8)]).astype(np.float64)
        ang = inv[np.arange(128) % 32][:, None] * pos[None, :]
        ang = ang.astype(np.float32).astype(np.float64)
        cos = np.cos(ang).astype(np.float32)
        sin = np.sin(ang).astype(np.float32)
        mask = np.zeros((128, 8, 128), np.float32)
        for ip in range(8):
            if ip > i:
                mask[:, ip, :] = NEG
            elif ip == i:
                kc = np.arange(128)[:, None] // 64
                qc = np.arange(128)[None, :] // 64
                mask[:, ip, :] = np.where(kc <= qc, 0.0, NEG)
        per_core.append({
            "cos": cos, "sin": sin, "rmat": rmat, "ones": np.ones((128, 128), np.float32), "blk64": blk,
            "ident": np.eye(128, dtype=np.float32).astype(bf), "onesb": np.ones((128, 128), np.float32).astype(bf),
            "mask": mask.astype(bf),
        })
    return per_core


def _pc(v):
    return np.ascontiguousarray(v.reshape(-1, 128).T)


def kernel(x, c, ada_w, ada_b, norm_g, w_in, conv_w, w_conv_out, q_norm_g, k_norm_g,
           lam_q1, lam_k1, lam_q2, lam_k2, subln_g, w_attn_out, w_o):
    f = lambda a: np.ascontiguousarray(np.asarray(a, dtype=np.float32))
    x, c, ada_w, ada_b, norm_g, w_in, conv_w, w_conv_out = map(f, (x, c, ada_w, ada_b, norm_g, w_in, conv_w, w_conv_out))
    q_norm_g, k_norm_g, lam_q1, lam_k1, lam_q2, lam_k2, subln_g, w_attn_out, w_o = map(
        f, (q_norm_g, k_norm_g, lam_q1, lam_k1, lam_q2, lam_k2, subln_g, w_attn_out, w_o))
    consts = _consts()
    xt = x[0].reshape(8, 8, 128, D)
    xT = [np.ascontiguousarray(xt[:, i].reshape(T, D).T) for i in range(NCORES)]
    state = None
    for k in range(DEPTH + 1):
        has_B = k > 0
        has_A = k < DEPTH
        nc = _prog(has_B, has_A, k == 0)
        in_maps = []
        for i in range(NCORES):
            m = dict(consts[i])
            m["xT_in"] = xT[i]
            if has_A:
                la = k
                m.update({
                    "c_vec": _pc(c[0]), "ada_w": ada_w[la], "ada_b": _pc(ada_b[la]), "norm_g": _pc(norm_g[la]),
                    "w_in": w_in[la],
                    "gq128": np.ascontiguousarray(np.tile(q_norm_g[la], 2)[:, None]),
                    "gk128": np.ascontiguousarray(np.tile(k_norm_g[la], 2)[:, None]),
                })
            if has_B:
                lb = k - 1
                lam_init = 0.8 - 0.6 * math.exp(-0.3 * lb)
                lconst = np.empty((128, 2), np.float32)
                lconst[:, 0] = lam_init
                lconst[:, 1] = 1.0 - lam_init
                m.update({
                    "qT_i": state[i]["qT_o"], "bz_i": state[i]["bz_o"], "szb_i": state[i]["szb_o"],
                    "sga_i": state[i]["sga_o"], "sgb_i": state[i]["sgb_o"], "gate_i": state[i]["gate_o"],
                    "cvh_i": state[i]["cvh"], "K_all": state["K_all"], "V_all": state["V_all"],
                    "conv_w": np.ascontiguousarray(conv_w[lb].reshape(3, 8, 128).transpose(2, 0, 1).reshape(128, 24)),
                    "w_conv_out": w_conv_out[lb], "w_attn_out": w_attn_out[lb], "w_o": w_o[lb],
                    "lamp": np.stack([lam_q1[lb], lam_k1[lb], lam_q2[lb], lam_k2[lb]]),
                    "lconst": lconst, "subg": np.ascontiguousarray(subln_g[lb][:, None]),
                })
            in_maps.append(m)
        res = run_bass_kernel_spmd(nc, in_maps, core_ids=list(range(NCORES)))
        outs = res.results
        xT = [np.asarray(outs[i]["xT_o"]) for i in range(NCORES)]
        if has_A:
            kk = np.stack([np.asarray(outs[i]["kT_o"]) for i in range(NCORES)])
            kk = kk.reshape(8, 8, 128, 8, 128).transpose(1, 2, 3, 0, 4)
            K_all = np.ascontiguousarray(kk.reshape(8, 128, SEQ))
            vv = np.stack([np.asarray(outs[i]["v_o"]) for i in range(NCORES)])
            vv = vv.reshape(8, 8, 128, 8, 128).transpose(3, 2, 1, 0, 4)
            V_all = np.ascontiguousarray(vv.reshape(8, 128, 64 * 128))
            cvs = [np.asarray(outs[i]["cv_o"]).reshape(1024, 8, 128) for i in range(NCORES)]
            state = {"K_all": K_all, "V_all": V_all}
            for i in range(NCORES):
                cvh = np.zeros((1024, 8, 130), np.float32)
                cvh[:, :, 2:] = cvs[i]
                for j in range(8):
                    g = 8 * j + i
                    if g > 0:
                        cvh[:, j, 0:2] = cvs[(g - 1) % 8][:, (g - 1) // 8, 126:128]
                st = {kname: np.asarray(outs[i][kname]) for kname in ("qT_o", "bz_o", "szb_o", "sga_o", "sgb_o", "gate_o")}
                st["cvh"] = cvh.reshape(1024, 8 * 130)
                state[i] = st
    out = np.empty((8, 8, 128, D), np.float32)
    for i in range(NCORES):
        out[:, i] = xT[i].T.reshape(8, 128, D)
    return out.reshape(1, SEQ, D)
```

```python
import math
import contextlib
import numpy as np
import ml_dtypes
import concourse.bass as bass
import concourse.mybir as mybir
from concourse.bass_utils import run_bass_kernel_spmd

F32 = mybir.dt.float32
BF16 = mybir.dt.bfloat16
AF = mybir.ActivationFunctionType
ALU = mybir.AluOpType
AX = mybir.AxisListType

NCORES = 8
D = 2048
SEQ = 8192
T = 1024
DEPTH = 4
DIN = 12288
EPS = 1e-6
C_U, C_BG, C_CG, C_ZA, C_Q, C_K, C_V, C_ZB, C_GA, C_GB = 0, 1024, 2048, 3072, 4096, 5120, 6144, 7168, 8192, 10240
NEG = -30000.0

ENGS = ("pe", "act", "dve", "pool", "sp")
SEM_LIMIT = 30000


class Region:
    __slots__ = ("w", "rc", "rd")

    def __init__(self):
        self.w = None
        self.rc = {}
        self.rd = []


class Rec:
    __slots__ = ("eng", "fn", "deps", "needs_inc", "is_dma", "dma_slot", "dma_val", "semref")

    def __init__(self, eng, fn, is_dma):
        self.eng = eng
        self.fn = fn
        self.deps = []
        self.needs_inc = False
        self.is_dma = is_dma
        self.dma_slot = None
        self.dma_val = None
        self.semref = None


class Sched:
    def __init__(self, nc, n_dma_sems=32, same_engine_sync=True):
        self.nc = nc
        self.q = {e: [] for e in ENGS}
        self.n_dma_sems = n_dma_sems
        self.n_sw = 8
        self.n_hw = n_dma_sems - self.n_sw
        self.dma_count = 0
        self.dma_cnt = {"sw": 0, "hw": 0}
        self.dma_slot_last = [None] * n_dma_sems
        self.same_engine_sync = same_engine_sync
        self.regions = {}
        self.pending = {}

    def R(self, *key):
        r = self.regions.get(key)
        if r is None:
            r = Region()
            self.regions[key] = r
        return r

    def barrier(self):
        deps = [self.q[e][-1] for e in ENGS if self.q[e]]
        deps += [d for d in self.dma_slot_last if d is not None]
        for e in ENGS:
            self.pending[e] = list(deps)

    def _add(self, eng, fn, reads, writes, is_dma):
        rec = Rec(eng, fn, is_dma)
        deps = []
        for r in reads:
            if r.w is not None:
                deps.append(r.w)
        for w in writes:
            if w.w is not None:
                deps.append(w.w)
            deps.extend(w.rc.values())
            deps.extend(w.rd)
        if eng in self.pending:
            deps.extend(self.pending.pop(eng))
        if is_dma:
            if eng == "pool":
                c = self.dma_cnt["sw"]
                self.dma_cnt["sw"] += 1
                slot = c % self.n_sw
                rec.dma_val = 16 * (c // self.n_sw + 1)
            else:
                c = self.dma_cnt["hw"]
                self.dma_cnt["hw"] += 1
                slot = self.n_sw + c % self.n_hw
                rec.dma_val = 16 * (c // self.n_hw + 1)
            rec.dma_slot = slot
            prev = self.dma_slot_last[slot]
            if prev is not None:
                deps.append(prev)
            self.dma_slot_last[slot] = rec
            self.dma_count += 1
        seen = set()
        for d in deps:
            if d is rec or id(d) in seen:
                continue
            seen.add(id(d))
            if (not d.is_dma) and d.eng == eng and not is_dma:
                if eng == "pe" or not self.same_engine_sync:
                    continue
            rec.deps.append(d)
            if not d.is_dma:
                d.needs_inc = True
        self.q[eng].append(rec)
        for r in reads:
            if is_dma:
                r.rd.append(rec)
            else:
                r.rc[eng] = rec
        for w in writes:
            w.w = rec
            w.rc = {}
            w.rd = []
        return rec

    def op(self, eng, fn, reads=(), writes=()):
        return self._add(eng, fn, reads, writes, False)

    def dma(self, eng, fn, reads=(), writes=()):
        return self._add(eng, fn, reads, writes, True)

    def emit(self, final_wait_recs=()):
        nc = self.nc
        with contextlib.ExitStack() as es:
            for e in ENGS:
                cnt = 0
                sems = []
                for rec in self.q[e]:
                    if rec.is_dma or not rec.needs_inc:
                        continue
                    si = cnt // SEM_LIMIT
                    while len(sems) <= si:
                        sems.append(es.enter_context(nc.semaphore(f"s_{e}_{len(sems)}")))
                    rec.semref = (sems[si], cnt % SEM_LIMIT + 1)
                    cnt += 1
            dma_sems = [es.enter_context(nc.semaphore(f"s_dma_{i}")) for i in range(self.n_dma_sems)]
            for e in ENGS:
                for rec in self.q[e]:
                    if rec.is_dma:
                        rec.semref = (dma_sems[rec.dma_slot], rec.dma_val)
            block = es.enter_context(nc.Block())
            engmap = {"pe": block.tensor, "act": block.scalar, "dve": block.vector,
                      "pool": block.gpsimd, "sp": block.sync}

            def make(e):
                def body(engine):
                    known = {}
                    for rec in self.q[e]:
                        for d in rec.deps:
                            sem, val = d.semref
                            k = id(sem)
                            if known.get(k, 0) >= val:
                                continue
                            known[k] = val
                            engine.wait_ge(sem, val)
                        ins = rec.fn(engine)
                        if rec.is_dma:
                            ins.then_inc(rec.semref[0], 16)
                        elif rec.needs_inc:
                            ins.then_inc(rec.semref[0], 1)
                    if e == "sp":
                        for d in final_wait_recs:
                            sem, val = d.semref
                            engine.wait_ge(sem, val)
                return body

            for e in ENGS:
                engmap[e](make(e))


DEBUG = False


def build_program(has_B, has_A, first):
    nc = bass.Bass("TRN2", target_bir_lowering=False)
    S = Sched(nc)
    R = S.R

    def din(name, shape, dt=F32):
        return nc.dram_tensor(name, list(shape), dt, kind="ExternalInput").ap()

    def dout(name, shape, dt=F32):
        return nc.dram_tensor(name, list(shape), dt, kind="ExternalOutput").ap()

    cos_d = din("cos", [128, T])
    sin_d = din("sin", [128, T])
    rmat_d = din("rmat", [128, 128])
    ones_d = din("ones", [128, 128])
    blk_d = din("blk64", [128, 128])
    ident_d = din("ident", [128, 128], BF16)
    onesb_d = din("onesb", [128, 128], BF16)
    mask_d = din("mask", [128, 8, 128], BF16)
    xT_in = din("xT_in", [D, T])

    if has_A:
        c_d = din("c_vec", [128, 16])
        adaw_d = din("ada_w", [D, 3 * D])
        adab_d = din("ada_b", [128, 48])
        ng_d = din("norm_g", [128, 16])
        win_d = din("w_in", [D, DIN])
        gq_d = din("gq128", [128, 1])
        gk_d = din("gk128", [128, 1])
        kT_o = dout("kT_o", [1024, T], BF16)
        v_o = dout("v_o", [T, 1024], BF16)
        cv_o = dout("cv_o", [1024, T])
        qT_o = dout("qT_o", [1024, T], BF16)
        bz_o = dout("bz_o", [1024, T], BF16)
        szb_o = dout("szb_o", [1024, T], BF16)
        sga_o = dout("sga_o", [D, T], BF16)
        sgb_o = dout("sgb_o", [D, T], BF16)
        gate_o = dout("gate_o", [128, 16])
        if DEBUG:
            dbg_hT = dout("dbg_hT", [D, T], BF16)
            dbg_w = dout("dbg_w", [128, 16 * 512], BF16)
    if has_B:
        qT_i = din("qT_i", [1024, T], BF16)
        bz_i = din("bz_i", [1024, T], BF16)
        szb_i = din("szb_i", [1024, T], BF16)
        sga_i = din("sga_i", [D, T], BF16)
        sgb_i = din("sgb_i", [D, T], BF16)
        gate_i = din("gate_i", [128, 16])
        cvh_i = din("cvh_i", [1024, 8 * 130])
        K_i = din("K_all", [8, 128, SEQ], BF16)
        V_i = din("V_all", [8, 128, 64 * 128], BF16)
        cw_d = din("conv_w", [128, 24])
        wco_d = din("w_conv_out", [1024, D])
        wao_d = din("w_attn_out", [1024, D])
        wo_d = din("w_o", [D, D])
        lamp_d = din("lamp", [4, 64])
        lconst_d = din("lconst", [128, 2])
        subg_d = din("subg", [128, 1])
    xT_o = dout("xT_o", [D, T])

    fin = []
    with contextlib.ExitStack() as es:
        def sb(name, shape, dt, stack=es):
            return stack.enter_context(nc.sbuf_tensor(name, list(shape), dt))

        ps = [es.enter_context(nc.psum_tensor(f"ps{i}", [128, 512], F32)) for i in range(8)]
        PS = [R("ps", i) for i in range(8)]

        xT = sb("xT", [128, 16, T], F32)
        rmat_s = sb("rmat_s", [128, 128], F32)
        ones_s = sb("ones_s", [128, 128], F32)
        blk_s = sb("blk_s", [128, 128], F32)
        ident_s = sb("ident_s", [128, 128], BF16)
        onesb_s = sb("onesb_s", [128, 128], BF16)
        mask_s = sb("mask_s", [128, 8, 128], BF16)
        wbuf = [None, None]

        def ld(eng, dst, src, reg):
            return S.dma(eng, lambda e: e.dma_start(out=dst, in_=src), writes=[reg])

        ld("sp", rmat_s[:], rmat_d, R("rmat"))
        ld("sp", ones_s[:], ones_d, R("ones"))
        ld("sp", blk_s[:], blk_d, R("blk"))
        ld("sp", ident_s[:], ident_d, R("ident"))
        ld("sp", onesb_s[:], onesb_d, R("onesb"))
        ld("sp", mask_s[:], mask_d, R("mask"))
        for kc in range(16):
            S.dma("sp", lambda e, kc=kc: e.dma_start(out=xT[:, kc, :], in_=xT_in[kc * 128:(kc + 1) * 128, :]),
                  writes=[R("xT", kc, 0), R("xT", kc, 1)])

        class WStream:
            def __init__(self, blocks):
                self.blocks = blocks
                self.i = 0
                self._issue(0)

            def _issue(self, j):
                if j >= len(self.blocks):
                    return
                s = j % 2
                for (w_ap, nk, col0, ncols, kc0) in self.blocks[j]:
                    src = w_ap[:, col0:col0 + ncols].rearrange("(kc p) n -> p kc n", p=128)
                    dst = wbuf[s][:, kc0:kc0 + nk, 0:ncols]
                    regs = [R("w", s, 0), R("w", s, 1)] if nk == 16 else [R("w", s, kc0 // 8)]
                    S.dma("pool", lambda e, dst=dst, src=src: e.dma_start(out=dst, in_=src), writes=regs)

            def get(self):
                j = self.i
                self.i += 1
                self._issue(j + 1)
                return j % 2

        def WR(s):
            return [R("w", s, 0), R("w", s, 1)]

        def mm(out, lhsT, rhs, start, stop, reads, writes):
            S.op("pe", lambda e: e.matmul(out, lhsT, rhs, start=start, stop=stop), reads=reads, writes=writes)

        rot = {}

        def rotbuf(stack, name, n, shape, dt):
            bufs = [sb(f"{name}{i}", shape, dt, stack) for i in range(n)]
            rot[name] = [0, n, bufs]

        def nxt(name):
            st = rot[name]
            i = st[0] % st[1]
            st[0] += 1
            return st[2][i], R(name, i)

        psr = {"n": 0}

        def nps(lo=0, n=4):
            i = lo + psr["n"] % n
            psr["n"] += 1
            return i

        def phase_B():
            with contextlib.ExitStack() as bs:
                gate_s = sb("gate_s", [128, 16], F32, bs)
                cw_s = sb("cw_s", [128, 24], F32, bs)
                lamp_s = sb("lamp_s", [128, 4, 64], F32, bs)
                lconst_s = sb("lconst_s", [128, 2], F32, bs)
                subg_s = sb("subg_s", [128, 1], F32, bs)
                lam_t = sb("lam_t", [128, 2, 64], F32, bs)
                lam_r = sb("lam_r", [128, 4], F32, bs)
                neglam = sb("neglam", [128, 1], F32, bs)
                g_s = sb("g_s", [128, 8, T], BF16, bs)
                onT = sb("onT", [128, 8, T], BF16, bs)
                ld("sp", gate_s[:], gate_i, R("gate_s"))
                ld("sp", cw_s[:], cw_d, R("cw"))
                ld("sp", lconst_s[:], lconst_d, R("lconst"))
                ld("sp", subg_s[:], subg_d, R("subg"))
                lamp_b = bass.AP(lamp_d.tensor, 0, [[0, 128], [1, 256]])
                ld("sp", lamp_s[:].rearrange("p a b -> p (a b)"), lamp_b, R("lamp"))
                S.op("dve", lambda e: e.tensor_tensor(lam_t[:, 0, :], lamp_s[:, 0, :], lamp_s[:, 1, :], ALU.mult), reads=[R("lamp")], writes=[R("lamt0")])
                S.op("dve", lambda e: e.tensor_tensor(lam_t[:, 1, :], lamp_s[:, 2, :], lamp_s[:, 3, :], ALU.mult), reads=[R("lamp")], writes=[R("lamt1")])
                S.op("dve", lambda e: e.reduce_sum(lam_r[:, 0:1], lam_t[:, 0, :], axis=AX.X), reads=[R("lamt0")], writes=[R("lamr0")])
                S.op("dve", lambda e: e.reduce_sum(lam_r[:, 1:2], lam_t[:, 1, :], axis=AX.X), reads=[R("lamt1")], writes=[R("lamr1")])
                S.op("act", lambda e: e.activation(out=lam_r[:, 2:4], in_=lam_r[:, 0:2], func=AF.Exp), reads=[R("lamr0"), R("lamr1")], writes=[R("lamr2")])
                S.op("dve", lambda e: e.tensor_tensor(neglam[:], lam_r[:, 3:4], lam_r[:, 2:3], ALU.subtract), reads=[R("lamr2")], writes=[R("neglam0")])
                S.op("dve", lambda e: e.tensor_tensor(neglam[:], neglam[:], lconst_s[:, 0:1], ALU.subtract), reads=[R("neglam0"), R("lconst")], writes=[R("neglam")])
                S.op("dve", lambda e: e.tensor_tensor(subg_s[:], subg_s[:], lconst_s[:, 1:2], ALU.mult), reads=[R("subg"), R("lconst")], writes=[R("subg")])

                with contextlib.ExitStack() as cs:
                    cvh = [sb(f"cvh{i}", [128, 8, 130], F32, cs) for i in range(2)]
                    bzs = [sb(f"bzs{i}", [128, T], BF16, cs) for i in range(2)]
                    ytmp = [sb(f"ytmp{i}", [128, 8, 128], F32, cs) for i in range(2)]
                    for c in range(8):
                        b = c % 2
                        ld("sp", cvh[b][:].rearrange("p a b -> p (a b)"), cvh_i[c * 128:(c + 1) * 128, :], R("cvh", b))
                        ld("sp", bzs[b][:], bz_i[c * 128:(c + 1) * 128, :], R("bzs", b))
                        y = ytmp[b]
                        cv = cvh[b]
                        S.op("dve", lambda e, y=y, cv=cv, c=c: e.tensor_scalar(y[:], cv[:, :, 2:130], cw_s[:, 16 + c:17 + c], None, ALU.mult),
                             reads=[R("cvh", b), R("cw")], writes=[R("ytmp", b)])
                        S.op("dve", lambda e, y=y, cv=cv, c=c: e.scalar_tensor_tensor(y[:], cv[:, :, 1:129], cw_s[:, 8 + c:9 + c], y[:], ALU.mult, ALU.add),
                             reads=[R("cvh", b), R("cw"), R("ytmp", b)], writes=[R("ytmp", b)])
                        S.op("dve", lambda e, y=y, cv=cv, c=c: e.scalar_tensor_tensor(y[:], cv[:, :, 0:128], cw_s[:, c:c + 1], y[:], ALU.mult, ALU.add),
                             reads=[R("cvh", b), R("cw"), R("ytmp", b)], writes=[R("ytmp", b)])
                        bz = bzs[b]
                        S.op("dve", lambda e, y=y, bz=bz, c=c: e.tensor_tensor(g_s[:, c, :], y[:].rearrange("p a b -> p (a b)"), bz[:], ALU.mult),
                             reads=[R("ytmp", b), R("bzs", b)], writes=[R("g", c)])
                S.barrier()

                with contextlib.ExitStack() as as_:
                    Kh = [sb(f"Kh{i}", [128, 32, 128], BF16, as_) for i in range(4)]
                    Vh = [sb(f"Vh{i}", [128, 32, 128], BF16, as_) for i in range(4)]
                    qh_s = [sb(f"qh{i}", [128, T], BF16, as_) for i in range(2)]
                    zb_s = [sb(f"zbh{i}", [128, T], BF16, as_) for i in range(2)]
                    rotbuf(as_, "pt", 4, [128, 512], BF16)
                    rotbuf(as_, "ef", 12, [128, 512], F32)
                    hslot = {"n": 0}

                    def load_half(h, hf):
                        s = hslot["n"] % 4
                        hslot["n"] += 1
                        ld("sp", Kh[s][:].rearrange("p a b -> p (a b)"), K_i[h, :, hf * 4096:(hf + 1) * 4096], R("Kh", s))
                        ld("sp", Vh[s][:].rearrange("p a b -> p (a b)"), V_i[h, :, hf * 4096:(hf + 1) * 4096], R("Vh", s))
                        return s

                    def load_head(h):
                        b = h % 2
                        ld("sp", qh_s[b][:], qT_i[h * 128:(h + 1) * 128, :], R("qh", b))
                        ld("sp", zb_s[b][:], szb_i[h * 128:(h + 1) * 128, :], R("zbh", b))
                        return (load_half(h, 0), load_half(h, 1))

                    slots_next = load_head(0)
                    deferred = []

                    def rec_S(h, hb, slots, qh, kt):
                        j0 = max(kt // 8, 4 * qh)
                        c0 = j0 * 128
                        c1 = (4 * qh + 4) * 128
                        n = c1 - c0
                        s = slots[kt // 32]
                        ktl = kt % 32
                        diag = (kt // 8 == j0)
                        for comp in range(2):
                            bnk = (kt % 2) * 2 + comp
                            lo, hi = comp * 64, comp * 64 + 64
                            mm(ps[bnk][:, 0:n], Kh[s][lo:hi, ktl, :], qh_s[hb][lo:hi, c0:c1], True, True,
                               [R("Kh", s), R("qh", hb)], [PS[bnk]])

                    def rec_PV(h, hb, slots, qh, kt, nkt):
                        j0 = max(kt // 8, 4 * qh)
                        c0 = j0 * 128
                        n = (4 * qh + 4) * 128 - c0
                        off = c0 - 4 * qh * 128
                        s = slots[kt // 32]
                        ktl = kt % 32
                        for comp in range(2):
                            bnk = (kt % 2) * 2 + comp
                            pt, ptr = nxt("pt")
                            S.op("act", lambda e, pt=pt, bnk=bnk, n=n: e.activation(out=pt[:, 0:n], in_=ps[bnk][:, 0:n], func=AF.Exp, scale=0.125),
                                 reads=[PS[bnk]], writes=[ptr])
                            if kt // 8 == j0:
                                S.op("pool", lambda e, pt=pt, kt=kt: e.tensor_tensor(pt[:, 0:128], pt[:, 0:128], mask_s[:, kt % 8, :], ALU.mult),
                                     reads=[ptr, R("mask")], writes=[ptr])
                            mm(ps[4 + comp][:, off:off + n], Vh[s][:, ktl, :], pt[:, 0:n], kt == 0, kt == nkt - 1,
                               [R("Vh", s), ptr], [PS[4 + comp]])
                            mm(ps[6 + comp][:, off:off + n], onesb_s[:], pt[:, 0:n], kt == 0, kt == nkt - 1,
                               [R("onesb"), ptr], [PS[6 + comp]])

                    def epilogue(h, hb, qh):
                        cs0 = qh * 512
                        r0, r0r = nxt("ef")
                        r1, r1r = nxt("ef")
                        o0, o0r = nxt("ef")
                        o1, o1r = nxt("ef")
                        sq, sqr = nxt("ef")
                        rs, rsr = nxt("ef")
                        S.op("dve", lambda e: e.reciprocal(r0[:], ps[6][:]), reads=[PS[6]], writes=[r0r])
                        S.op("dve", lambda e: e.reciprocal(r1[:], ps[7][:]), reads=[PS[7]], writes=[r1r])
                        S.op("dve", lambda e: e.tensor_tensor(o0[:], ps[4][:], r0[:], ALU.mult), reads=[PS[4], r0r], writes=[o0r])
                        S.op("dve", lambda e: e.tensor_tensor(o1[:], ps[5][:], r1[:], ALU.mult), reads=[PS[5], r1r], writes=[o1r])

                        def part2():
                            S.op("dve", lambda e: e.scalar_tensor_tensor(o0[:], o1[:], neglam[:, 0:1], o0[:], ALU.mult, ALU.add),
                                 reads=[o0r, o1r, R("neglam")], writes=[o0r])
                            S.op("pool", lambda e: e.tensor_tensor(sq[:], o0[:], o0[:], ALU.mult), reads=[o0r], writes=[sqr])
                            mm(ps[0][:], ones_s[:], sq[:], True, True, [R("ones"), sqr], [PS[0]])
                            S.op("act", lambda e: e.activation(out=rs[:], in_=ps[0][:], func=AF.Sqrt, scale=1.0 / 128.0, bias=eps_s[:, 0:1]),
                                 reads=[PS[0], R("eps")], writes=[rsr])
                            S.op("dve", lambda e: e.reciprocal(rs[:], rs[:]), reads=[rsr], writes=[rsr])
                            S.op("dve", lambda e: e.tensor_tensor(o0[:], o0[:], rs[:], ALU.mult), reads=[o0r, rsr], writes=[o0r])
                            S.op("dve", lambda e: e.scalar_tensor_tensor(
                                onT[:, h, cs0:cs0 + 512], o0[:], subg_s[:, 0:1], zb_s[hb][:, cs0:cs0 + 512], ALU.mult, ALU.mult),
                                reads=[o0r, R("subg"), R("zbh", hb)], writes=[R("onT", h, qh)])
                        deferred.append(part2)

                    for h in range(8):
                        slots = slots_next
                        hb = h % 2
                        for qh in range(2):
                            nkt = 32 * qh + 32
                            rec_S(h, hb, slots, qh, 0)
                            for kt in range(nkt):
                                if kt + 1 < nkt:
                                    rec_S(h, hb, slots, qh, kt + 1)
                                rec_PV(h, hb, slots, qh, kt, nkt)
                                if kt == 0 and deferred:
                                    deferred.pop(0)()
                            epilogue(h, hb, qh)
                            if qh == 0 and h + 1 < 8:
                                slots_next = load_head(h + 1)
                    while deferred:
                        deferred.pop(0)()
                S.barrier()

                with contextlib.ExitStack() as ms:
                    merged = sb("merged", [128, 16, T], BF16, ms)
                    wbuf[0] = sb("wbufB0", [128, 16, 512], BF16, ms)
                    wbuf[1] = sb("wbufB1", [128, 16, 512], BF16, ms)
                    sga_s = [sb(f"sga{i}", [128, T], BF16, ms) for i in range(2)]
                    sgb_s = [sb(f"sgb{i}", [128, T], BF16, ms) for i in range(2)]
                    rotbuf(ms, "mf", 4, [128, 512], F32)
                    wsB = WStream([[(wco_d, 8, blk * 512, 512, 0), (wao_d, 8, blk * 512, 512, 8)] for blk in range(4)]
                                  + [[(wo_d, 16, blk * 512, 512, 0)] for blk in range(4)])
                    for blk in range(4):
                        sc = wsB.get()
                        sa = sc
                        for f in range(4):
                            dc = blk * 4 + f
                            gb_ = dc % 2
                            ld("sp", sga_s[gb_][:], sga_i[dc * 128:(dc + 1) * 128, :], R("sga", gb_))
                            ld("sp", sgb_s[gb_][:], sgb_i[dc * 128:(dc + 1) * 128, :], R("sgb", gb_))
                            for tb in range(2):
                                tsl = slice(tb * 512, tb * 512 + 512)
                                pc = nps()
                                for c in range(8):
                                    mm(ps[pc][:], wbuf[sc][:, c, f * 128:(f + 1) * 128], g_s[:, c, tsl], c == 0, c == 7,
                                       [R("w", sc, 0), R("g", c)], [PS[pc]])
                                pa = nps()
                                for hh in range(8):
                                    mm(ps[pa][:], wbuf[sa][:, 8 + hh, f * 128:(f + 1) * 128], onT[:, hh, tsl], hh == 0, hh == 7,
                                       [R("w", sa, 1), R("onT", hh, tb)], [PS[pa]])
                                t1, t1r = nxt("mf")
                                t2, t2r = nxt("mf")
                                S.op("dve", lambda e, t1=t1, pc=pc, gb_=gb_, tsl=tsl: e.tensor_tensor(t1[:], ps[pc][:], sga_s[gb_][:, tsl], ALU.mult),
                                     reads=[PS[pc], R("sga", gb_)], writes=[t1r])
                                S.op("dve", lambda e, t2=t2, pa=pa, gb_=gb_, tsl=tsl: e.tensor_tensor(t2[:], ps[pa][:], sgb_s[gb_][:, tsl], ALU.mult),
                                     reads=[PS[pa], R("sgb", gb_)], writes=[t2r])
                                S.op("pool", lambda e, t1=t1, t2=t2, dc=dc, tsl=tsl: e.tensor_tensor(merged[:, dc, tsl], t1[:], t2[:], ALU.add),
                                     reads=[t1r, t2r], writes=[R("merged", dc, tb)])
                    for blk in range(4):
                        so = wsB.get()
                        for f in range(4):
                            dc = blk * 4 + f
                            for tb in range(2):
                                tsl = slice(tb * 512, tb * 512 + 512)
                                po = nps()
                                for kc in range(16):
                                    mm(ps[po][:], wbuf[so][:, kc, f * 128:(f + 1) * 128], merged[:, kc, tsl], kc == 0, kc == 15,
                                       WR(so) + [R("merged", kc, tb)], [PS[po]])
                                S.op("dve", lambda e, po=po, dc=dc, tsl=tsl: e.scalar_tensor_tensor(
                                    xT[:, dc, tsl], ps[po][:], gate_s[:, dc:dc + 1], xT[:, dc, tsl], ALU.mult, ALU.add),
                                    reads=[PS[po], R("gate_s"), R("xT", dc, tb)], writes=[R("xT", dc, tb)])
                S.barrier()

        def phase_A():
            with contextlib.ExitStack() as a_:
                hT = sb("hT", [128, 16, T], BF16, a_)
                c_s = sb("c_s", [128, 16], F32, a_)
                cact = sb("cact", [128, 16], F32, a_)
                adab_s = sb("adab_s", [128, 48], F32, a_)
                ng_s = sb("ng_s", [128, 16], F32, a_)
                mod_s = sb("mod_s", [128, 48], F32, a_)
                asc = sb("asc", [128, 16], F32, a_)
                gq_s = sb("gq_s", [128, 1], F32, a_)
                gk_s = sb("gk_s", [128, 1], F32, a_)
                cos_s = sb("cos_s", [128, T], F32, a_)
                sin_s = sb("sin_s", [128, T], F32, a_)
                ld("sp", cos_s[:], cos_d, R("cos"))
                ld("sp", sin_s[:], sin_d, R("sin"))
                rotbuf(a_, "af", 12, [128, 512], F32)
                rsn = [sb(f"rsn{i}", [128, 512], F32, a_) for i in range(2)]
                rotbuf(a_, "ob", 4, [128, 512], BF16)
                rotbuf(a_, "of", 2, [128, 512], F32)
                m_ = contextlib.ExitStack()
                adaw = [sb(f"adaw{i}", [128, 16, 256], F32, m_) for i in range(2)]
                rotbuf(m_, "macc_dve", 2, [128, 256], F32)
                rotbuf(m_, "macc_pool", 2, [128, 256], F32)
                ld("sp", c_s[:], c_d, R("c_s"))
                ld("sp", adab_s[:], adab_d, R("adab"))
                ld("sp", ng_s[:], ng_d, R("ng"))
                ld("sp", gq_s[:], gq_d, R("gq"))
                ld("sp", gk_s[:], gk_d, R("gk"))
                for kc in range(16):
                    fin.append(S.dma("sp", lambda e, kc=kc: e.dma_start(out=xT_o[kc * 128:(kc + 1) * 128, :], in_=xT[:, kc, :]),
                                     reads=[R("xT", kc, 0), R("xT", kc, 1)]))
                S.op("act", lambda e: e.activation(out=cact[:], in_=c_s[:], func=AF.Silu), reads=[R("c_s")], writes=[R("cact")])
                for blk in range(24):
                    b = blk % 2
                    src = adaw_d[:, blk * 256:(blk + 1) * 256].rearrange("(kc p) n -> p kc n", p=128)
                    S.dma("sp", lambda e, b=b, src=src: e.dma_start(out=adaw[b][:], in_=src), writes=[R("adaw", b)])
                    eng = "dve"
                    acc, _ = nxt("macc_" + eng)
                    ai = rot["macc_" + eng][0]
                    hr = [R("macc", ai % 2, 0), R("macc", ai % 2, 1)]
                    for kc in range(16):
                        for hf in range(2):
                            cs_ = slice(hf * 128, hf * 128 + 128)
                            if kc == 0:
                                S.op(eng, lambda e, acc=acc, b=b, cs_=cs_: e.tensor_scalar(acc[:, cs_], adaw[b][:, 0, cs_], cact[:, 0:1], 0.0, ALU.mult, ALU.add),
                                     reads=[R("adaw", b), R("cact")], writes=[hr[hf]])
                            else:
                                S.op(eng, lambda e, acc=acc, b=b, kc=kc, cs_=cs_: e.scalar_tensor_tensor(
                                    acc[:, cs_], adaw[b][:, kc, cs_], cact[:, kc:kc + 1], acc[:, cs_], ALU.mult, ALU.add),
                                    reads=[R("adaw", b), R("cact"), hr[hf]], writes=[hr[hf]])
                    for f in range(2):
                        cc = blk * 2 + f
                        mm(ps[7][:, cc:cc + 1], acc[:, f * 128:(f + 1) * 128], ones_s[:, 0:1], True, True,
                           [hr[f], R("ones")], [PS[7]])
                S.op("dve", lambda e: e.tensor_tensor(mod_s[:], ps[7][:, 0:48], adab_s[:], ALU.add), reads=[PS[7], R("adab")], writes=[R("mod")])
                S.op("dve", lambda e: e.tensor_scalar(asc[:], mod_s[:, 16:32], 1.0, None, ALU.add), reads=[R("mod")], writes=[R("asc0")])
                S.op("dve", lambda e: e.tensor_tensor(asc[:], asc[:], ng_s[:], ALU.mult), reads=[R("asc0"), R("ng")], writes=[R("asc")])
                fin.append(S.dma("sp", lambda e: e.dma_start(out=gate_o, in_=mod_s[:, 32:48]), reads=[R("mod")]))
                S.barrier()
                m_.close()
                pair = sb("pair", [128, 4, T], F32, a_)
                wbuf[0] = sb("wbufA0", [128, 16, 512], BF16, a_)
                wbuf[1] = sb("wbufA1", [128, 16, 512], BF16, a_)
                for tb in range(2):
                    tsl = slice(tb * 512, tb * 512 + 512)
                    for kc in range(16):
                        sq, sqr = nxt("af")
                        S.op("pool", lambda e, sq=sq, kc=kc, tsl=tsl: e.tensor_tensor(sq[:], xT[:, kc, tsl], xT[:, kc, tsl], ALU.mult),
                             reads=[R("xT", kc, tb)], writes=[sqr])
                        mm(ps[4][:], ones_s[:], sq[:], kc == 0, kc == 15, [R("ones"), sqr], [PS[4]])
                    rs, rsr = rsn[tb], R("rsn", tb)
                    S.op("act", lambda e, rs=rs: e.activation(out=rs[:], in_=ps[4][:], func=AF.Sqrt, scale=1.0 / D, bias=eps_s[:, 0:1]),
                         reads=[PS[4], R("eps")], writes=[rsr])
                    S.op("dve", lambda e, rs=rs: e.reciprocal(rs[:], rs[:]), reads=[rsr], writes=[rsr])
                    for kc in range(16):
                        tmp, tmr = nxt("af")
                        S.op("dve", lambda e, tmp=tmp, rs=rs, kc=kc, tsl=tsl: e.scalar_tensor_tensor(
                            tmp[:], xT[:, kc, tsl], asc[:, kc:kc + 1], rs[:], ALU.mult, ALU.mult),
                            reads=[R("xT", kc, tb), R("asc"), rsr], writes=[tmr])
                        S.op("act", lambda e, tmp=tmp, kc=kc, tsl=tsl: e.activation(
                            out=hT[:, kc, tsl], in_=tmp[:], func=AF.Identity, bias=mod_s[:, kc:kc + 1], scale=1.0),
                            reads=[tmr, R("mod")], writes=[R("hT", kc, tb)])

                if DEBUG:
                    for kc in range(16):
                        fin.append(S.dma("sp", lambda e, kc=kc: e.dma_start(out=dbg_hT[kc * 128:(kc + 1) * 128, :], in_=hT[:, kc, :]),
                                         reads=[R("hT", kc, 0), R("hT", kc, 1)]))

                blocksA = []
                for col in (C_K, C_V):
                    blocksA += [[(win_d, 16, col + blk * 512, 512, 0)] for blk in range(2)]
                for blk in range(2):
                    blocksA += [[(win_d, 16, C_U + blk * 512, 512, 0)], [(win_d, 16, C_CG + blk * 512, 512, 0)]]
                blocksA += [[(win_d, 16, C_Q + blk * 512, 512, 0)] for blk in range(2)]
                for blk in range(2):
                    blocksA += [[(win_d, 16, C_ZA + blk * 512, 512, 0)], [(win_d, 16, C_BG + blk * 512, 512, 0)]]
                blocksA += [[(win_d, 16, C_ZB + blk * 512, 512, 0)] for blk in range(2)]
                blocksA += [[(win_d, 16, C_GA + blk * 512, 512, 0)] for blk in range(4)]
                blocksA += [[(win_d, 16, C_GB + blk * 512, 512, 0)] for blk in range(4)]
                wsA = WStream(blocksA)

                def hreads(tb):
                    return [R("hT", kc, tb) for kc in range(16)]

                def proj_fm(ws, f, tb):
                    b = nps()
                    tsl = slice(tb * 512, tb * 512 + 512)
                    for kc in range(16):
                        mm(ps[b][:], wbuf[ws][:, kc, f * 128:(f + 1) * 128], hT[:, kc, tsl], kc == 0, kc == 15,
                           WR(ws) + [R("hT", kc, tb)], [PS[b]])
                    return b

                def st(dst, src_buf, reg):
                    fin.append(S.dma("sp", lambda e: e.dma_start(out=dst, in_=src_buf), reads=[reg]))

                pend = []

                def flush():
                    while pend:
                        pend.pop(0)()

                def qk_epilogue(b, g_s_, g_r, tb, dst):
                    tsl = slice(tb * 512, tb * 512 + 512)
                    raw, rawr = nxt("af")
                    kg, kgr = nxt("af")
                    sq, sqr = nxt("af")
                    rs, rsr = nxt("af")
                    t1, t1r = nxt("af")
                    t2, t2r = nxt("af")
                    S.op("act", lambda e: e.activation(out=raw[:], in_=ps[b][:], func=AF.Identity), reads=[PS[b]], writes=[rawr])
                    S.op("dve", lambda e: e.tensor_scalar(kg[:], raw[:], g_s_[:, 0:1], None, ALU.mult), reads=[rawr, g_r], writes=[kgr])
                    S.op("pool", lambda e: e.tensor_tensor(sq[:], raw[:], raw[:], ALU.mult), reads=[rawr], writes=[sqr])

                    def part2():
                        mm(ps[6][:], blk_s[:], sq[:], True, True, [R("blk"), sqr], [PS[6]])
                        mm(ps[5][:], rmat_s[:], kg[:], True, True, [R("rmat"), kgr], [PS[5]])
                        S.op("act", lambda e: e.activation(out=rs[:], in_=ps[6][:], func=AF.Sqrt, scale=1.0 / 64.0, bias=eps_s[:, 0:1]),
                             reads=[PS[6], R("eps")], writes=[rsr])
                        S.op("dve", lambda e: e.reciprocal(rs[:], rs[:]), reads=[rsr], writes=[rsr])
                        S.op("dve", lambda e: e.tensor_tensor(t1[:], kg[:], cos_s[:, tsl], ALU.mult), reads=[kgr, R("cos")], writes=[t1r])
                        S.op("dve", lambda e: e.tensor_tensor(t2[:], ps[5][:], sin_s[:, tsl], ALU.mult), reads=[PS[5], R("sin")], writes=[t2r])
                        S.op("pool", lambda e: e.tensor_tensor(t1[:], t1[:], t2[:], ALU.add), reads=[t1r, t2r], writes=[t1r])
                        ob, obr = nxt("ob")
                        S.op("dve", lambda e: e.tensor_tensor(ob[:], t1[:], rs[:], ALU.mult), reads=[t1r, rsr], writes=[obr])
                        st(dst, ob[:], obr)
                    pend.append(part2)

                for blk in range(2):
                    ws = wsA.get()
                    if DEBUG and blk == 0:
                        fin.append(S.dma("sp", lambda e, ws=ws: e.dma_start(out=dbg_w, in_=wbuf[ws][:].rearrange("p a b -> p (a b)")), reads=WR(ws)))
                    for f in range(4):
                        hh = blk * 4 + f
                        for tb in range(2):
                            b = proj_fm(ws, f, tb)
                            flush()
                            qk_epilogue(b, gk_s, R("gk"), tb, kT_o[hh * 128:(hh + 1) * 128, tb * 512:(tb + 1) * 512])
                for blk in range(2):
                    ws = wsA.get()
                    for tt in range(8):
                        b = nps()
                        for kc in range(16):
                            mm(ps[b][:], hT[:, kc, tt * 128:(tt + 1) * 128], wbuf[ws][:, kc, :], kc == 0, kc == 15,
                               WR(ws) + [R("hT", kc, tt // 4)], [PS[b]])
                        flush()
                        ob, obr = nxt("ob")
                        S.op("act", lambda e, ob=ob, b=b: e.activation(out=ob[:], in_=ps[b][:], func=AF.Identity), reads=[PS[b]], writes=[obr])
                        st(v_o[tt * 128:(tt + 1) * 128, blk * 512:(blk + 1) * 512], ob[:], obr)
                for blk in range(2):
                    ws = wsA.get()
                    for f in range(4):
                        for tb in range(2):
                            b = proj_fm(ws, f, tb)
                            S.op("act", lambda e, b=b, f=f, tb=tb: e.activation(out=pair[:, f, tb * 512:(tb + 1) * 512], in_=ps[b][:], func=AF.Identity),
                                 reads=[PS[b]], writes=[R("pair", f, tb)])
                    ws = wsA.get()
                    for f in range(4):
                        c = blk * 4 + f
                        for tb in range(2):
                            b = proj_fm(ws, f, tb)
                            of, ofr = nxt("of")
                            S.op("dve", lambda e, of=of, b=b, f=f, tb=tb: e.tensor_tensor(of[:], ps[b][:], pair[:, f, tb * 512:(tb + 1) * 512], ALU.mult),
                                 reads=[PS[b], R("pair", f, tb)], writes=[ofr])
                            st(cv_o[c * 128:(c + 1) * 128, tb * 512:(tb + 1) * 512], of[:], ofr)
                for blk in range(2):
                    ws = wsA.get()
                    for f in range(4):
                        hh = blk * 4 + f
                        for tb in range(2):
                            b = proj_fm(ws, f, tb)
                            flush()
                            qk_epilogue(b, gq_s, R("gq"), tb, qT_o[hh * 128:(hh + 1) * 128, tb * 512:(tb + 1) * 512])
                flush()
                for blk in range(2):
                    ws = wsA.get()
                    for f in range(4):
                        for tb in range(2):
                            b = proj_fm(ws, f, tb)
                            S.op("act", lambda e, b=b, f=f, tb=tb: e.activation(out=pair[:, f, tb * 512:(tb + 1) * 512], in_=ps[b][:], func=AF.Silu),
                                 reads=[PS[b]], writes=[R("pair", f, tb)])
                    ws = wsA.get()
                    for f in range(4):
                        c = blk * 4 + f
                        for tb in range(2):
                            b = proj_fm(ws, f, tb)
                            ob, obr = nxt("ob")
                            S.op("dve", lambda e, ob=ob, b=b, f=f, tb=tb: e.tensor_tensor(ob[:], ps[b][:], pair[:, f, tb * 512:(tb + 1) * 512], ALU.mult),
                                 reads=[PS[b], R("pair", f, tb)], writes=[obr])
                            st(bz_o[c * 128:(c + 1) * 128, tb * 512:(tb + 1) * 512], ob[:], obr)
                for (col, nblk, func, dst) in ((C_ZB, 2, AF.Silu, szb_o), (C_GA, 4, AF.Sigmoid, sga_o), (C_GB, 4, AF.Sigmoid, sgb_o)):
                    for blk in range(nblk):
                        ws = wsA.get()
                        for f in range(4):
                            c = blk * 4 + f
                            for tb in range(2):
                                b = proj_fm(ws, f, tb)
                                ob, obr = nxt("ob")
                                S.op("act", lambda e, ob=ob, b=b, func=func: e.activation(out=ob[:], in_=ps[b][:], func=func), reads=[PS[b]], writes=[obr])
                                st(dst[c * 128:(c + 1) * 128, tb * 512:(tb + 1) * 512], ob[:], obr)
                S.barrier()

        eps_s = sb("eps_s", [128, 1], F32)
        S.op("pool", lambda e: e.memset(eps_s[:], EPS), writes=[R("eps")])

        if has_B:
            phase_B()
        if has_A:
            phase_A()
        else:
            for kc in range(16):
                fin.append(S.dma("sp", lambda e, kc=kc: e.dma_start(out=xT_o[kc * 128:(kc + 1) * 128, :], in_=xT[:, kc, :]),
                                 reads=[R("xT", kc, 0), R("xT", kc, 1)]))
        S.emit(final_wait_recs=fin)
    return nc


_PROGS = {}


def _prog(has_B, has_A, first):
    key = (has_B, has_A, first)
    if key not in _PROGS:
        _PROGS[key] = build_program(has_B, has_A, first)
    return _PROGS[key]


def _consts():
    bf = ml_dtypes.bfloat16
    half = 32
    inv = (10000.0 ** (-np.arange(half, dtype=np.float64) / half))
    rmat = np.zeros((128, 128), np.float32)
    for m in range(128):
        d = m % 64
        if d < 32:
            rmat[m + 32, m] = -1.0
        else:
            rmat[m - 32, m] = 1.0
    blk = np.zeros((128, 128), np.float32)
    blk[:64, :64] = 1.0
    blk[64:, 64:] = 1.0
    per_core = []
    for i in range(NCORES):
        pos = np.concatenate([np.arange(128) + (8 * j + i) * 128 for j in range(8)]).astype(np.float64)
        ang = inv[np.arange(128) % 32][:, None] * pos[None, :]
        ang = ang.astype(np.float32).astype(np.float64)
        cos = np.cos(ang).astype(np.float32)
        sin = np.sin(ang).astype(np.float32)
        mask = np.ones((128, 8, 128), np.float32)
        for ip in range(8):
            if ip > i:
                mask[:, ip, :] = 0.0
            elif ip == i:
                kc = np.arange(128)[:, None] // 64
                qc = np.arange(128)[None, :] // 64
                mask[:, ip, :] = np.where(kc <= qc, 1.0, 0.0)
        per_core.append({
            "cos": cos, "sin": sin, "rmat": rmat, "ones": np.ones((128, 128), np.float32), "blk64": blk,
            "ident": np.eye(128, dtype=np.float32).astype(bf), "onesb": np.ones((128, 128), np.float32).astype(bf),
            "mask": mask.astype(bf),
        })
    return per_core


def _pc(v):
    return np.ascontiguousarray(v.reshape(-1, 128).T)


def kernel(x, c, ada_w, ada_b, norm_g, w_in, conv_w, w_conv_out, q_norm_g, k_norm_g,
           lam_q1, lam_k1, lam_q2, lam_k2, subln_g, w_attn_out, w_o):
    f = lambda a: np.ascontiguousarray(np.asarray(a, dtype=np.float32))
    x, c, ada_w, ada_b, norm_g, w_in, conv_w, w_conv_out = map(f, (x, c, ada_w, ada_b, norm_g, w_in, conv_w, w_conv_out))
    q_norm_g, k_norm_g, lam_q1, lam_k1, lam_q2, lam_k2, subln_g, w_attn_out, w_o = map(
        f, (q_norm_g, k_norm_g, lam_q1, lam_k1, lam_q2, lam_k2, subln_g, w_attn_out, w_o))
    consts = _consts()
    xt = x[0].reshape(8, 8, 128, D)
    xT = [np.ascontiguousarray(xt[:, i].reshape(T, D).T) for i in range(NCORES)]
    state = None
    for k in range(DEPTH + 1):
        has_B = k > 0
        has_A = k < DEPTH
        nc = _prog(has_B, has_A, k == 0)
        in_maps = []
        for i in range(NCORES):
            m = dict(consts[i])
            m["xT_in"] = xT[i]
            if has_A:
                la = k
                m.update({
                    "c_vec": _pc(c[0]), "ada_w": ada_w[la], "ada_b": _pc(ada_b[la]), "norm_g": _pc(norm_g[la]),
                    "w_in": w_in[la],
                    "gq128": np.ascontiguousarray(np.tile(q_norm_g[la], 2)[:, None]),
                    "gk128": np.ascontiguousarray(np.tile(k_norm_g[la], 2)[:, None]),
                })
            if has_B:
                lb = k - 1
                lam_init = 0.8 - 0.6 * math.exp(-0.3 * lb)
                lconst = np.empty((128, 2), np.float32)
                lconst[:, 0] = lam_init
                lconst[:, 1] = 1.0 - lam_init
                m.update({
                    "qT_i": state[i]["qT_o"], "bz_i": state[i]["bz_o"], "szb_i": state[i]["szb_o"],
                    "sga_i": state[i]["sga_o"], "sgb_i": state[i]["sgb_o"], "gate_i": state[i]["gate_o"],
                    "cvh_i": state[i]["cvh"], "K_all": state["K_all"], "V_all": state["V_all"],
                    "conv_w": np.ascontiguousarray(conv_w[lb].reshape(3, 8, 128).transpose(2, 0, 1).reshape(128, 24)),
                    "w_conv_out": w_conv_out[lb], "w_attn_out": w_attn_out[lb], "w_o": w_o[lb],
                    "lamp": np.stack([lam_q1[lb], lam_k1[lb], lam_q2[lb], lam_k2[lb]]),
                    "lconst": lconst, "subg": np.ascontiguousarray(subln_g[lb][:, None]),
                })
            in_maps.append(m)
        res = run_bass_kernel_spmd(nc, in_maps, core_ids=list(range(NCORES)))
        outs = res.results
        xT = [np.asarray(outs[i]["xT_o"]) for i in range(NCORES)]
        if has_A:
            kk = np.stack([np.asarray(outs[i]["kT_o"]) for i in range(NCORES)])
            kk = kk.reshape(8, 8, 128, 8, 128).transpose(1, 2, 3, 0, 4)
            K_all = np.ascontiguousarray(kk.reshape(8, 128, SEQ))
            vv = np.stack([np.asarray(outs[i]["v_o"]) for i in range(NCORES)])
            vv = vv.reshape(8, 8, 128, 8, 128).transpose(3, 2, 1, 0, 4)
            V_all = np.ascontiguousarray(vv.reshape(8, 128, 64 * 128))
            cvs = [np.asarray(outs[i]["cv_o"]).reshape(1024, 8, 128) for i in range(NCORES)]
            state = {"K_all": K_all, "V_all": V_all}
            for i in range(NCORES):
                cvh = np.zeros((1024, 8, 130), np.float32)
                cvh[:, :, 2:] = cvs[i]
                for j in range(8):
                    g = 8 * j + i
                    if g > 0:
                        cvh[:, j, 0:2] = cvs[(g - 1) % 8][:, (g - 1) // 8, 126:128]
                st = {kname: np.asarray(outs[i][kname]) for kname in ("qT_o", "bz_o", "szb_o", "sga_o", "sgb_o", "gate_o")}
                st["cvh"] = cvh.reshape(1024, 8 * 130)
                state[i] = st
    out = np.empty((8, 8, 128, D), np.float32)
    for i in range(NCORES):
        out[:, i] = xT[i].T.reshape(8, 128, D)
    return out.reshape(1, SEQ, D)
```

```python
import math
import contextlib
import numpy as np
import ml_dtypes
import concourse.bass as bass
import concourse.mybir as mybir
from concourse.bass_utils import run_bass_kernel_spmd

F32 = mybir.dt.float32
BF16 = mybir.dt.bfloat16
AF = mybir.ActivationFunctionType
ALU = mybir.AluOpType
AX = mybir.AxisListType

NCORES = 8
D = 2048
SEQ = 8192
T = 1024
DEPTH = 4
DIN = 12288
EPS = 1e-6
C_U, C_BG, C_CG, C_ZA, C_Q, C_K, C_V, C_ZB, C_GA, C_GB = 0, 1024, 2048, 3072, 4096, 5120, 6144, 7168, 8192, 10240
NEG = -30000.0

ENGS = ("pe", "act", "dve", "pool", "sp")
SEM_LIMIT = 30000


class Region:
    __slots__ = ("w", "rc", "rd")

    def __init__(self):
        self.w = None
        self.rc = {}
        self.rd = []


class Rec:
    __slots__ = ("eng", "fn", "deps", "needs_inc", "is_dma", "dma_slot", "dma_val", "semref")

    def __init__(self, eng, fn, is_dma):
        self.eng = eng
        self.fn = fn
        self.deps = []
        self.needs_inc = False
        self.is_dma = is_dma
        self.dma_slot = None
        self.dma_val = None
        self.semref = None


class Sched:
    def __init__(self, nc, n_dma_sems=32, same_engine_sync=True):
        self.nc = nc
        self.q = {e: [] for e in ENGS}
        self.n_dma_sems = n_dma_sems
        self.n_sw = 8
        self.n_hw = n_dma_sems - self.n_sw
        self.dma_count = 0
        self.dma_cnt = {"sw": 0, "hw": 0}
        self.dma_slot_last = [None] * n_dma_sems
        self.same_engine_sync = same_engine_sync
        self.regions = {}
        self.pending = {}

    def R(self, *key):
        r = self.regions.get(key)
        if r is None:
            r = Region()
            self.regions[key] = r
        return r

    def barrier(self):
        deps = [self.q[e][-1] for e in ENGS if self.q[e]]
        deps += [d for d in self.dma_slot_last if d is not None]
        for e in ENGS:
            self.pending[e] = list(deps)

    def _add(self, eng, fn, reads, writes, is_dma):
        rec = Rec(eng, fn, is_dma)
        deps = []
        for r in reads:
            if r.w is not None:
                deps.append(r.w)
        for w in writes:
            if w.w is not None:
                deps.append(w.w)
            deps.extend(w.rc.values())
            deps.extend(w.rd)
        if eng in self.pending:
            deps.extend(self.pending.pop(eng))
        if is_dma:
            if eng == "pool":
                c = self.dma_cnt["sw"]
                self.dma_cnt["sw"] += 1
                slot = c % self.n_sw
                rec.dma_val = 16 * (c // self.n_sw + 1)
            else:
                c = self.dma_cnt["hw"]
                self.dma_cnt["hw"] += 1
                slot = self.n_sw + c % self.n_hw
                rec.dma_val = 16 * (c // self.n_hw + 1)
            rec.dma_slot = slot
            prev = self.dma_slot_last[slot]
            if prev is not None:
                deps.append(prev)
            self.dma_slot_last[slot] = rec
            self.dma_count += 1
        seen = set()
        for d in deps:
            if d is rec or id(d) in seen:
                continue
            seen.add(id(d))
            if (not d.is_dma) and d.eng == eng and not is_dma:
                if eng == "pe" or not self.same_engine_sync:
                    continue
            rec.deps.append(d)
            if not d.is_dma:
                d.needs_inc = True
        self.q[eng].append(rec)
        for r in reads:
            if is_dma:
                r.rd.append(rec)
            else:
                r.rc[eng] = rec
        for w in writes:
            w.w = rec
            w.rc = {}
            w.rd = []
        return rec

    def op(self, eng, fn, reads=(), writes=()):
        return self._add(eng, fn, reads, writes, False)

    def dma(self, eng, fn, reads=(), writes=()):
        return self._add(eng, fn, reads, writes, True)

    def emit(self, final_wait_recs=()):
        nc = self.nc
        with contextlib.ExitStack() as es:
            for e in ENGS:
                cnt = 0
                sems = []
                for rec in self.q[e]:
                    if rec.is_dma or not rec.needs_inc:
                        continue
                    si = cnt // SEM_LIMIT
                    while len(sems) <= si:
                        sems.append(es.enter_context(nc.semaphore(f"s_{e}_{len(sems)}")))
                    rec.semref = (sems[si], cnt % SEM_LIMIT + 1)
                    cnt += 1
            dma_sems = [es.enter_context(nc.semaphore(f"s_dma_{i}")) for i in range(self.n_dma_sems)]
            for e in ENGS:
                for rec in self.q[e]:
                    if rec.is_dma:
                        rec.semref = (dma_sems[rec.dma_slot], rec.dma_val)
            block = es.enter_context(nc.Block())
            engmap = {"pe": block.tensor, "act": block.scalar, "dve": block.vector,
                      "pool": block.gpsimd, "sp": block.sync}

            def make(e):
                def body(engine):
                    known = {}
                    for rec in self.q[e]:
                        for d in rec.deps:
                            sem, val = d.semref
                            k = id(sem)
                            if known.get(k, 0) >= val:
                                continue
                            known[k] = val
                            engine.wait_ge(sem, val)
                        ins = rec.fn(engine)
                        if rec.is_dma:
                            ins.then_inc(rec.semref[0], 16)
                        elif rec.needs_inc:
                            ins.then_inc(rec.semref[0], 1)
                    if e == "sp":
                        for d in final_wait_recs:
                            sem, val = d.semref
                            engine.wait_ge(sem, val)
                return body

            for e in ENGS:
                engmap[e](make(e))


DEBUG = False


def build_program(has_B, has_A, first, mod_full=True, mod_next=False):
    nc = bass.Bass("TRN2", target_bir_lowering=False)
    S = Sched(nc)
    R = S.R

    def din(name, shape, dt=F32):
        return nc.dram_tensor(name, list(shape), dt, kind="ExternalInput").ap()

    def dout(name, shape, dt=F32):
        return nc.dram_tensor(name, list(shape), dt, kind="ExternalOutput").ap()

    cos_d = din("cos", [128, T])
    sin_d = din("sin", [128, T])
    rmat_d = din("rmat", [128, 128])
    ones_d = din("ones", [128, 128])
    blk_d = din("blk64", [128, 128])
    ident_d = din("ident", [128, 128], BF16)
    onesb_d = din("onesb", [128, 128], BF16)
    mask_d = din("mask", [128, 8, 128], BF16)
    xT_in = din("xT_in", [D, T])

    if has_A:
        c_d = din("c_vec", [128, 16])
        if mod_full:
            adaw_d = din("ada_w", [D, 3 * D])
        else:
            modraw_d = din("mod_raw", [128, 48])
        if mod_next:
            adawn_d = din("ada_w_next", [D, 768])
            modsl_o = dout("mod_slice_o", [128, 6])
        adab_d = din("ada_b", [128, 48])
        ng_d = din("norm_g", [128, 16])
        win_d = din("w_in", [D, DIN])
        gq_d = din("gq128", [128, 1])
        gk_d = din("gk128", [128, 1])
        kT_o = dout("kT_o", [1024, T], BF16)
        v_o = dout("v_o", [T, 1024], BF16)
        cv_o = dout("cv_o", [1024, T])
        qT_o = dout("qT_o", [1024, T], BF16)
        bz_o = dout("bz_o", [1024, T], BF16)
        szb_o = dout("szb_o", [1024, T], BF16)
        sga_o = dout("sga_o", [D, T], BF16)
        sgb_o = dout("sgb_o", [D, T], BF16)
        gate_o = dout("gate_o", [128, 16])
        if DEBUG:
            dbg_hT = dout("dbg_hT", [D, T], BF16)
            dbg_w = dout("dbg_w", [128, 16 * 512], BF16)
    if has_B:
        qT_i = din("qT_i", [1024, T], BF16)
        bz_i = din("bz_i", [1024, T], BF16)
        szb_i = din("szb_i", [1024, T], BF16)
        sga_i = din("sga_i", [D, T], BF16)
        sgb_i = din("sgb_i", [D, T], BF16)
        gate_i = din("gate_i", [128, 16])
        cvh_i = din("cvh_i", [1024, 8 * 130])
        K_i = din("K_all", [8, 128, SEQ], BF16)
        V_i = din("V_all", [8, 128, 64 * 128], BF16)
        cw_d = din("conv_w", [128, 24])
        wco_d = din("w_conv_out", [1024, D])
        wao_d = din("w_attn_out", [1024, D])
        wo_d = din("w_o", [D, D])
        lamp_d = din("lamp", [4, 64])
        lconst_d = din("lconst", [128, 2])
        subg_d = din("subg", [128, 1])
    xT_o = dout("xT_o", [D, T])

    fin = []
    with contextlib.ExitStack() as es:
        def sb(name, shape, dt, stack=es):
            return stack.enter_context(nc.sbuf_tensor(name, list(shape), dt))

        ps = [es.enter_context(nc.psum_tensor(f"ps{i}", [128, 512], F32)) for i in range(8)]
        PS = [R("ps", i) for i in range(8)]

        xT = sb("xT", [128, 16, T], F32)
        rmat_s = sb("rmat_s", [128, 128], F32)
        ones_s = sb("ones_s", [128, 128], F32)
        blk_s = sb("blk_s", [128, 128], F32)
        ident_s = sb("ident_s", [128, 128], BF16)
        onesb_s = sb("onesb_s", [128, 128], BF16)
        mask_s = sb("mask_s", [128, 8, 128], BF16)
        wbuf = [None, None]

        def ld(eng, dst, src, reg):
            return S.dma(eng, lambda e: e.dma_start(out=dst, in_=src), writes=[reg])

        ld("sp", rmat_s[:], rmat_d, R("rmat"))
        ld("sp", ones_s[:], ones_d, R("ones"))
        ld("sp", blk_s[:], blk_d, R("blk"))
        ld("sp", ident_s[:], ident_d, R("ident"))
        ld("sp", onesb_s[:], onesb_d, R("onesb"))
        ld("sp", mask_s[:], mask_d, R("mask"))
        for kc in range(16):
            S.dma("sp", lambda e, kc=kc: e.dma_start(out=xT[:, kc, :], in_=xT_in[kc * 128:(kc + 1) * 128, :]),
                  writes=[R("xT", kc, 0), R("xT", kc, 1)])

        class WStream:
            def __init__(self, blocks):
                self.blocks = blocks
                self.i = 0
                self._issue(0)

            def _issue(self, j):
                if j >= len(self.blocks):
                    return
                s = j % 2
                for (w_ap, nk, col0, ncols, kc0) in self.blocks[j]:
                    src = w_ap[:, col0:col0 + ncols].rearrange("(kc p) n -> p kc n", p=128)
                    dst = wbuf[s][:, kc0:kc0 + nk, 0:ncols]
                    regs = [R("w", s, 0), R("w", s, 1)] if nk == 16 else [R("w", s, kc0 // 8)]
                    S.dma("pool", lambda e, dst=dst, src=src: e.dma_start(out=dst, in_=src), writes=regs)

            def get(self):
                j = self.i
                self.i += 1
                self._issue(j + 1)
                return j % 2

        def WR(s):
            return [R("w", s, 0), R("w", s, 1)]

        def mm(out, lhsT, rhs, start, stop, reads, writes):
            S.op("pe", lambda e: e.matmul(out, lhsT, rhs, start=start, stop=stop), reads=reads, writes=writes)

        rot = {}

        def rotbuf(stack, name, n, shape, dt):
            bufs = [sb(f"{name}{i}", shape, dt, stack) for i in range(n)]
            rot[name] = [0, n, bufs]

        def nxt(name):
            st = rot[name]
            i = st[0] % st[1]
            st[0] += 1
            return st[2][i], R(name, i)

        psr = {"n": 0}

        def nps(lo=0, n=4):
            i = lo + psr["n"] % n
            psr["n"] += 1
            return i

        def phase_B():
            with contextlib.ExitStack() as bs:
                gate_s = sb("gate_s", [128, 16], F32, bs)
                cw_s = sb("cw_s", [128, 24], F32, bs)
                lamp_s = sb("lamp_s", [128, 4, 64], F32, bs)
                lconst_s = sb("lconst_s", [128, 2], F32, bs)
                subg_s = sb("subg_s", [128, 1], F32, bs)
                lam_t = sb("lam_t", [128, 2, 64], F32, bs)
                lam_r = sb("lam_r", [128, 4], F32, bs)
                neglam = sb("neglam", [128, 1], F32, bs)
                g_s = sb("g_s", [128, 8, T], BF16, bs)
                onT = sb("onT", [128, 8, T], BF16, bs)
                ld("sp", gate_s[:], gate_i, R("gate_s"))
                ld("sp", cw_s[:], cw_d, R("cw"))
                ld("sp", lconst_s[:], lconst_d, R("lconst"))
                ld("sp", subg_s[:], subg_d, R("subg"))
                lamp_b = bass.AP(lamp_d.tensor, 0, [[0, 128], [1, 256]])
                ld("sp", lamp_s[:].rearrange("p a b -> p (a b)"), lamp_b, R("lamp"))
                S.op("dve", lambda e: e.tensor_tensor(lam_t[:, 0, :], lamp_s[:, 0, :], lamp_s[:, 1, :], ALU.mult), reads=[R("lamp")], writes=[R("lamt0")])
                S.op("dve", lambda e: e.tensor_tensor(lam_t[:, 1, :], lamp_s[:, 2, :], lamp_s[:, 3, :], ALU.mult), reads=[R("lamp")], writes=[R("lamt1")])
                S.op("dve", lambda e: e.reduce_sum(lam_r[:, 0:1], lam_t[:, 0, :], axis=AX.X), reads=[R("lamt0")], writes=[R("lamr0")])
                S.op("dve", lambda e: e.reduce_sum(lam_r[:, 1:2], lam_t[:, 1, :], axis=AX.X), reads=[R("lamt1")], writes=[R("lamr1")])
                S.op("act", lambda e: e.activation(out=lam_r[:, 2:4], in_=lam_r[:, 0:2], func=AF.Exp), reads=[R("lamr0"), R("lamr1")], writes=[R("lamr2")])
                S.op("dve", lambda e: e.tensor_tensor(neglam[:], lam_r[:, 3:4], lam_r[:, 2:3], ALU.subtract), reads=[R("lamr2")], writes=[R("neglam0")])
                S.op("dve", lambda e: e.tensor_tensor(neglam[:], neglam[:], lconst_s[:, 0:1], ALU.subtract), reads=[R("neglam0"), R("lconst")], writes=[R("neglam")])
                S.op("dve", lambda e: e.tensor_tensor(subg_s[:], subg_s[:], lconst_s[:, 1:2], ALU.mult), reads=[R("subg"), R("lconst")], writes=[R("subg")])

                with contextlib.ExitStack() as cs:
                    cvh = [sb(f"cvh{i}", [128, 8, 130], F32, cs) for i in range(2)]
                    bzs = [sb(f"bzs{i}", [128, T], BF16, cs) for i in range(2)]
                    ytmp = [sb(f"ytmp{i}", [128, 8, 128], F32, cs) for i in range(2)]
                    for c in range(8):
                        b = c % 2
                        ld("sp", cvh[b][:].rearrange("p a b -> p (a b)"), cvh_i[c * 128:(c + 1) * 128, :], R("cvh", b))
                        ld("sp", bzs[b][:], bz_i[c * 128:(c + 1) * 128, :], R("bzs", b))
                        y = ytmp[b]
                        cv = cvh[b]
                        S.op("dve", lambda e, y=y, cv=cv, c=c: e.tensor_scalar(y[:], cv[:, :, 2:130], cw_s[:, 16 + c:17 + c], None, ALU.mult),
                             reads=[R("cvh", b), R("cw")], writes=[R("ytmp", b)])
                        S.op("dve", lambda e, y=y, cv=cv, c=c: e.scalar_tensor_tensor(y[:], cv[:, :, 1:129], cw_s[:, 8 + c:9 + c], y[:], ALU.mult, ALU.add),
                             reads=[R("cvh", b), R("cw"), R("ytmp", b)], writes=[R("ytmp", b)])
                        S.op("dve", lambda e, y=y, cv=cv, c=c: e.scalar_tensor_tensor(y[:], cv[:, :, 0:128], cw_s[:, c:c + 1], y[:], ALU.mult, ALU.add),
                             reads=[R("cvh", b), R("cw"), R("ytmp", b)], writes=[R("ytmp", b)])
                        bz = bzs[b]
                        S.op("dve", lambda e, y=y, bz=bz, c=c: e.tensor_tensor(g_s[:, c, :], y[:].rearrange("p a b -> p (a b)"), bz[:], ALU.mult),
                             reads=[R("ytmp", b), R("bzs", b)], writes=[R("g", c)])
                S.barrier()

                with contextlib.ExitStack() as as_:
                    Kh = [sb(f"Kh{i}", [128, 32, 128], BF16, as_) for i in range(4)]
                    Vh = [sb(f"Vh{i}", [128, 32, 128], BF16, as_) for i in range(4)]
                    qh_s = [sb(f"qh{i}", [128, T], BF16, as_) for i in range(2)]
                    zb_s = [sb(f"zbh{i}", [128, T], BF16, as_) for i in range(2)]
                    rotbuf(as_, "pt", 4, [128, 512], BF16)
                    rotbuf(as_, "ef", 12, [128, 512], F32)
                    hslot = {"n": 0}

                    def load_half(h, hf):
                        s = hslot["n"] % 4
                        hslot["n"] += 1
                        ld("sp", Kh[s][:].rearrange("p a b -> p (a b)"), K_i[h, :, hf * 4096:(hf + 1) * 4096], R("Kh", s))
                        ld("sp", Vh[s][:].rearrange("p a b -> p (a b)"), V_i[h, :, hf * 4096:(hf + 1) * 4096], R("Vh", s))
                        return s

                    def load_head(h):
                        b = h % 2
                        ld("sp", qh_s[b][:], qT_i[h * 128:(h + 1) * 128, :], R("qh", b))
                        ld("sp", zb_s[b][:], szb_i[h * 128:(h + 1) * 128, :], R("zbh", b))
                        return (load_half(h, 0), load_half(h, 1))

                    slots_next = load_head(0)
                    deferred = []

                    def rec_S(h, hb, slots, qh, kt):
                        j0 = max(kt // 8, 4 * qh)
                        c0 = j0 * 128
                        c1 = (4 * qh + 4) * 128
                        n = c1 - c0
                        s = slots[kt // 32]
                        ktl = kt % 32
                        diag = (kt // 8 == j0)
                        for comp in range(2):
                            bnk = (kt % 2) * 2 + comp
                            lo, hi = comp * 64, comp * 64 + 64
                            mm(ps[bnk][:, 0:n], Kh[s][lo:hi, ktl, :], qh_s[hb][lo:hi, c0:c1], True, True,
                               [R("Kh", s), R("qh", hb)], [PS[bnk]])

                    def rec_PV(h, hb, slots, qh, kt, nkt):
                        j0 = max(kt // 8, 4 * qh)
                        c0 = j0 * 128
                        n = (4 * qh + 4) * 128 - c0
                        off = c0 - 4 * qh * 128
                        s = slots[kt // 32]
                        ktl = kt % 32
                        for comp in range(2):
                            bnk = (kt % 2) * 2 + comp
                            pt, ptr = nxt("pt")
                            S.op("act", lambda e, pt=pt, bnk=bnk, n=n: e.activation(out=pt[:, 0:n], in_=ps[bnk][:, 0:n], func=AF.Exp, scale=0.125),
                                 reads=[PS[bnk]], writes=[ptr])
                            if kt // 8 == j0:
                                S.op("pool", lambda e, pt=pt, kt=kt: e.tensor_tensor(pt[:, 0:128], pt[:, 0:128], mask_s[:, kt % 8, :], ALU.mult),
                                     reads=[ptr, R("mask")], writes=[ptr])
                            mm(ps[4 + comp][:, off:off + n], Vh[s][:, ktl, :], pt[:, 0:n], kt == 0, kt == nkt - 1,
                               [R("Vh", s), ptr], [PS[4 + comp]])
                            mm(ps[6 + comp][:, off:off + n], onesb_s[:], pt[:, 0:n], kt == 0, kt == nkt - 1,
                               [R("onesb"), ptr], [PS[6 + comp]])

                    def epilogue(h, hb, qh):
                        cs0 = qh * 512
                        r0, r0r = nxt("ef")
                        r1, r1r = nxt("ef")
                        o0, o0r = nxt("ef")
                        o1, o1r = nxt("ef")
                        sq, sqr = nxt("ef")
                        rs, rsr = nxt("ef")
                        S.op("dve", lambda e: e.reciprocal(r0[:], ps[6][:]), reads=[PS[6]], writes=[r0r])
                        S.op("dve", lambda e: e.reciprocal(r1[:], ps[7][:]), reads=[PS[7]], writes=[r1r])
                        S.op("dve", lambda e: e.tensor_tensor(o0[:], ps[4][:], r0[:], ALU.mult), reads=[PS[4], r0r], writes=[o0r])
                        S.op("dve", lambda e: e.tensor_tensor(o1[:], ps[5][:], r1[:], ALU.mult), reads=[PS[5], r1r], writes=[o1r])

                        def part2():
                            S.op("dve", lambda e: e.scalar_tensor_tensor(o0[:], o1[:], neglam[:, 0:1], o0[:], ALU.mult, ALU.add),
                                 reads=[o0r, o1r, R("neglam")], writes=[o0r])
                            S.op("pool", lambda e: e.tensor_tensor(sq[:], o0[:], o0[:], ALU.mult), reads=[o0r], writes=[sqr])
                            mm(ps[0][:], ones_s[:], sq[:], True, True, [R("ones"), sqr], [PS[0]])
                            S.op("act", lambda e: e.activation(out=rs[:], in_=ps[0][:], func=AF.Sqrt, scale=1.0 / 128.0, bias=eps_s[:, 0:1]),
                                 reads=[PS[0], R("eps")], writes=[rsr])
                            S.op("dve", lambda e: e.reciprocal(rs[:], rs[:]), reads=[rsr], writes=[rsr])
                            S.op("dve", lambda e: e.tensor_tensor(o0[:], o0[:], rs[:], ALU.mult), reads=[o0r, rsr], writes=[o0r])
                            S.op("dve", lambda e: e.scalar_tensor_tensor(
                                onT[:, h, cs0:cs0 + 512], o0[:], subg_s[:, 0:1], zb_s[hb][:, cs0:cs0 + 512], ALU.mult, ALU.mult),
                                reads=[o0r, R("subg"), R("zbh", hb)], writes=[R("onT", h, qh)])
                        deferred.append(part2)

                    for h in range(8):
                        slots = slots_next
                        hb = h % 2
                        for qh in range(2):
                            nkt = 32 * qh + 32
                            rec_S(h, hb, slots, qh, 0)
                            for kt in range(nkt):
                                if kt + 1 < nkt:
                                    rec_S(h, hb, slots, qh, kt + 1)
                                rec_PV(h, hb, slots, qh, kt, nkt)
                                if kt == 0 and deferred:
                                    deferred.pop(0)()
                            epilogue(h, hb, qh)
                            if qh == 0 and h + 1 < 8:
                                slots_next = load_head(h + 1)
                    while deferred:
                        deferred.pop(0)()
                S.barrier()

                with contextlib.ExitStack() as ms:
                    merged = sb("merged", [128, 16, T], BF16, ms)
                    wbuf[0] = sb("wbufB0", [128, 16, 512], BF16, ms)
                    wbuf[1] = sb("wbufB1", [128, 16, 512], BF16, ms)
                    sga_s = [sb(f"sga{i}", [128, T], BF16, ms) for i in range(2)]
                    sgb_s = [sb(f"sgb{i}", [128, T], BF16, ms) for i in range(2)]
                    rotbuf(ms, "mf", 4, [128, 512], F32)
                    wsB = WStream([[(wco_d, 8, blk * 512, 512, 0), (wao_d, 8, blk * 512, 512, 8)] for blk in range(4)]
                                  + [[(wo_d, 16, blk * 512, 512, 0)] for blk in range(4)])
                    for blk in range(4):
                        sc = wsB.get()
                        sa = sc
                        for f in range(4):
                            dc = blk * 4 + f
                            gb_ = dc % 2
                            ld("sp", sga_s[gb_][:], sga_i[dc * 128:(dc + 1) * 128, :], R("sga", gb_))
                            ld("sp", sgb_s[gb_][:], sgb_i[dc * 128:(dc + 1) * 128, :], R("sgb", gb_))
                            for tb in range(2):
                                tsl = slice(tb * 512, tb * 512 + 512)
                                pc = nps()
                                for c in range(8):
                                    mm(ps[pc][:], wbuf[sc][:, c, f * 128:(f + 1) * 128], g_s[:, c, tsl], c == 0, c == 7,
                                       [R("w", sc, 0), R("g", c)], [PS[pc]])
                                pa = nps()
                                for hh in range(8):
                                    mm(ps[pa][:], wbuf[sa][:, 8 + hh, f * 128:(f + 1) * 128], onT[:, hh, tsl], hh == 0, hh == 7,
                                       [R("w", sa, 1), R("onT", hh, tb)], [PS[pa]])
                                t1, t1r = nxt("mf")
                                t2, t2r = nxt("mf")
                                S.op("dve", lambda e, t1=t1, pc=pc, gb_=gb_, tsl=tsl: e.tensor_tensor(t1[:], ps[pc][:], sga_s[gb_][:, tsl], ALU.mult),
                                     reads=[PS[pc], R("sga", gb_)], writes=[t1r])
                                S.op("dve", lambda e, t2=t2, pa=pa, gb_=gb_, tsl=tsl: e.tensor_tensor(t2[:], ps[pa][:], sgb_s[gb_][:, tsl], ALU.mult),
                                     reads=[PS[pa], R("sgb", gb_)], writes=[t2r])
                                S.op("pool", lambda e, t1=t1, t2=t2, dc=dc, tsl=tsl: e.tensor_tensor(merged[:, dc, tsl], t1[:], t2[:], ALU.add),
                                     reads=[t1r, t2r], writes=[R("merged", dc, tb)])
                    for blk in range(4):
                        so = wsB.get()
                        for f in range(4):
                            dc = blk * 4 + f
                            for tb in range(2):
                                tsl = slice(tb * 512, tb * 512 + 512)
                                po = nps()
                                for kc in range(16):
                                    mm(ps[po][:], wbuf[so][:, kc, f * 128:(f + 1) * 128], merged[:, kc, tsl], kc == 0, kc == 15,
                                       WR(so) + [R("merged", kc, tb)], [PS[po]])
                                S.op("dve", lambda e, po=po, dc=dc, tsl=tsl: e.scalar_tensor_tensor(
                                    xT[:, dc, tsl], ps[po][:], gate_s[:, dc:dc + 1], xT[:, dc, tsl], ALU.mult, ALU.add),
                                    reads=[PS[po], R("gate_s"), R("xT", dc, tb)], writes=[R("xT", dc, tb)])
                S.barrier()

        def phase_A():
            with contextlib.ExitStack() as a_:
                hT = sb("hT", [128, 16, T], BF16, a_)
                c_s = sb("c_s", [128, 16], F32, a_)
                cact = sb("cact", [128, 16], F32, a_)
                adab_s = sb("adab_s", [128, 48], F32, a_)
                ng_s = sb("ng_s", [128, 16], F32, a_)
                mod_s = sb("mod_s", [128, 48], F32, a_)
                modraw_s = sb("modraw_s", [128, 48], F32, a_)
                modsl_s = sb("modsl_s", [128, 6], F32, a_)
                asc = sb("asc", [128, 16], F32, a_)
                gq_s = sb("gq_s", [128, 1], F32, a_)
                gk_s = sb("gk_s", [128, 1], F32, a_)
                cos_s = sb("cos_s", [128, T], F32, a_)
                sin_s = sb("sin_s", [128, T], F32, a_)
                ld("sp", cos_s[:], cos_d, R("cos"))
                ld("sp", sin_s[:], sin_d, R("sin"))
                rotbuf(a_, "af", 12, [128, 512], F32)
                rsn = [sb(f"rsn{i}", [128, 512], F32, a_) for i in range(2)]
                rotbuf(a_, "ob", 4, [128, 512], BF16)
                rotbuf(a_, "of", 2, [128, 512], F32)
                m_ = contextlib.ExitStack()
                adaw = [sb(f"adaw{i}", [128, 16, 256], F32, m_) for i in range(2)]
                rotbuf(m_, "macc_dve", 2, [128, 256], F32)
                rotbuf(m_, "macc_pool", 2, [128, 256], F32)
                ld("sp", c_s[:], c_d, R("c_s"))
                ld("sp", adab_s[:], adab_d, R("adab"))
                ld("sp", ng_s[:], ng_d, R("ng"))
                ld("sp", gq_s[:], gq_d, R("gq"))
                ld("sp", gk_s[:], gk_d, R("gk"))
                for kc in range(16):
                    fin.append(S.dma("sp", lambda e, kc=kc: e.dma_start(out=xT_o[kc * 128:(kc + 1) * 128, :], in_=xT[:, kc, :]),
                                     reads=[R("xT", kc, 0), R("xT", kc, 1)]))
                S.op("act", lambda e: e.activation(out=cact[:], in_=c_s[:], func=AF.Silu), reads=[R("c_s")], writes=[R("cact")])
                def gemv(w_ap, nblk, col0):
                    for blk in range(nblk):
                        b = gemv_n[0] % 2
                        gemv_n[0] += 1
                        src = w_ap[:, blk * 256:(blk + 1) * 256].rearrange("(kc p) n -> p kc n", p=128)
                        S.dma("sp", lambda e, b=b, src=src: e.dma_start(out=adaw[b][:], in_=src), writes=[R("adaw", b)])
                        eng = "dve"
                        acc, _ = nxt("macc_" + eng)
                        ai = rot["macc_" + eng][0]
                        hr = [R("macc", ai % 2, 0), R("macc", ai % 2, 1)]
                        for kc in range(16):
                            for hf in range(2):
                                cs_ = slice(hf * 128, hf * 128 + 128)
                                if kc == 0:
                                    S.op(eng, lambda e, acc=acc, b=b, cs_=cs_: e.tensor_scalar(acc[:, cs_], adaw[b][:, 0, cs_], cact[:, 0:1], 0.0, ALU.mult, ALU.add),
                                         reads=[R("adaw", b), R("cact")], writes=[hr[hf]])
                                else:
                                    S.op(eng, lambda e, acc=acc, b=b, kc=kc, cs_=cs_: e.scalar_tensor_tensor(
                                        acc[:, cs_], adaw[b][:, kc, cs_], cact[:, kc:kc + 1], acc[:, cs_], ALU.mult, ALU.add),
                                        reads=[R("adaw", b), R("cact"), hr[hf]], writes=[hr[hf]])
                        for f in range(2):
                            cc = col0 + blk * 2 + f
                            mm(ps[7][:, cc:cc + 1], acc[:, f * 128:(f + 1) * 128], ones_s[:, 0:1], True, True,
                               [hr[f], R("ones")], [PS[7]])

                gemv_n = [0]
                if mod_full:
                    gemv(adaw_d, 24, 0)
                    S.op("dve", lambda e: e.tensor_tensor(mod_s[:], ps[7][:, 0:48], adab_s[:], ALU.add), reads=[PS[7], R("adab")], writes=[R("mod")])
                else:
                    ld("sp", modraw_s[:], modraw_d, R("modraw"))
                    S.op("dve", lambda e: e.tensor_tensor(mod_s[:], modraw_s[:], adab_s[:], ALU.add), reads=[R("modraw"), R("adab")], writes=[R("mod")])
                if mod_next:
                    gemv(adawn_d, 3, 48)
                    S.op("dve", lambda e: e.tensor_copy(out=modsl_s[:], in_=ps[7][:, 48:54]), reads=[PS[7]], writes=[R("modsl")])
                    fin.append(S.dma("sp", lambda e: e.dma_start(out=modsl_o, in_=modsl_s[:]), reads=[R("modsl")]))
                S.op("dve", lambda e: e.tensor_scalar(asc[:], mod_s[:, 16:32], 1.0, None, ALU.add), reads=[R("mod")], writes=[R("asc0")])
                S.op("dve", lambda e: e.tensor_tensor(asc[:], asc[:], ng_s[:], ALU.mult), reads=[R("asc0"), R("ng")], writes=[R("asc")])
                fin.append(S.dma("sp", lambda e: e.dma_start(out=gate_o, in_=mod_s[:, 32:48]), reads=[R("mod")]))
                S.barrier()
                m_.close()
                pair = sb("pair", [128, 4, T], F32, a_)
                wbuf[0] = sb("wbufA0", [128, 16, 512], BF16, a_)
                wbuf[1] = sb("wbufA1", [128, 16, 512], BF16, a_)
                for tb in range(2):
                    tsl = slice(tb * 512, tb * 512 + 512)
                    for kc in range(16):
                        sq, sqr = nxt("af")
                        S.op("pool", lambda e, sq=sq, kc=kc, tsl=tsl: e.tensor_tensor(sq[:], xT[:, kc, tsl], xT[:, kc, tsl], ALU.mult),
                             reads=[R("xT", kc, tb)], writes=[sqr])
                        mm(ps[4][:], ones_s[:], sq[:], kc == 0, kc == 15, [R("ones"), sqr], [PS[4]])
                    rs, rsr = rsn[tb], R("rsn", tb)
                    S.op("act", lambda e, rs=rs: e.activation(out=rs[:], in_=ps[4][:], func=AF.Sqrt, scale=1.0 / D, bias=eps_s[:, 0:1]),
                         reads=[PS[4], R("eps")], writes=[rsr])
                    S.op("dve", lambda e, rs=rs: e.reciprocal(rs[:], rs[:]), reads=[rsr], writes=[rsr])
                    for kc in range(16):
                        tmp, tmr = nxt("af")
                        S.op("dve", lambda e, tmp=tmp, rs=rs, kc=kc, tsl=tsl: e.scalar_tensor_tensor(
                            tmp[:], xT[:, kc, tsl], asc[:, kc:kc + 1], rs[:], ALU.mult, ALU.mult),
                            reads=[R("xT", kc, tb), R("asc"), rsr], writes=[tmr])
                        S.op("act", lambda e, tmp=tmp, kc=kc, tsl=tsl: e.activation(
                            out=hT[:, kc, tsl], in_=tmp[:], func=AF.Identity, bias=mod_s[:, kc:kc + 1], scale=1.0),
                            reads=[tmr, R("mod")], writes=[R("hT", kc, tb)])

                if DEBUG:
                    for kc in range(16):
                        fin.append(S.dma("sp", lambda e, kc=kc: e.dma_start(out=dbg_hT[kc * 128:(kc + 1) * 128, :], in_=hT[:, kc, :]),
                                         reads=[R("hT", kc, 0), R("hT", kc, 1)]))

                blocksA = []
                for col in (C_K, C_V):
                    blocksA += [[(win_d, 16, col + blk * 512, 512, 0)] for blk in range(2)]
                for blk in range(2):
                    blocksA += [[(win_d, 16, C_U + blk * 512, 512, 0)], [(win_d, 16, C_CG + blk * 512, 512, 0)]]
                blocksA += [[(win_d, 16, C_Q + blk * 512, 512, 0)] for blk in range(2)]
                for blk in range(2):
                    blocksA += [[(win_d, 16, C_ZA + blk * 512, 512, 0)], [(win_d, 16, C_BG + blk * 512, 512, 0)]]
                blocksA += [[(win_d, 16, C_ZB + blk * 512, 512, 0)] for blk in range(2)]
                blocksA += [[(win_d, 16, C_GA + blk * 512, 512, 0)] for blk in range(4)]
                blocksA += [[(win_d, 16, C_GB + blk * 512, 512, 0)] for blk in range(4)]
                wsA = WStream(blocksA)

                def hreads(tb):
                    return [R("hT", kc, tb) for kc in range(16)]

                def proj_fm(ws, f, tb):
                    b = nps()
                    tsl = slice(tb * 512, tb * 512 + 512)
                    for kc in range(16):
                        mm(ps[b][:], wbuf[ws][:, kc, f * 128:(f + 1) * 128], hT[:, kc, tsl], kc == 0, kc == 15,
                           WR(ws) + [R("hT", kc, tb)], [PS[b]])
                    return b

                def st(dst, src_buf, reg):
                    fin.append(S.dma("sp", lambda e: e.dma_start(out=dst, in_=src_buf), reads=[reg]))

                pend = []

                def flush():
                    while pend:
                        pend.pop(0)()

                def qk_epilogue(b, g_s_, g_r, tb, dst):
                    tsl = slice(tb * 512, tb * 512 + 512)
                    raw, rawr = nxt("af")
                    kg, kgr = nxt("af")
                    sq, sqr = nxt("af")
                    rs, rsr = nxt("af")
                    t1, t1r = nxt("af")
                    t2, t2r = nxt("af")
                    S.op("act", lambda e: e.activation(out=raw[:], in_=ps[b][:], func=AF.Identity), reads=[PS[b]], writes=[rawr])
                    S.op("dve", lambda e: e.tensor_scalar(kg[:], raw[:], g_s_[:, 0:1], None, ALU.mult), reads=[rawr, g_r], writes=[kgr])
                    S.op("pool", lambda e: e.tensor_tensor(sq[:], raw[:], raw[:], ALU.mult), reads=[rawr], writes=[sqr])

                    def part2():
                        mm(ps[6][:], blk_s[:], sq[:], True, True, [R("blk"), sqr], [PS[6]])
                        mm(ps[5][:], rmat_s[:], kg[:], True, True, [R("rmat"), kgr], [PS[5]])
                        S.op("act", lambda e: e.activation(out=rs[:], in_=ps[6][:], func=AF.Sqrt, scale=1.0 / 64.0, bias=eps_s[:, 0:1]),
                             reads=[PS[6], R("eps")], writes=[rsr])
                        S.op("dve", lambda e: e.reciprocal(rs[:], rs[:]), reads=[rsr], writes=[rsr])
                        S.op("dve", lambda e: e.tensor_tensor(t1[:], kg[:], cos_s[:, tsl], ALU.mult), reads=[kgr, R("cos")], writes=[t1r])
                        S.op("dve", lambda e: e.tensor_tensor(t2[:], ps[5][:], sin_s[:, tsl], ALU.mult), reads=[PS[5], R("sin")], writes=[t2r])
                        S.op("pool", lambda e: e.tensor_tensor(t1[:], t1[:], t2[:], ALU.add), reads=[t1r, t2r], writes=[t1r])
                        ob, obr = nxt("ob")
                        S.op("dve", lambda e: e.tensor_tensor(ob[:], t1[:], rs[:], ALU.mult), reads=[t1r, rsr], writes=[obr])
                        st(dst, ob[:], obr)
                    pend.append(part2)

                for blk in range(2):
                    ws = wsA.get()
                    if DEBUG and blk == 0:
                        fin.append(S.dma("sp", lambda e, ws=ws: e.dma_start(out=dbg_w, in_=wbuf[ws][:].rearrange("p a b -> p (a b)")), reads=WR(ws)))
                    for f in range(4):
                        hh = blk * 4 + f
                        for tb in range(2):
                            b = proj_fm(ws, f, tb)
                            flush()
                            qk_epilogue(b, gk_s, R("gk"), tb, kT_o[hh * 128:(hh + 1) * 128, tb * 512:(tb + 1) * 512])
                for blk in range(2):
                    ws = wsA.get()
                    for tt in range(8):
                        b = nps()
                        for kc in range(16):
                            mm(ps[b][:], hT[:, kc, tt * 128:(tt + 1) * 128], wbuf[ws][:, kc, :], kc == 0, kc == 15,
                               WR(ws) + [R("hT", kc, tt // 4)], [PS[b]])
                        flush()
                        ob, obr = nxt("ob")
                        S.op("act", lambda e, ob=ob, b=b: e.activation(out=ob[:], in_=ps[b][:], func=AF.Identity), reads=[PS[b]], writes=[obr])
                        st(v_o[tt * 128:(tt + 1) * 128, blk * 512:(blk + 1) * 512], ob[:], obr)
                for blk in range(2):
                    ws = wsA.get()
                    for f in range(4):
                        for tb in range(2):
                            b = proj_fm(ws, f, tb)
                            S.op("act", lambda e, b=b, f=f, tb=tb: e.activation(out=pair[:, f, tb * 512:(tb + 1) * 512], in_=ps[b][:], func=AF.Identity),
                                 reads=[PS[b]], writes=[R("pair", f, tb)])
                    ws = wsA.get()
                    for f in range(4):
                        c = blk * 4 + f
                        for tb in range(2):
                            b = proj_fm(ws, f, tb)
                            of, ofr = nxt("of")
                            S.op("dve", lambda e, of=of, b=b, f=f, tb=tb: e.tensor_tensor(of[:], ps[b][:], pair[:, f, tb * 512:(tb + 1) * 512], ALU.mult),
                                 reads=[PS[b], R("pair", f, tb)], writes=[ofr])
                            st(cv_o[c * 128:(c + 1) * 128, tb * 512:(tb + 1) * 512], of[:], ofr)
                for blk in range(2):
                    ws = wsA.get()
                    for f in range(4):
                        hh = blk * 4 + f
                        for tb in range(2):
                            b = proj_fm(ws, f, tb)
                            flush()
                            qk_epilogue(b, gq_s, R("gq"), tb, qT_o[hh * 128:(hh + 1) * 128, tb * 512:(tb + 1) * 512])
                flush()
                for blk in range(2):
                    ws = wsA.get()
                    for f in range(4):
                        for tb in range(2):
                            b = proj_fm(ws, f, tb)
                            S.op("act", lambda e, b=b, f=f, tb=tb: e.activation(out=pair[:, f, tb * 512:(tb + 1) * 512], in_=ps[b][:], func=AF.Silu),
                                 reads=[PS[b]], writes=[R("pair", f, tb)])
                    ws = wsA.get()
                    for f in range(4):
                        c = blk * 4 + f
                        for tb in range(2):
                            b = proj_fm(ws, f, tb)
                            ob, obr = nxt("ob")
                            S.op("dve", lambda e, ob=ob, b=b, f=f, tb=tb: e.tensor_tensor(ob[:], ps[b][:], pair[:, f, tb * 512:(tb + 1) * 512], ALU.mult),
                                 reads=[PS[b], R("pair", f, tb)], writes=[obr])
                            st(bz_o[c * 128:(c + 1) * 128, tb * 512:(tb + 1) * 512], ob[:], obr)
                for (col, nblk, func, dst) in ((C_ZB, 2, AF.Silu, szb_o), (C_GA, 4, AF.Sigmoid, sga_o), (C_GB, 4, AF.Sigmoid, sgb_o)):
                    for blk in range(nblk):
                        ws = wsA.get()
                        for f in range(4):
                            c = blk * 4 + f
                            for tb in range(2):
                                b = proj_fm(ws, f, tb)
                                ob, obr = nxt("ob")
                                S.op("act", lambda e, ob=ob, b=b, func=func: e.activation(out=ob[:], in_=ps[b][:], func=func), reads=[PS[b]], writes=[obr])
                                st(dst[c * 128:(c + 1) * 128, tb * 512:(tb + 1) * 512], ob[:], obr)
                S.barrier()

        eps_s = sb("eps_s", [128, 1], F32)
        S.op("pool", lambda e: e.memset(eps_s[:], EPS), writes=[R("eps")])

        if has_B:
            phase_B()
        if has_A:
            phase_A()
        else:
            for kc in range(16):
                fin.append(S.dma("sp", lambda e, kc=kc: e.dma_start(out=xT_o[kc * 128:(kc + 1) * 128, :], in_=xT[:, kc, :]),
                                 reads=[R("xT", kc, 0), R("xT", kc, 1)]))
        S.emit(final_wait_recs=fin)
    return nc


_PROGS = {}


def _prog(has_B, has_A, first, mod_full=True, mod_next=False):
    key = (has_B, has_A, first, mod_full, mod_next)
    if key not in _PROGS:
        _PROGS[key] = build_program(has_B, has_A, first, mod_full, mod_next)
    return _PROGS[key]


def _consts():
    bf = ml_dtypes.bfloat16
    half = 32
    inv = (10000.0 ** (-np.arange(half, dtype=np.float64) / half))
    rmat = np.zeros((128, 128), np.float32)
    for m in range(128):
        d = m % 64
        if d < 32:
            rmat[m + 32, m] = -1.0
        else:
            rmat[m - 32, m] = 1.0
    blk = np.zeros((128, 128), np.float32)
    blk[:64, :64] = 1.0
    blk[64:, 64:] = 1.0
    per_core = []
    for i in range(NCORES):
        pos = np.concatenate([np.arange(128) + (8 * j + i) * 128 for j in range(8)]).astype(np.float64)
        ang = inv[np.arange(128) % 32][:, None] * pos[None, :]
        ang = ang.astype(np.float32).astype(np.float64)
        cos = np.cos(ang).astype(np.float32)
        sin = np.sin(ang).astype(np.float32)
        mask = np.ones((128, 8, 128), np.float32)
        for ip in range(8):
            if ip > i:
                mask[:, ip, :] = 0.0
            elif ip == i:
                kc = np.arange(128)[:, None] // 64
                qc = np.arange(128)[None, :] // 64
                mask[:, ip, :] = np.where(kc <= qc, 1.0, 0.0)
        per_core.append({
            "cos": cos, "sin": sin, "rmat": rmat, "ones": np.ones((128, 128), np.float32), "blk64": blk,
            "ident": np.eye(128, dtype=np.float32).astype(bf), "onesb": np.ones((128, 128), np.float32).astype(bf),
            "mask": mask.astype(bf),
        })
    return per_core


def _pc(v):
    return np.ascontiguousarray(v.reshape(-1, 128).T)


def kernel(x, c, ada_w, ada_b, norm_g, w_in, conv_w, w_conv_out, q_norm_g, k_norm_g,
           lam_q1, lam_k1, lam_q2, lam_k2, subln_g, w_attn_out, w_o):
    f = lambda a: np.ascontiguousarray(np.asarray(a, dtype=np.float32))
    x, c, ada_w, ada_b, norm_g, w_in, conv_w, w_conv_out = map(f, (x, c, ada_w, ada_b, norm_g, w_in, conv_w, w_conv_out))
    q_norm_g, k_norm_g, lam_q1, lam_k1, lam_q2, lam_k2, subln_g, w_attn_out, w_o = map(
        f, (q_norm_g, k_norm_g, lam_q1, lam_k1, lam_q2, lam_k2, subln_g, w_attn_out, w_o))
    consts = _consts()
    xt = x[0].reshape(8, 8, 128, D)
    xT = [np.ascontiguousarray(xt[:, i].reshape(T, D).T) for i in range(NCORES)]
    state = None
    for k in range(DEPTH + 1):
        has_B = k > 0
        has_A = k < DEPTH
        mod_full = (k == 0)
        mod_next = has_A and (k + 1 < DEPTH)
        nc = _prog(has_B, has_A, k == 0, mod_full, mod_next)
        in_maps = []
        for i in range(NCORES):
            m = dict(consts[i])
            m["xT_in"] = xT[i]
            if has_A:
                la = k
                m.update({
                    "c_vec": _pc(c[0]), "ada_b": _pc(ada_b[la]), "norm_g": _pc(norm_g[la]),
                    "w_in": w_in[la],
                    "gq128": np.ascontiguousarray(np.tile(q_norm_g[la], 2)[:, None]),
                    "gk128": np.ascontiguousarray(np.tile(k_norm_g[la], 2)[:, None]),
                })
            if has_A and mod_full:
                m["ada_w"] = ada_w[k]
            if has_A and not mod_full:
                m["mod_raw"] = mod_raw
            if mod_next:
                m["ada_w_next"] = np.ascontiguousarray(ada_w[k + 1][:, i * 768:(i + 1) * 768])
            if has_B:
                lb = k - 1
                lam_init = 0.8 - 0.6 * math.exp(-0.3 * lb)
                lconst = np.empty((128, 2), np.float32)
                lconst[:, 0] = lam_init
                lconst[:, 1] = 1.0 - lam_init
                m.update({
                    "qT_i": state[i]["qT_o"], "bz_i": state[i]["bz_o"], "szb_i": state[i]["szb_o"],
                    "sga_i": state[i]["sga_o"], "sgb_i": state[i]["sgb_o"], "gate_i": state[i]["gate_o"],
                    "cvh_i": state[i]["cvh"], "K_all": state["K_all"], "V_all": state["V_all"],
                    "conv_w": np.ascontiguousarray(conv_w[lb].reshape(3, 8, 128).transpose(2, 0, 1).reshape(128, 24)),
                    "w_conv_out": w_conv_out[lb], "w_attn_out": w_attn_out[lb], "w_o": w_o[lb],
                    "lamp": np.stack([lam_q1[lb], lam_k1[lb], lam_q2[lb], lam_k2[lb]]),
                    "lconst": lconst, "subg": np.ascontiguousarray(subln_g[lb][:, None]),
                })
            in_maps.append(m)
        res = run_bass_kernel_spmd(nc, in_maps, core_ids=list(range(NCORES)))
        outs = res.results
        xT = [np.asarray(outs[i]["xT_o"]) for i in range(NCORES)]
        if mod_next:
            mod_raw = np.ascontiguousarray(np.concatenate([np.asarray(outs[i]["mod_slice_o"]) for i in range(NCORES)], axis=1))
        if has_A:
            kk = np.stack([np.asarray(outs[i]["kT_o"]) for i in range(NCORES)])
            kk = kk.reshape(8, 8, 128, 8, 128).transpose(1, 2, 3, 0, 4)
            K_all = np.ascontiguousarray(kk.reshape(8, 128, SEQ))
            vv = np.stack([np.asarray(outs[i]["v_o"]) for i in range(NCORES)])
            vv = vv.reshape(8, 8, 128, 8, 128).transpose(3, 2, 1, 0, 4)
            V_all = np.ascontiguousarray(vv.reshape(8, 128, 64 * 128))
            cvs = [np.asarray(outs[i]["cv_o"]).reshape(1024, 8, 128) for i in range(NCORES)]
            state = {"K_all": K_all, "V_all": V_all}
            for i in range(NCORES):
                cvh = np.zeros((1024, 8, 130), np.float32)
                cvh[:, :, 2:] = cvs[i]
                for j in range(8):
                    g = 8 * j + i
                    if g > 0:
                        cvh[:, j, 0:2] = cvs[(g - 1) % 8][:, (g - 1) // 8, 126:128]
                st = {kname: np.asarray(outs[i][kname]) for kname in ("qT_o", "bz_o", "szb_o", "sga_o", "sgb_o", "gate_o")}
                st["cvh"] = cvh.reshape(1024, 8 * 130)
                state[i] = st
    out = np.empty((8, 8, 128, D), np.float32)
    for i in range(NCORES):
        out[:, i] = xT[i].T.reshape(8, 128, D)
    return out.reshape(1, SEQ, D)
```

```python
import math
import contextlib
import numpy as np
import ml_dtypes
import concourse.bass as bass
import concourse.mybir as mybir
from concourse.bass_utils import run_bass_kernel_spmd

F32 = mybir.dt.float32
BF16 = mybir.dt.bfloat16
AF = mybir.ActivationFunctionType
ALU = mybir.AluOpType
AX = mybir.AxisListType

NCORES = 8
D = 2048
SEQ = 8192
T = 1024
DEPTH = 4
DIN = 12288
EPS = 1e-6
C_U, C_BG, C_CG, C_ZA, C_Q, C_K, C_V, C_ZB, C_GA, C_GB = 0, 1024, 2048, 3072, 4096, 5120, 6144, 7168, 8192, 10240
NEG = -30000.0

ENGS = ("pe", "act", "dve", "pool", "sp")
SEM_LIMIT = 30000


class Region:
    __slots__ = ("w", "rc", "rd")

    def __init__(self):
        self.w = None
        self.rc = {}
        self.rd = []


class Rec:
    __slots__ = ("eng", "fn", "deps", "needs_inc", "is_dma", "dma_slot", "dma_val", "semref")

    def __init__(self, eng, fn, is_dma):
        self.eng = eng
        self.fn = fn
        self.deps = []
        self.needs_inc = False
        self.is_dma = is_dma
        self.dma_slot = None
        self.dma_val = None
        self.semref = None


class Sched:
    def __init__(self, nc, n_dma_sems=32, same_engine_sync=True):
        self.nc = nc
        self.q = {e: [] for e in ENGS}
        self.n_dma_sems = n_dma_sems
        self.n_sw = 8
        self.n_hw = n_dma_sems - self.n_sw
        self.dma_count = 0
        self.dma_cnt = {"sw": 0, "hw": 0}
        self.dma_slot_last = [None] * n_dma_sems
        self.same_engine_sync = same_engine_sync
        self.regions = {}
        self.pending = {}

    def R(self, *key):
        r = self.regions.get(key)
        if r is None:
            r = Region()
            self.regions[key] = r
        return r

    def barrier(self):
        deps = [self.q[e][-1] for e in ENGS if self.q[e]]
        deps += [d for d in self.dma_slot_last if d is not None]
        for e in ENGS:
            self.pending[e] = list(deps)

    def _add(self, eng, fn, reads, writes, is_dma):
        rec = Rec(eng, fn, is_dma)
        deps = []
        for r in reads:
            if r.w is not None:
                deps.append(r.w)
        for w in writes:
            if w.w is not None:
                deps.append(w.w)
            deps.extend(w.rc.values())
            deps.extend(w.rd)
        if eng in self.pending:
            deps.extend(self.pending.pop(eng))
        if is_dma:
            if eng == "pool":
                c = self.dma_cnt["sw"]
                self.dma_cnt["sw"] += 1
                slot = c % self.n_sw
                rec.dma_val = 16 * (c // self.n_sw + 1)
            else:
                c = self.dma_cnt["hw"]
                self.dma_cnt["hw"] += 1
                slot = self.n_sw + c % self.n_hw
                rec.dma_val = 16 * (c // self.n_hw + 1)
            rec.dma_slot = slot
            prev = self.dma_slot_last[slot]
            if prev is not None:
                deps.append(prev)
            self.dma_slot_last[slot] = rec
            self.dma_count += 1
        seen = set()
        for d in deps:
            if d is rec or id(d) in seen:
                continue
            seen.add(id(d))
            if (not d.is_dma) and d.eng == eng and not is_dma:
                if eng == "pe" or not self.same_engine_sync:
                    continue
            rec.deps.append(d)
            if not d.is_dma:
                d.needs_inc = True
        self.q[eng].append(rec)
        for r in reads:
            if is_dma:
                r.rd.append(rec)
            else:
                r.rc[eng] = rec
        for w in writes:
            w.w = rec
            w.rc = {}
            w.rd = []
        return rec

    def op(self, eng, fn, reads=(), writes=()):
        return self._add(eng, fn, reads, writes, False)

    def dma(self, eng, fn, reads=(), writes=()):
        return self._add(eng, fn, reads, writes, True)

    def emit(self, final_wait_recs=()):
        nc = self.nc
        with contextlib.ExitStack() as es:
            for e in ENGS:
                cnt = 0
                sems = []
                for rec in self.q[e]:
                    if rec.is_dma or not rec.needs_inc:
                        continue
                    si = cnt // SEM_LIMIT
                    while len(sems) <= si:
                        sems.append(es.enter_context(nc.semaphore(f"s_{e}_{len(sems)}")))
                    rec.semref = (sems[si], cnt % SEM_LIMIT + 1)
                    cnt += 1
            dma_sems = [es.enter_context(nc.semaphore(f"s_dma_{i}")) for i in range(self.n_dma_sems)]
            for e in ENGS:
                for rec in self.q[e]:
                    if rec.is_dma:
                        rec.semref = (dma_sems[rec.dma_slot], rec.dma_val)
            block = es.enter_context(nc.Block())
            engmap = {"pe": block.tensor, "act": block.scalar, "dve": block.vector,
                      "pool": block.gpsimd, "sp": block.sync}

            def make(e):
                def body(engine):
                    known = {}
                    for rec in self.q[e]:
                        for d in rec.deps:
                            sem, val = d.semref
                            k = id(sem)
                            if known.get(k, 0) >= val:
                                continue
                            known[k] = val
                            engine.wait_ge(sem, val)
                        ins = rec.fn(engine)
                        if rec.is_dma:
                            ins.then_inc(rec.semref[0], 16)
                        elif rec.needs_inc:
                            ins.then_inc(rec.semref[0], 1)
                    if e == "sp":
                        for d in final_wait_recs:
                            sem, val = d.semref
                            engine.wait_ge(sem, val)
                return body

            for e in ENGS:
                engmap[e](make(e))


DEBUG = False


def build_program(has_B, has_A, first, mod_full=True, mod_next=False):
    nc = bass.Bass("TRN2", target_bir_lowering=False)
    S = Sched(nc)
    R = S.R

    def din(name, shape, dt=F32):
        return nc.dram_tensor(name, list(shape), dt, kind="ExternalInput").ap()

    def dout(name, shape, dt=F32):
        return nc.dram_tensor(name, list(shape), dt, kind="ExternalOutput").ap()

    cos_d = din("cos", [128, T])
    sin_d = din("sin", [128, T])
    rmat_d = din("rmat", [128, 128])
    ones_d = din("ones", [128, 128])
    blk_d = din("blk64", [128, 128])
    ident_d = din("ident", [128, 128], BF16)
    onesb_d = din("onesb", [128, 128], BF16)
    mask_d = din("mask", [128, 8, 128], BF16)
    xT_in = din("xT_in", [D, T])

    if has_A:
        c_d = din("c_vec", [128, 16])
        if mod_full:
            adaw_d = din("ada_w", [D, 3 * D])
        else:
            modraw_d = din("mod_raw", [128, 48])
        if mod_next:
            adawn_d = din("ada_w_next", [D, 768])
            modsl_o = dout("mod_slice_o", [128, 6])
        adab_d = din("ada_b", [128, 48])
        ng_d = din("norm_g", [128, 16])
        win_d = din("w_in", [D, DIN])
        gq_d = din("gq128", [128, 1])
        gk_d = din("gk128", [128, 1])
        kT_o = dout("kT_o", [1024, T], BF16)
        v_o = dout("v_o", [T, 1024], BF16)
        cv_o = dout("cv_o", [1024, T])
        qT_o = dout("qT_o", [1024, T], BF16)
        bz_o = dout("bz_o", [1024, T], BF16)
        szb_o = dout("szb_o", [1024, T], BF16)
        sga_o = dout("sga_o", [D, T], BF16)
        sgb_o = dout("sgb_o", [D, T], BF16)
        gate_o = dout("gate_o", [128, 16])
        if DEBUG:
            dbg_hT = dout("dbg_hT", [D, T], BF16)
            dbg_w = dout("dbg_w", [128, 16 * 512], BF16)
    if has_B:
        qT_i = din("qT_i", [1024, T], BF16)
        bz_i = din("bz_i", [1024, T], BF16)
        szb_i = din("szb_i", [1024, T], BF16)
        sga_i = din("sga_i", [D, T], BF16)
        sgb_i = din("sgb_i", [D, T], BF16)
        gate_i = din("gate_i", [128, 16])
        cvh_i = din("cvh_i", [1024, 8 * 130])
        K_i = din("K_all", [8, 128, SEQ], BF16)
        V_i = din("V_all", [8, 128, 64 * 128], BF16)
        cw_d = din("conv_w", [128, 24])
        wco_d = din("w_conv_out", [1024, D])
        wao_d = din("w_attn_out", [1024, D])
        wo_d = din("w_o", [D, D])
        lamp_d = din("lamp", [4, 64])
        lconst_d = din("lconst", [128, 2])
        subg_d = din("subg", [128, 1])
    xT_o = dout("xT_o", [D, T])

    fin = []
    with contextlib.ExitStack() as es:
        def sb(name, shape, dt, stack=es):
            return stack.enter_context(nc.sbuf_tensor(name, list(shape), dt))

        ps = [es.enter_context(nc.psum_tensor(f"ps{i}", [128, 512], F32)) for i in range(8)]
        PS = [R("ps", i) for i in range(8)]

        xT = sb("xT", [128, 16, T], F32)
        rmat_s = sb("rmat_s", [128, 128], F32)
        ones_s = sb("ones_s", [128, 128], F32)
        blk_s = sb("blk_s", [128, 128], F32)
        ident_s = sb("ident_s", [128, 128], BF16)
        onesb_s = sb("onesb_s", [128, 128], BF16)
        mask_s = sb("mask_s", [128, 8, 128], BF16)
        wbuf = [None, None]

        def ld(eng, dst, src, reg):
            return S.dma(eng, lambda e: e.dma_start(out=dst, in_=src), writes=[reg])

        ld("sp", rmat_s[:], rmat_d, R("rmat"))
        ld("sp", ones_s[:], ones_d, R("ones"))
        ld("sp", blk_s[:], blk_d, R("blk"))
        ld("sp", ident_s[:], ident_d, R("ident"))
        ld("sp", onesb_s[:], onesb_d, R("onesb"))
        ld("sp", mask_s[:], mask_d, R("mask"))
        for kc in range(16):
            S.dma("sp", lambda e, kc=kc: e.dma_start(out=xT[:, kc, :], in_=xT_in[kc * 128:(kc + 1) * 128, :]),
                  writes=[R("xT", kc, 0), R("xT", kc, 1)])

        class WStream:
            def __init__(self, blocks):
                self.blocks = blocks
                self.i = 0
                self._issue(0)

            def _issue(self, j):
                if j >= len(self.blocks):
                    return
                s = j % 2
                for (w_ap, nk, col0, ncols, kc0) in self.blocks[j]:
                    src = w_ap[:, col0:col0 + ncols].rearrange("(kc p) n -> p kc n", p=128)
                    dst = wbuf[s][:, kc0:kc0 + nk, 0:ncols]
                    regs = [R("w", s, 0), R("w", s, 1)] if nk == 16 else [R("w", s, kc0 // 8)]
                    S.dma("pool", lambda e, dst=dst, src=src: e.dma_start(out=dst, in_=src), writes=regs)

            def get(self):
                j = self.i
                self.i += 1
                self._issue(j + 1)
                return j % 2

        def WR(s):
            return [R("w", s, 0), R("w", s, 1)]

        def mm(out, lhsT, rhs, start, stop, reads, writes):
            S.op("pe", lambda e: e.matmul(out, lhsT, rhs, start=start, stop=stop), reads=reads, writes=writes)

        rot = {}

        def rotbuf(stack, name, n, shape, dt):
            bufs = [sb(f"{name}{i}", shape, dt, stack) for i in range(n)]
            rot[name] = [0, n, bufs]

        def nxt(name):
            st = rot[name]
            i = st[0] % st[1]
            st[0] += 1
            return st[2][i], R(name, i)

        psr = {"n": 0}

        def nps(lo=0, n=4):
            i = lo + psr["n"] % n
            psr["n"] += 1
            return i

        def phase_B():
            with contextlib.ExitStack() as bs:
                gate_s = sb("gate_s", [128, 16], F32, bs)
                cw_s = sb("cw_s", [128, 24], F32, bs)
                lamp_s = sb("lamp_s", [128, 4, 64], F32, bs)
                lconst_s = sb("lconst_s", [128, 2], F32, bs)
                subg_s = sb("subg_s", [128, 1], F32, bs)
                lam_t = sb("lam_t", [128, 2, 64], F32, bs)
                lam_r = sb("lam_r", [128, 4], F32, bs)
                neglam = sb("neglam", [128, 1], F32, bs)
                g_s = sb("g_s", [128, 8, T], BF16, bs)
                onT = sb("onT", [128, 8, T], BF16, bs)
                ld("sp", gate_s[:], gate_i, R("gate_s"))
                ld("sp", cw_s[:], cw_d, R("cw"))
                ld("sp", lconst_s[:], lconst_d, R("lconst"))
                ld("sp", subg_s[:], subg_d, R("subg"))
                lamp_b = bass.AP(lamp_d.tensor, 0, [[0, 128], [1, 256]])
                ld("sp", lamp_s[:].rearrange("p a b -> p (a b)"), lamp_b, R("lamp"))
                S.op("dve", lambda e: e.tensor_tensor(lam_t[:, 0, :], lamp_s[:, 0, :], lamp_s[:, 1, :], ALU.mult), reads=[R("lamp")], writes=[R("lamt0")])
                S.op("dve", lambda e: e.tensor_tensor(lam_t[:, 1, :], lamp_s[:, 2, :], lamp_s[:, 3, :], ALU.mult), reads=[R("lamp")], writes=[R("lamt1")])
                S.op("dve", lambda e: e.reduce_sum(lam_r[:, 0:1], lam_t[:, 0, :], axis=AX.X), reads=[R("lamt0")], writes=[R("lamr0")])
                S.op("dve", lambda e: e.reduce_sum(lam_r[:, 1:2], lam_t[:, 1, :], axis=AX.X), reads=[R("lamt1")], writes=[R("lamr1")])
                S.op("act", lambda e: e.activation(out=lam_r[:, 2:4], in_=lam_r[:, 0:2], func=AF.Exp), reads=[R("lamr0"), R("lamr1")], writes=[R("lamr2")])
                S.op("dve", lambda e: e.tensor_tensor(neglam[:], lam_r[:, 3:4], lam_r[:, 2:3], ALU.subtract), reads=[R("lamr2")], writes=[R("neglam0")])
                S.op("dve", lambda e: e.tensor_tensor(neglam[:], neglam[:], lconst_s[:, 0:1], ALU.subtract), reads=[R("neglam0"), R("lconst")], writes=[R("neglam")])
                S.op("dve", lambda e: e.tensor_tensor(subg_s[:], subg_s[:], lconst_s[:, 1:2], ALU.mult), reads=[R("subg"), R("lconst")], writes=[R("subg")])

                with contextlib.ExitStack() as cs:
                    cvh = [sb(f"cvh{i}", [128, 8, 130], F32, cs) for i in range(2)]
                    bzs = [sb(f"bzs{i}", [128, T], BF16, cs) for i in range(2)]
                    ytmp = [sb(f"ytmp{i}", [128, 8, 128], F32, cs) for i in range(2)]
                    for c in range(8):
                        b = c % 2
                        ld("sp", cvh[b][:].rearrange("p a b -> p (a b)"), cvh_i[c * 128:(c + 1) * 128, :], R("cvh", b))
                        ld("sp", bzs[b][:], bz_i[c * 128:(c + 1) * 128, :], R("bzs", b))
                        y = ytmp[b]
                        cv = cvh[b]
                        S.op("dve", lambda e, y=y, cv=cv, c=c: e.tensor_scalar(y[:], cv[:, :, 2:130], cw_s[:, 16 + c:17 + c], None, ALU.mult),
                             reads=[R("cvh", b), R("cw")], writes=[R("ytmp", b)])
                        S.op("dve", lambda e, y=y, cv=cv, c=c: e.scalar_tensor_tensor(y[:], cv[:, :, 1:129], cw_s[:, 8 + c:9 + c], y[:], ALU.mult, ALU.add),
                             reads=[R("cvh", b), R("cw"), R("ytmp", b)], writes=[R("ytmp", b)])
                        S.op("dve", lambda e, y=y, cv=cv, c=c: e.scalar_tensor_tensor(y[:], cv[:, :, 0:128], cw_s[:, c:c + 1], y[:], ALU.mult, ALU.add),
                             reads=[R("cvh", b), R("cw"), R("ytmp", b)], writes=[R("ytmp", b)])
                        bz = bzs[b]
                        S.op("dve", lambda e, y=y, bz=bz, c=c: e.tensor_tensor(g_s[:, c, :], y[:].rearrange("p a b -> p (a b)"), bz[:], ALU.mult),
                             reads=[R("ytmp", b), R("bzs", b)], writes=[R("g", c)])
                S.barrier()

                with contextlib.ExitStack() as as_:
                    Kh = [sb(f"Kh{i}", [128, 32, 128], BF16, as_) for i in range(4)]
                    Vh = [sb(f"Vh{i}", [128, 32, 128], BF16, as_) for i in range(4)]
                    qh_s = [sb(f"qh{i}", [128, T], BF16, as_) for i in range(2)]
                    zb_s = [sb(f"zbh{i}", [128, T], BF16, as_) for i in range(2)]
                    rotbuf(as_, "pt", 4, [128, 512], BF16)
                    rotbuf(as_, "ef", 12, [128, 512], F32)
                    hslot = {"n": 0}

                    def load_half(h, hf):
                        s = hslot["n"] % 4
                        hslot["n"] += 1
                        ld("sp", Kh[s][:].rearrange("p a b -> p (a b)"), K_i[h, :, hf * 4096:(hf + 1) * 4096], R("Kh", s))
                        ld("sp", Vh[s][:].rearrange("p a b -> p (a b)"), V_i[h, :, hf * 4096:(hf + 1) * 4096], R("Vh", s))
                        return s

                    def load_head(h):
                        b = h % 2
                        ld("sp", qh_s[b][:], qT_i[h * 128:(h + 1) * 128, :], R("qh", b))
                        ld("sp", zb_s[b][:], szb_i[h * 128:(h + 1) * 128, :], R("zbh", b))
                        return (load_half(h, 0), load_half(h, 1))

                    slots_next = load_head(0)
                    deferred = []

                    def rec_S(h, hb, slots, qh, kt):
                        j0 = max(kt // 8, 4 * qh)
                        c0 = j0 * 128
                        c1 = (4 * qh + 4) * 128
                        n = c1 - c0
                        s = slots[kt // 32]
                        ktl = kt % 32
                        diag = (kt // 8 == j0)
                        for comp in range(2):
                            bnk = (kt % 2) * 2 + comp
                            lo, hi = comp * 64, comp * 64 + 64
                            mm(ps[bnk][:, 0:n], Kh[s][lo:hi, ktl, :], qh_s[hb][lo:hi, c0:c1], True, True,
                               [R("Kh", s), R("qh", hb)], [PS[bnk]])

                    def rec_PV(h, hb, slots, qh, kt, nkt):
                        j0 = max(kt // 8, 4 * qh)
                        c0 = j0 * 128
                        n = (4 * qh + 4) * 128 - c0
                        off = c0 - 4 * qh * 128
                        s = slots[kt // 32]
                        ktl = kt % 32
                        for comp in range(2):
                            bnk = (kt % 2) * 2 + comp
                            pt, ptr = nxt("pt")
                            S.op("act", lambda e, pt=pt, bnk=bnk, n=n: e.activation(out=pt[:, 0:n], in_=ps[bnk][:, 0:n], func=AF.Exp, scale=0.125),
                                 reads=[PS[bnk]], writes=[ptr])
                            if kt // 8 == j0:
                                S.op("pool", lambda e, pt=pt, kt=kt: e.tensor_tensor(pt[:, 0:128], pt[:, 0:128], mask_s[:, kt % 8, :], ALU.mult),
                                     reads=[ptr, R("mask")], writes=[ptr])
                            mm(ps[4 + comp][:, off:off + n], Vh[s][:, ktl, :], pt[:, 0:n], kt == 0, kt == nkt - 1,
                               [R("Vh", s), ptr], [PS[4 + comp]])
                            mm(ps[6 + comp][:, off:off + n], onesb_s[:], pt[:, 0:n], kt == 0, kt == nkt - 1,
                               [R("onesb"), ptr], [PS[6 + comp]])

                    def epilogue(h, hb, qh):
                        cs0 = qh * 512
                        r0, r0r = nxt("ef")
                        r1, r1r = nxt("ef")
                        o0, o0r = nxt("ef")
                        o1, o1r = nxt("ef")
                        sq, sqr = nxt("ef")
                        rs, rsr = nxt("ef")
                        S.op("act", lambda e: e.activation(out=r0[:], in_=ps[6][:], func=AF.Ln), reads=[PS[6]], writes=[r0r])
                        S.op("act", lambda e: e.activation(out=r1[:], in_=ps[7][:], func=AF.Ln), reads=[PS[7]], writes=[r1r])
                        S.op("act", lambda e: e.activation(out=r0[:], in_=r0[:], func=AF.Exp, scale=-1.0), reads=[r0r], writes=[r0r])
                        S.op("act", lambda e: e.activation(out=r1[:], in_=r1[:], func=AF.Exp, scale=-1.0), reads=[r1r], writes=[r1r])
                        S.op("dve", lambda e: e.tensor_tensor(o0[:], ps[4][:], r0[:], ALU.mult), reads=[PS[4], r0r], writes=[o0r])
                        S.op("dve", lambda e: e.tensor_tensor(o1[:], ps[5][:], r1[:], ALU.mult), reads=[PS[5], r1r], writes=[o1r])

                        def part2():
                            S.op("dve", lambda e: e.scalar_tensor_tensor(o0[:], o1[:], neglam[:, 0:1], o0[:], ALU.mult, ALU.add),
                                 reads=[o0r, o1r, R("neglam")], writes=[o0r])
                            S.op("pool", lambda e: e.tensor_tensor(sq[:], o0[:], o0[:], ALU.mult), reads=[o0r], writes=[sqr])
                            mm(ps[0][:], ones_s[:], sq[:], True, True, [R("ones"), sqr], [PS[0]])
                            S.op("act", lambda e: e.activation(out=rs[:], in_=ps[0][:], func=AF.Ln, scale=1.0 / 128.0, bias=eps_s[:, 0:1]),
                                 reads=[PS[0], R("eps")], writes=[rsr])
                            S.op("act", lambda e: e.activation(out=rs[:], in_=rs[:], func=AF.Exp, scale=-0.5), reads=[rsr], writes=[rsr])
                            S.op("dve", lambda e: e.tensor_tensor(o0[:], o0[:], rs[:], ALU.mult), reads=[o0r, rsr], writes=[o0r])
                            S.op("dve", lambda e: e.scalar_tensor_tensor(
                                onT[:, h, cs0:cs0 + 512], o0[:], subg_s[:, 0:1], zb_s[hb][:, cs0:cs0 + 512], ALU.mult, ALU.mult),
                                reads=[o0r, R("subg"), R("zbh", hb)], writes=[R("onT", h, qh)])
                        deferred.append(part2)

                    for h in range(8):
                        slots = slots_next
                        hb = h % 2
                        for qh in range(2):
                            nkt = 32 * qh + 32
                            rec_S(h, hb, slots, qh, 0)
                            for kt in range(nkt):
                                if kt + 1 < nkt:
                                    rec_S(h, hb, slots, qh, kt + 1)
                                rec_PV(h, hb, slots, qh, kt, nkt)
                                if kt == 0 and deferred:
                                    deferred.pop(0)()
                            epilogue(h, hb, qh)
                            if qh == 0 and h + 1 < 8:
                                slots_next = load_head(h + 1)
                    while deferred:
                        deferred.pop(0)()
                S.barrier()

                with contextlib.ExitStack() as ms:
                    merged = sb("merged", [128, 16, T], BF16, ms)
                    wbuf[0] = sb("wbufB0", [128, 16, 512], BF16, ms)
                    wbuf[1] = sb("wbufB1", [128, 16, 512], BF16, ms)
                    sga_s = [sb(f"sga{i}", [128, T], BF16, ms) for i in range(2)]
                    sgb_s = [sb(f"sgb{i}", [128, T], BF16, ms) for i in range(2)]
                    rotbuf(ms, "mf", 4, [128, 512], F32)
                    wsB = WStream([[(wco_d, 8, blk * 512, 512, 0), (wao_d, 8, blk * 512, 512, 8)] for blk in range(4)]
                                  + [[(wo_d, 16, blk * 512, 512, 0)] for blk in range(4)])
                    for blk in range(4):
                        sc = wsB.get()
                        sa = sc
                        for f in range(4):
                            dc = blk * 4 + f
                            gb_ = dc % 2
                            ld("sp", sga_s[gb_][:], sga_i[dc * 128:(dc + 1) * 128, :], R("sga", gb_))
                            ld("sp", sgb_s[gb_][:], sgb_i[dc * 128:(dc + 1) * 128, :], R("sgb", gb_))
                            for tb in range(2):
                                tsl = slice(tb * 512, tb * 512 + 512)
                                pc = nps()
                                for c in range(8):
                                    mm(ps[pc][:], wbuf[sc][:, c, f * 128:(f + 1) * 128], g_s[:, c, tsl], c == 0, c == 7,
                                       [R("w", sc, 0), R("g", c)], [PS[pc]])
                                pa = nps()
                                for hh in range(8):
                                    mm(ps[pa][:], wbuf[sa][:, 8 + hh, f * 128:(f + 1) * 128], onT[:, hh, tsl], hh == 0, hh == 7,
                                       [R("w", sa, 1), R("onT", hh, tb)], [PS[pa]])
                                t1, t1r = nxt("mf")
                                t2, t2r = nxt("mf")
                                S.op("dve", lambda e, t1=t1, pc=pc, gb_=gb_, tsl=tsl: e.tensor_tensor(t1[:], ps[pc][:], sga_s[gb_][:, tsl], ALU.mult),
                                     reads=[PS[pc], R("sga", gb_)], writes=[t1r])
                                S.op("dve", lambda e, t2=t2, pa=pa, gb_=gb_, tsl=tsl: e.tensor_tensor(t2[:], ps[pa][:], sgb_s[gb_][:, tsl], ALU.mult),
                                     reads=[PS[pa], R("sgb", gb_)], writes=[t2r])
                                S.op("pool", lambda e, t1=t1, t2=t2, dc=dc, tsl=tsl: e.tensor_tensor(merged[:, dc, tsl], t1[:], t2[:], ALU.add),
                                     reads=[t1r, t2r], writes=[R("merged", dc, tb)])
                    for blk in range(4):
                        so = wsB.get()
                        for f in range(4):
                            dc = blk * 4 + f
                            for tb in range(2):
                                tsl = slice(tb * 512, tb * 512 + 512)
                                po = nps()
                                for kc in range(16):
                                    mm(ps[po][:], wbuf[so][:, kc, f * 128:(f + 1) * 128], merged[:, kc, tsl], kc == 0, kc == 15,
                                       WR(so) + [R("merged", kc, tb)], [PS[po]])
                                S.op("dve", lambda e, po=po, dc=dc, tsl=tsl: e.scalar_tensor_tensor(
                                    xT[:, dc, tsl], ps[po][:], gate_s[:, dc:dc + 1], xT[:, dc, tsl], ALU.mult, ALU.add),
                                    reads=[PS[po], R("gate_s"), R("xT", dc, tb)], writes=[R("xT", dc, tb)])
                S.barrier()

        def phase_A():
            with contextlib.ExitStack() as a_:
                hT = sb("hT", [128, 16, T], BF16, a_)
                c_s = sb("c_s", [128, 16], F32, a_)
                cact = sb("cact", [128, 16], F32, a_)
                adab_s = sb("adab_s", [128, 48], F32, a_)
                ng_s = sb("ng_s", [128, 16], F32, a_)
                mod_s = sb("mod_s", [128, 48], F32, a_)
                modraw_s = sb("modraw_s", [128, 48], F32, a_)
                modsl_s = sb("modsl_s", [128, 6], F32, a_)
                asc = sb("asc", [128, 16], F32, a_)
                gq_s = sb("gq_s", [128, 1], F32, a_)
                gk_s = sb("gk_s", [128, 1], F32, a_)
                cos_s = sb("cos_s", [128, T], F32, a_)
                sin_s = sb("sin_s", [128, T], F32, a_)
                ld("sp", cos_s[:], cos_d, R("cos"))
                ld("sp", sin_s[:], sin_d, R("sin"))
                rotbuf(a_, "af", 12, [128, 512], F32)
                rsn = [sb(f"rsn{i}", [128, 512], F32, a_) for i in range(2)]
                rotbuf(a_, "ob", 4, [128, 512], BF16)
                rotbuf(a_, "of", 2, [128, 512], F32)
                m_ = contextlib.ExitStack()
                adaw = [sb(f"adaw{i}", [128, 16, 256], F32, m_) for i in range(2)]
                rotbuf(m_, "macc_dve", 2, [128, 256], F32)
                rotbuf(m_, "macc_pool", 2, [128, 256], F32)
                ld("sp", c_s[:], c_d, R("c_s"))
                ld("sp", adab_s[:], adab_d, R("adab"))
                ld("sp", ng_s[:], ng_d, R("ng"))
                ld("sp", gq_s[:], gq_d, R("gq"))
                ld("sp", gk_s[:], gk_d, R("gk"))
                for kc in range(16):
                    fin.append(S.dma("sp", lambda e, kc=kc: e.dma_start(out=xT_o[kc * 128:(kc + 1) * 128, :], in_=xT[:, kc, :]),
                                     reads=[R("xT", kc, 0), R("xT", kc, 1)]))
                S.op("act", lambda e: e.activation(out=cact[:], in_=c_s[:], func=AF.Silu), reads=[R("c_s")], writes=[R("cact")])
                def gemv(w_ap, nblk, col0):
                    for blk in range(nblk):
                        b = gemv_n[0] % 2
                        gemv_n[0] += 1
                        src = w_ap[:, blk * 256:(blk + 1) * 256].rearrange("(kc p) n -> p kc n", p=128)
                        S.dma("sp", lambda e, b=b, src=src: e.dma_start(out=adaw[b][:], in_=src), writes=[R("adaw", b)])
                        eng = "dve"
                        acc, _ = nxt("macc_" + eng)
                        ai = rot["macc_" + eng][0]
                        hr = [R("macc", ai % 2, 0), R("macc", ai % 2, 1)]
                        for kc in range(16):
                            for hf in range(2):
                                cs_ = slice(hf * 128, hf * 128 + 128)
                                if kc == 0:
                                    S.op(eng, lambda e, acc=acc, b=b, cs_=cs_: e.tensor_scalar(acc[:, cs_], adaw[b][:, 0, cs_], cact[:, 0:1], 0.0, ALU.mult, ALU.add),
                                         reads=[R("adaw", b), R("cact")], writes=[hr[hf]])
                                else:
                                    S.op(eng, lambda e, acc=acc, b=b, kc=kc, cs_=cs_: e.scalar_tensor_tensor(
                                        acc[:, cs_], adaw[b][:, kc, cs_], cact[:, kc:kc + 1], acc[:, cs_], ALU.mult, ALU.add),
                                        reads=[R("adaw", b), R("cact"), hr[hf]], writes=[hr[hf]])
                        for f in range(2):
                            cc = col0 + blk * 2 + f
                            mm(ps[7][:, cc:cc + 1], acc[:, f * 128:(f + 1) * 128], ones_s[:, 0:1], True, True,
                               [hr[f], R("ones")], [PS[7]])

                gemv_n = [0]
                if mod_full:
                    gemv(adaw_d, 24, 0)
                    S.op("dve", lambda e: e.tensor_tensor(mod_s[:], ps[7][:, 0:48], adab_s[:], ALU.add), reads=[PS[7], R("adab")], writes=[R("mod")])
                else:
                    ld("sp", modraw_s[:], modraw_d, R("modraw"))
                    S.op("dve", lambda e: e.tensor_tensor(mod_s[:], modraw_s[:], adab_s[:], ALU.add), reads=[R("modraw"), R("adab")], writes=[R("mod")])
                if mod_next:
                    gemv(adawn_d, 3, 48)
                    S.op("dve", lambda e: e.tensor_copy(out=modsl_s[:], in_=ps[7][:, 48:54]), reads=[PS[7]], writes=[R("modsl")])
                    fin.append(S.dma("sp", lambda e: e.dma_start(out=modsl_o, in_=modsl_s[:]), reads=[R("modsl")]))
                S.op("dve", lambda e: e.tensor_scalar(asc[:], mod_s[:, 16:32], 1.0, None, ALU.add), reads=[R("mod")], writes=[R("asc0")])
                S.op("dve", lambda e: e.tensor_tensor(asc[:], asc[:], ng_s[:], ALU.mult), reads=[R("asc0"), R("ng")], writes=[R("asc")])
                fin.append(S.dma("sp", lambda e: e.dma_start(out=gate_o, in_=mod_s[:, 32:48]), reads=[R("mod")]))
                S.barrier()
                m_.close()
                pair = sb("pair", [128, 4, T], F32, a_)
                wbuf[0] = sb("wbufA0", [128, 16, 512], BF16, a_)
                wbuf[1] = sb("wbufA1", [128, 16, 512], BF16, a_)
                for tb in range(2):
                    tsl = slice(tb * 512, tb * 512 + 512)
                    for kc in range(16):
                        sq, sqr = nxt("af")
                        S.op("pool", lambda e, sq=sq, kc=kc, tsl=tsl: e.tensor_tensor(sq[:], xT[:, kc, tsl], xT[:, kc, tsl], ALU.mult),
                             reads=[R("xT", kc, tb)], writes=[sqr])
                        mm(ps[4][:], ones_s[:], sq[:], kc == 0, kc == 15, [R("ones"), sqr], [PS[4]])
                    rs, rsr = rsn[tb], R("rsn", tb)
                    S.op("act", lambda e, rs=rs: e.activation(out=rs[:], in_=ps[4][:], func=AF.Ln, scale=1.0 / D, bias=eps_s[:, 0:1]),
                         reads=[PS[4], R("eps")], writes=[rsr])
                    S.op("act", lambda e, rs=rs: e.activation(out=rs[:], in_=rs[:], func=AF.Exp, scale=-0.5), reads=[rsr], writes=[rsr])
                    for kc in range(16):
                        tmp, tmr = nxt("af")
                        S.op("dve", lambda e, tmp=tmp, rs=rs, kc=kc, tsl=tsl: e.scalar_tensor_tensor(
                            tmp[:], xT[:, kc, tsl], asc[:, kc:kc + 1], rs[:], ALU.mult, ALU.mult),
                            reads=[R("xT", kc, tb), R("asc"), rsr], writes=[tmr])
                        S.op("act", lambda e, tmp=tmp, kc=kc, tsl=tsl: e.activation(
                            out=hT[:, kc, tsl], in_=tmp[:], func=AF.Identity, bias=mod_s[:, kc:kc + 1], scale=1.0),
                            reads=[tmr, R("mod")], writes=[R("hT", kc, tb)])

                if DEBUG:
                    for kc in range(16):
                        fin.append(S.dma("sp", lambda e, kc=kc: e.dma_start(out=dbg_hT[kc * 128:(kc + 1) * 128, :], in_=hT[:, kc, :]),
                                         reads=[R("hT", kc, 0), R("hT", kc, 1)]))

                blocksA = []
                for col in (C_K, C_V):
                    blocksA += [[(win_d, 16, col + blk * 512, 512, 0)] for blk in range(2)]
                for blk in range(2):
                    blocksA += [[(win_d, 16, C_U + blk * 512, 512, 0)], [(win_d, 16, C_CG + blk * 512, 512, 0)]]
                blocksA += [[(win_d, 16, C_Q + blk * 512, 512, 0)] for blk in range(2)]
                for blk in range(2):
                    blocksA += [[(win_d, 16, C_ZA + blk * 512, 512, 0)], [(win_d, 16, C_BG + blk * 512, 512, 0)]]
                blocksA += [[(win_d, 16, C_ZB + blk * 512, 512, 0)] for blk in range(2)]
                blocksA += [[(win_d, 16, C_GA + blk * 512, 512, 0)] for blk in range(4)]
                blocksA += [[(win_d, 16, C_GB + blk * 512, 512, 0)] for blk in range(4)]
                wsA = WStream(blocksA)

                def hreads(tb):
                    return [R("hT", kc, tb) for kc in range(16)]

                def proj_fm(ws, f, tb):
                    b = nps()
                    tsl = slice(tb * 512, tb * 512 + 512)
                    for kc in range(16):
                        mm(ps[b][:], wbuf[ws][:, kc, f * 128:(f + 1) * 128], hT[:, kc, tsl], kc == 0, kc == 15,
                           WR(ws) + [R("hT", kc, tb)], [PS[b]])
                    return b

                def st(dst, src_buf, reg):
                    fin.append(S.dma("sp", lambda e: e.dma_start(out=dst, in_=src_buf), reads=[reg]))

                pend = []

                def flush():
                    while pend:
                        pend.pop(0)()

                def qk_epilogue(b, g_s_, g_r, tb, dst):
                    tsl = slice(tb * 512, tb * 512 + 512)
                    raw, rawr = nxt("af")
                    kg, kgr = nxt("af")
                    sq, sqr = nxt("af")
                    rs, rsr = nxt("af")
                    t1, t1r = nxt("af")
                    t2, t2r = nxt("af")
                    S.op("act", lambda e: e.activation(out=raw[:], in_=ps[b][:], func=AF.Identity), reads=[PS[b]], writes=[rawr])
                    S.op("dve", lambda e: e.tensor_scalar(kg[:], raw[:], g_s_[:, 0:1], None, ALU.mult), reads=[rawr, g_r], writes=[kgr])
                    S.op("pool", lambda e: e.tensor_tensor(sq[:], raw[:], raw[:], ALU.mult), reads=[rawr], writes=[sqr])

                    def part2():
                        mm(ps[6][:], blk_s[:], sq[:], True, True, [R("blk"), sqr], [PS[6]])
                        mm(ps[5][:], rmat_s[:], kg[:], True, True, [R("rmat"), kgr], [PS[5]])
                        S.op("act", lambda e: e.activation(out=rs[:], in_=ps[6][:], func=AF.Ln, scale=1.0 / 64.0, bias=eps_s[:, 0:1]),
                             reads=[PS[6], R("eps")], writes=[rsr])
                        S.op("act", lambda e: e.activation(out=rs[:], in_=rs[:], func=AF.Exp, scale=-0.5), reads=[rsr], writes=[rsr])
                        S.op("dve", lambda e: e.tensor_tensor(t1[:], kg[:], cos_s[:, tsl], ALU.mult), reads=[kgr, R("cos")], writes=[t1r])
                        S.op("dve", lambda e: e.tensor_tensor(t2[:], ps[5][:], sin_s[:, tsl], ALU.mult), reads=[PS[5], R("sin")], writes=[t2r])
                        S.op("pool", lambda e: e.tensor_tensor(t1[:], t1[:], t2[:], ALU.add), reads=[t1r, t2r], writes=[t1r])
                        ob, obr = nxt("ob")
                        S.op("dve", lambda e: e.tensor_tensor(ob[:], t1[:], rs[:], ALU.mult), reads=[t1r, rsr], writes=[obr])
                        st(dst, ob[:], obr)
                    pend.append(part2)

                for blk in range(2):
                    ws = wsA.get()
                    if DEBUG and blk == 0:
                        fin.append(S.dma("sp", lambda e, ws=ws: e.dma_start(out=dbg_w, in_=wbuf[ws][:].rearrange("p a b -> p (a b)")), reads=WR(ws)))
                    for f in range(4):
                        hh = blk * 4 + f
                        for tb in range(2):
                            b = proj_fm(ws, f, tb)
                            flush()
                            qk_epilogue(b, gk_s, R("gk"), tb, kT_o[hh * 128:(hh + 1) * 128, tb * 512:(tb + 1) * 512])
                for blk in range(2):
                    ws = wsA.get()
                    for tt in range(8):
                        b = nps()
                        for kc in range(16):
                            mm(ps[b][:], hT[:, kc, tt * 128:(tt + 1) * 128], wbuf[ws][:, kc, :], kc == 0, kc == 15,
                               WR(ws) + [R("hT", kc, tt // 4)], [PS[b]])
                        flush()
                        ob, obr = nxt("ob")
                        S.op("act", lambda e, ob=ob, b=b: e.activation(out=ob[:], in_=ps[b][:], func=AF.Identity), reads=[PS[b]], writes=[obr])
                        st(v_o[tt * 128:(tt + 1) * 128, blk * 512:(blk + 1) * 512], ob[:], obr)
                for blk in range(2):
                    ws = wsA.get()
                    for f in range(4):
                        for tb in range(2):
                            b = proj_fm(ws, f, tb)
                            S.op("act", lambda e, b=b, f=f, tb=tb: e.activation(out=pair[:, f, tb * 512:(tb + 1) * 512], in_=ps[b][:], func=AF.Identity),
                                 reads=[PS[b]], writes=[R("pair", f, tb)])
                    ws = wsA.get()
                    for f in range(4):
                        c = blk * 4 + f
                        for tb in range(2):
                            b = proj_fm(ws, f, tb)
                            of, ofr = nxt("of")
                            S.op("dve", lambda e, of=of, b=b, f=f, tb=tb: e.tensor_tensor(of[:], ps[b][:], pair[:, f, tb * 512:(tb + 1) * 512], ALU.mult),
                                 reads=[PS[b], R("pair", f, tb)], writes=[ofr])
                            st(cv_o[c * 128:(c + 1) * 128, tb * 512:(tb + 1) * 512], of[:], ofr)
                for blk in range(2):
                    ws = wsA.get()
                    for f in range(4):
                        hh = blk * 4 + f
                        for tb in range(2):
                            b = proj_fm(ws, f, tb)
                            flush()
                            qk_epilogue(b, gq_s, R("gq"), tb, qT_o[hh * 128:(hh + 1) * 128, tb * 512:(tb + 1) * 512])
                flush()
                for blk in range(2):
                    ws = wsA.get()
                    for f in range(4):
                        for tb in range(2):
                            b = proj_fm(ws, f, tb)
                            S.op("act", lambda e, b=b, f=f, tb=tb: e.activation(out=pair[:, f, tb * 512:(tb + 1) * 512], in_=ps[b][:], func=AF.Silu),
                                 reads=[PS[b]], writes=[R("pair", f, tb)])
                    ws = wsA.get()
                    for f in range(4):
                        c = blk * 4 + f
                        for tb in range(2):
                            b = proj_fm(ws, f, tb)
                            ob, obr = nxt("ob")
                            S.op("dve", lambda e, ob=ob, b=b, f=f, tb=tb: e.tensor_tensor(ob[:], ps[b][:], pair[:, f, tb * 512:(tb + 1) * 512], ALU.mult),
                                 reads=[PS[b], R("pair", f, tb)], writes=[obr])
                            st(bz_o[c * 128:(c + 1) * 128, tb * 512:(tb + 1) * 512], ob[:], obr)
                for (col, nblk, func, dst) in ((C_ZB, 2, AF.Silu, szb_o), (C_GA, 4, AF.Sigmoid, sga_o), (C_GB, 4, AF.Sigmoid, sgb_o)):
                    for blk in range(nblk):
                        ws = wsA.get()
                        for f in range(4):
                            c = blk * 4 + f
                            for tb in range(2):
                                b = proj_fm(ws, f, tb)
                                ob, obr = nxt("ob")
                                S.op("act", lambda e, ob=ob, b=b, func=func: e.activation(out=ob[:], in_=ps[b][:], func=func), reads=[PS[b]], writes=[obr])
                                st(dst[c * 128:(c + 1) * 128, tb * 512:(tb + 1) * 512], ob[:], obr)
                S.barrier()

        eps_s = sb("eps_s", [128, 1], F32)
        S.op("pool", lambda e: e.memset(eps_s[:], EPS), writes=[R("eps")])

        if has_B:
            phase_B()
        if has_A:
            phase_A()
        else:
            for kc in range(16):
                fin.append(S.dma("sp", lambda e, kc=kc: e.dma_start(out=xT_o[kc * 128:(kc + 1) * 128, :], in_=xT[:, kc, :]),
                                 reads=[R("xT", kc, 0), R("xT", kc, 1)]))
        S.emit(final_wait_recs=fin)
    return nc


_PROGS = {}


def _prog(has_B, has_A, first, mod_full=True, mod_next=False):
    key = (has_B, has_A, first, mod_full, mod_next)
    if key not in _PROGS:
        _PROGS[key] = build_program(has_B, has_A, first, mod_full, mod_next)
    return _PROGS[key]


def _consts():
    bf = ml_dtypes.bfloat16
    half = 32
    inv = (10000.0 ** (-np.arange(half, dtype=np.float64) / half))
    rmat = np.zeros((128, 128), np.float32)
    for m in range(128):
        d = m % 64
        if d < 32:
            rmat[m + 32, m] = -1.0
        else:
            rmat[m - 32, m] = 1.0
    blk = np.zeros((128, 128), np.float32)
    blk[:64, :64] = 1.0
    blk[64:, 64:] = 1.0
    per_core = []
    for i in range(NCORES):
        pos = np.concatenate([np.arange(128) + (8 * j + i) * 128 for j in range(8)]).astype(np.float64)
        ang = inv[np.arange(128) % 32][:, None] * pos[None, :]
        ang = ang.astype(np.float32).astype(np.float64)
        cos = np.cos(ang).astype(np.float32)
        sin = np.sin(ang).astype(np.float32)
        mask = np.ones((128, 8, 128), np.float32)
        for ip in range(8):
            if ip > i:
                mask[:, ip, :] = 0.0
            elif ip == i:
                kc = np.arange(128)[:, None] // 64
                qc = np.arange(128)[None, :] // 64
                mask[:, ip, :] = np.where(kc <= qc, 1.0, 0.0)
        per_core.append({
            "cos": cos, "sin": sin, "rmat": rmat, "ones": np.ones((128, 128), np.float32), "blk64": blk,
            "ident": np.eye(128, dtype=np.float32).astype(bf), "onesb": np.ones((128, 128), np.float32).astype(bf),
            "mask": mask.astype(bf),
        })
    return per_core


def _pc(v):
    return np.ascontiguousarray(v.reshape(-1, 128).T)


def kernel(x, c, ada_w, ada_b, norm_g, w_in, conv_w, w_conv_out, q_norm_g, k_norm_g,
           lam_q1, lam_k1, lam_q2, lam_k2, subln_g, w_attn_out, w_o):
    f = lambda a: np.ascontiguousarray(np.asarray(a, dtype=np.float32))
    x, c, ada_w, ada_b, norm_g, w_in, conv_w, w_conv_out = map(f, (x, c, ada_w, ada_b, norm_g, w_in, conv_w, w_conv_out))
    q_norm_g, k_norm_g, lam_q1, lam_k1, lam_q2, lam_k2, subln_g, w_attn_out, w_o = map(
        f, (q_norm_g, k_norm_g, lam_q1, lam_k1, lam_q2, lam_k2, subln_g, w_attn_out, w_o))
    consts = _consts()
    xt = x[0].reshape(8, 8, 128, D)
    xT = [np.ascontiguousarray(xt[:, i].reshape(T, D).T) for i in range(NCORES)]
    state = None
    for k in range(DEPTH + 1):
        has_B = k > 0
        has_A = k < DEPTH
        mod_full = (k == 0)
        mod_next = has_A and (k + 1 < DEPTH)
        nc = _prog(has_B, has_A, k == 0, mod_full, mod_next)
        in_maps = []
        for i in range(NCORES):
            m = dict(consts[i])
            m["xT_in"] = xT[i]
            if has_A:
                la = k
                m.update({
                    "c_vec": _pc(c[0]), "ada_b": _pc(ada_b[la]), "norm_g": _pc(norm_g[la]),
                    "w_in": w_in[la],
                    "gq128": np.ascontiguousarray(np.tile(q_norm_g[la], 2)[:, None]),
                    "gk128": np.ascontiguousarray(np.tile(k_norm_g[la], 2)[:, None]),
                })
            if has_A and mod_full:
                m["ada_w"] = ada_w[k]
            if has_A and not mod_full:
                m["mod_raw"] = mod_raw
            if mod_next:
                m["ada_w_next"] = np.ascontiguousarray(ada_w[k + 1][:, i * 768:(i + 1) * 768])
            if has_B:
                lb = k - 1
                lam_init = 0.8 - 0.6 * math.exp(-0.3 * lb)
                lconst = np.empty((128, 2), np.float32)
                lconst[:, 0] = lam_init
                lconst[:, 1] = 1.0 - lam_init
                m.update({
                    "qT_i": state[i]["qT_o"], "bz_i": state[i]["bz_o"], "szb_i": state[i]["szb_o"],
                    "sga_i": state[i]["sga_o"], "sgb_i": state[i]["sgb_o"], "gate_i": state[i]["gate_o"],
                    "cvh_i": state[i]["cvh"], "K_all": state["K_all"], "V_all": state["V_all"],
                    "conv_w": np.ascontiguousarray(conv_w[lb].reshape(3, 8, 128).transpose(2, 0, 1).reshape(128, 24)),
                    "w_conv_out": w_conv_out[lb], "w_attn_out": w_attn_out[lb], "w_o": w_o[lb],
                    "lamp": np.stack([lam_q1[lb], lam_k1[lb], lam_q2[lb], lam_k2[lb]]),
                    "lconst": lconst, "subg": np.ascontiguousarray(subln_g[lb][:, None]),
                })
            in_maps.append(m)
        res = run_bass_kernel_spmd(nc, in_maps, core_ids=list(range(NCORES)))
        outs = res.results
        xT = [np.asarray(outs[i]["xT_o"]) for i in range(NCORES)]
        if mod_next:
            mod_raw = np.ascontiguousarray(np.concatenate([np.asarray(outs[i]["mod_slice_o"]) for i in range(NCORES)], axis=1))
        if has_A:
            kk = np.stack([np.asarray(outs[i]["kT_o"]) for i in range(NCORES)])
            kk = kk.reshape(8, 8, 128, 8, 128).transpose(1, 2, 3, 0, 4)
            K_all = np.ascontiguousarray(kk.reshape(8, 128, SEQ))
            vv = np.stack([np.asarray(outs[i]["v_o"]) for i in range(NCORES)])
            vv = vv.reshape(8, 8, 128, 8, 128).transpose(3, 2, 1, 0, 4)
            V_all = np.ascontiguousarray(vv.reshape(8, 128, 64 * 128))
            cvs = [np.asarray(outs[i]["cv_o"]).reshape(1024, 8, 128) for i in range(NCORES)]
            state = {"K_all": K_all, "V_all": V_all}
            for i in range(NCORES):
                cvh = np.zeros((1024, 8, 130), np.float32)
                cvh[:, :, 2:] = cvs[i]
                for j in range(8):
                    g = 8 * j + i
                    if g > 0:
                        cvh[:, j, 0:2] = cvs[(g - 1) % 8][:, (g - 1) // 8, 126:128]
                st = {kname: np.asarray(outs[i][kname]) for kname in ("qT_o", "bz_o", "szb_o", "sga_o", "sgb_o", "gate_o")}
                st["cvh"] = cvh.reshape(1024, 8 * 130)
                state[i] = st
    out = np.empty((8, 8, 128, D), np.float32)
    for i in range(NCORES):
        out[:, i] = xT[i].T.reshape(8, 128, D)
    return out.reshape(1, SEQ, D)
```
